# Optimizing a Trainium2 kernel written in Bass

```python
import math
import jax, jax.numpy as jnp
from jax import lax
import numpy as np

D_MODEL = 1024
BATCH = 8
SEQ = 4096
DEPTH = 2
DEC_BATCH = 4
DEC_SEQ = 4096
PAST_LEN = 128

D_HY = 512
HY_SHORT = 3
HY_EMB_BANDS = 8
HY_EMB = 1 + 2 * HY_EMB_BANDS
HY_FILT_HID = 64
HY_FAST_DECAY = 0.3
HY_SLOW_DECAY = 1.5
HY_TARGET = 1e-2
HEAD_DIM = 64
HEADS_PER_GROUP = 4
DIL_PATTERNS = ((128, 1), (512, 4), (2048, 16))
N_GROUPS = len(DIL_PATTERNS)
N_ATT_HEADS = N_GROUPS * HEADS_PER_GROUP
D_ATT = N_ATT_HEADS * HEAD_DIM
D_ATT_OUT = HEADS_PER_GROUP * HEAD_DIM
ATT_BLOCK = 64
ROPE_THETA = 10000.0
NEG_INF = -1e30
DEN_FLOOR = 1e-30
D_RG = 512
RG_BLOCKS = 8
RG_BLOCK_DIM = D_RG // RG_BLOCKS
RG_CONV = 4
RG_C = 8.0
D_FF = 3 * D_MODEL
FFN_CONV = 3
DN_ALPHA = (2 * DEPTH) ** 0.25
DN_BETA = (8 * DEPTH) ** -0.25
LN_EPS = 1e-5
D_IN = 3 * D_HY + 3 * D_ATT + 2 * D_RG

kernel_name = 'hybrid_hyena_dilattn_rglru_encoder'


def layer_norm(x, g, b):
    xf = x.astype(jnp.float32)
    mu = jnp.mean(xf, axis=-1, keepdims=True)
    var = jnp.mean(jnp.square(xf - mu), axis=-1, keepdims=True)
    y = (xf - mu) * lax.rsqrt(var + LN_EPS)
    return (y * g.astype(jnp.float32) + b.astype(jnp.float32)).astype(x.dtype)


def depthwise_conv(x, w, b, pad_left, pad_right):
    c = x.shape[-1]
    y = lax.conv_general_dilated(x, w.astype(x.dtype)[:, None, :], window_strides=(1,),
                                 padding=[(pad_left, pad_right)],
                                 dimension_numbers=('NWC', 'WIO', 'NWC'), feature_group_count=c)
    return y + b.astype(x.dtype)


def hyena_filters(L, w1, b1, w2, b2, w3, b3, freq):
    f32 = jnp.float32
    t = jnp.linspace(0.0, 1.0, L, dtype=f32)[:, None]
    w = 2.0 * math.pi * jnp.arange(L, dtype=f32)[:, None] / L
    bands = jnp.linspace(1e-4, HY_EMB_BANDS - 1, HY_EMB_BANDS, dtype=f32)[None, :]
    z = jnp.concatenate([t, jnp.cos(bands * w), -jnp.sin(bands * w)], axis=-1)
    fr = freq.astype(f32)
    h = jnp.sin(fr * (z @ w1.astype(f32) + b1.astype(f32)))
    h = jnp.sin(fr * (h @ w2.astype(f32) + b2.astype(f32)))
    h = h @ w3.astype(f32) + b3.astype(f32)
    deltas = jnp.abs(jnp.linspace(math.log(HY_TARGET) / HY_SLOW_DECAY,
                                  math.log(HY_TARGET) / HY_FAST_DECAY, D_HY, dtype=f32))
    decay = jnp.exp(-t * deltas[None, :])
    h_fwd = h[:, :D_HY] * decay
    h_bwd = h[:, D_HY:] * decay
    k = jnp.concatenate([h_fwd, jnp.zeros((1, D_HY), f32), h_bwd[:0:-1]], axis=0)
    return k / jnp.sum(jnp.abs(k), axis=0, keepdims=True)


def hyena_long_conv(u, k, bias):
    L = u.shape[1]
    U = jnp.fft.rfft(u, n=2 * L, axis=1)
    K = jnp.fft.rfft(k, n=2 * L, axis=0)
    y = jnp.fft.irfft(U * K[None], n=2 * L, axis=1)[:, :L]
    return y + u * bias


def hyena_mixer(xa, conv_w, conv_b, w1, b1, w2, b2, w3, b3, freq, bias):
    L = xa.shape[1]
    u = depthwise_conv(xa, conv_w, conv_b, 1, 1).astype(jnp.float32)
    x0, x1, v = jnp.split(u, 3, axis=-1)
    k = hyena_filters(L, w1, b1, w2, b2, w3, b3, freq)
    z = hyena_long_conv(v * x1, k, bias.astype(jnp.float32))
    return (x0 * z).astype(xa.dtype)


def rope(t):
    L, dh = t.shape[1], t.shape[-1]
    inv = ROPE_THETA ** (-jnp.arange(0, dh, 2, dtype=jnp.float32) / dh)
    ang = jnp.arange(L, dtype=jnp.float32)[:, None] * inv[None, :]
    cos = jnp.cos(ang)[None, :, None, :]
    sin = jnp.sin(ang)[None, :, None, :]
    t1, t2 = t[..., :dh // 2], t[..., dh // 2:]
    return jnp.concatenate([t1 * cos - t2 * sin, t2 * cos + t1 * sin], axis=-1)


def strided_window_attention(q, k, v, dil, radius):
    B, S, H, dh = q.shape
    W = ATT_BLOCK
    n = S // dil
    nb = -(-n // W)
    n_pad = nb * W

    def to_sub(t):
        t = t.reshape(B, n, dil, H, dh).transpose(0, 2, 1, 3, 4)
        return jnp.pad(t, ((0, 0), (0, 0), (0, n_pad - n), (0, 0), (0, 0)))

    def windows(t):
        t = jnp.pad(t, ((0, 0), (0, 0), (W, W), (0, 0), (0, 0))).reshape(B, dil, nb + 2, W, H, dh)
        return jnp.concatenate([t[:, :, :-2], t[:, :, 1:-1], t[:, :, 2:]], axis=3)

    qb = to_sub(q).reshape(B, dil, nb, W, H, dh)
    kw = windows(to_sub(k))
    vw = windows(to_sub(v))
    s = jnp.einsum('bdnqhe,bdnkhe->bdnhqk', qb, kw) * (dh ** -0.5)
    qi = jnp.arange(nb)[:, None] * W + jnp.arange(W)[None, :]
    kj = (jnp.arange(nb)[:, None] - 1) * W + jnp.arange(3 * W)[None, :]
    rel = kj[:, None, :] - qi[:, :, None]
    mask = (jnp.abs(rel) <= radius) & (kj[:, None, :] >= 0) & (kj[:, None, :] < n)
    mask = mask[:, None]
    s = jnp.where(mask, s, NEG_INF)
    m = jnp.max(s, axis=-1, keepdims=True)
    e = jnp.where(mask, jnp.exp(s - m), 0.0)
    den = jnp.maximum(jnp.sum(e, axis=-1, keepdims=True), DEN_FLOOR)
    o = jnp.einsum('bdnhqk,bdnkhe->bdnqhe', e / den, vw)
    lse = (m + jnp.log(den))[..., 0]
    o = o.reshape(B, dil, n_pad, H, dh)[:, :, :n].transpose(0, 2, 1, 3, 4).reshape(B, S, H, dh)
    lse = lse.transpose(0, 1, 2, 4, 3).reshape(B, dil, n_pad, H)[:, :, :n]
    lse = lse.transpose(0, 2, 1, 3).reshape(B, S, H)
    return o, lse


def dilated_attention(q, k, v):
    outs, lses = [], []
    for g, (window, dil) in enumerate(DIL_PATTERNS):
        o, lse = strided_window_attention(q[:, :, g], k[:, :, g], v[:, :, g], dil, window // (2 * dil))
        outs.append(o)
        lses.append(lse)
    o = jnp.stack(outs, axis=2)
    wgt = jax.nn.softmax(jnp.stack(lses, axis=2), axis=2)
    return jnp.einsum('blgh,blghe->blhe', wgt, o)


def _lru_combine(c1, c2):
    a1, b1 = c1
    a2, b2 = c2
    return a1 * a2, a2 * b1 + b2


def rglru_scan(x, gate_w, gate_b, lam):
    B, L, D = x.shape
    xb = x.reshape(B, L, RG_BLOCKS, RG_BLOCK_DIM)
    g = jnp.einsum('blnd,gnde->gblne', xb, gate_w.astype(jnp.float32)).reshape(2, B, L, D)
    g = g + gate_b.astype(jnp.float32)[:, None, None, :]
    r = jax.nn.sigmoid(g[0])
    i = jax.nn.sigmoid(g[1])
    log_a = -RG_C * r * jax.nn.softplus(-lam.astype(jnp.float32))
    a = jnp.exp(log_a)
    mult = jnp.sqrt(-jnp.expm1(2.0 * log_a))
    mult = mult.at[:, 0].set(1.0)
    xn = x * i * mult
    _, h = lax.associative_scan(_lru_combine, (a, xn), axis=1)
    return h


def rglru_mixer(xc, conv_w, conv_b, gate_w, gate_b, lam):
    xr, gate = jnp.split(xc, 2, axis=-1)
    xr = depthwise_conv(xr, conv_w, conv_b, 2, 1).astype(jnp.float32)
    h_f = rglru_scan(xr, gate_w[0], gate_b[0], lam[0])
    h_b = jnp.flip(rglru_scan(jnp.flip(xr, axis=1), gate_w[1], gate_b[1], lam[1]), axis=1)
    return ((h_f + h_b) * jax.nn.gelu(gate.astype(jnp.float32))).astype(xc.dtype)


def encoder_layer(x, w_in, hy_conv_w, hy_conv_b, hy_filt_w1, hy_filt_b1, hy_filt_w2, hy_filt_b2,
                  hy_filt_w3, hy_filt_b3, hy_filt_freq, hy_bias, rg_conv_w, rg_conv_b, rg_gate_w,
                  rg_gate_b, rg_lam, w_gate, b_gate, w_br_a, w_br_b, w_br_c, w_o, ln1_g, ln1_b,
                  w_up, ffn_conv_w, ffn_conv_b, w_down, ln2_g, ln2_b):
    B, L, _ = x.shape
    dt = x.dtype
    proj = x @ w_in
    xa, xq, xk, xv, xc = jnp.split(
        proj, [3 * D_HY, 3 * D_HY + D_ATT, 3 * D_HY + 2 * D_ATT, 3 * D_HY + 3 * D_ATT], axis=-1)
    ya = hyena_mixer(xa, hy_conv_w, hy_conv_b, hy_filt_w1, hy_filt_b1, hy_filt_w2, hy_filt_b2,
                     hy_filt_w3, hy_filt_b3, hy_filt_freq, hy_bias)
    q = rope(xq.astype(jnp.float32).reshape(B, L, N_ATT_HEADS, HEAD_DIM))
    k = rope(xk.astype(jnp.float32).reshape(B, L, N_ATT_HEADS, HEAD_DIM))
    v = xv.astype(jnp.float32).reshape(B, L, N_ATT_HEADS, HEAD_DIM)
    gshape = (B, L, N_GROUPS, HEADS_PER_GROUP, HEAD_DIM)
    yb = dilated_attention(q.reshape(gshape), k.reshape(gshape), v.reshape(gshape))
    yb = yb.reshape(B, L, D_ATT_OUT).astype(dt)
    yc = rglru_mixer(xc, rg_conv_w, rg_conv_b, rg_gate_w, rg_gate_b, rg_lam)
    gates = jax.nn.sigmoid((x @ w_gate + b_gate).astype(jnp.float32)).reshape(B, L, 3, D_MODEL)
    mixed = (gates[:, :, 0] * (ya @ w_br_a).astype(jnp.float32)
             + gates[:, :, 1] * (yb @ w_br_b).astype(jnp.float32)
             + gates[:, :, 2] * (yc @ w_br_c).astype(jnp.float32)).astype(dt)
    x = layer_norm(DN_ALPHA * x + mixed @ w_o, ln1_g, ln1_b)
    hg, hu = jnp.split(x @ w_up, 2, axis=-1)
    hg = depthwise_conv(hg, ffn_conv_w, ffn_conv_b, 1, 1)
    f = (jax.nn.gelu(hg) * hu) @ w_down
    return layer_norm(DN_ALPHA * x + f, ln2_g, ln2_b)


def setup_inputs(seed: int = 0) -> dict:
    key = jax.random.key(seed)
    ks = iter(jax.random.split(key, 40))
    f32 = jnp.float32

    def nrm(shape, scale):
        return jax.random.normal(next(ks), shape, f32) * scale

    x_prompt = nrm((BATCH, SEQ, D_MODEL), 1.0)
    x_sample = nrm((DEC_BATCH, DEC_SEQ, D_MODEL), 1.0)
    w_in = nrm((DEPTH, D_MODEL, D_IN), D_MODEL ** -0.5)
    hy_conv_w = nrm((DEPTH, HY_SHORT, 3 * D_HY), HY_SHORT ** -0.5)
    hy_conv_b = nrm((DEPTH, 3 * D_HY), 0.02)
    hy_filt_w1 = nrm((DEPTH, HY_EMB, HY_FILT_HID), HY_EMB ** -0.5)
    hy_filt_b1 = nrm((DEPTH, HY_FILT_HID), 0.02)
    hy_filt_w2 = nrm((DEPTH, HY_FILT_HID, HY_FILT_HID), HY_FILT_HID ** -0.5)
    hy_filt_b2 = nrm((DEPTH, HY_FILT_HID), 0.02)
    hy_filt_w3 = nrm((DEPTH, HY_FILT_HID, 2 * D_HY), HY_FILT_HID ** -0.5)
    hy_filt_b3 = nrm((DEPTH, 2 * D_HY), 0.02)
    hy_filt_freq = 1.0 + nrm((DEPTH, HY_FILT_HID), 0.01)
    hy_bias = nrm((DEPTH, D_HY), 1.0)
    rg_conv_w = nrm((DEPTH, RG_CONV, D_RG), RG_CONV ** -0.5)
    rg_conv_b = nrm((DEPTH, D_RG), 0.02)
    rg_gate_w = nrm((DEPTH, 2, 2, RG_BLOCKS, RG_BLOCK_DIM, RG_BLOCK_DIM), RG_BLOCK_DIM ** -0.5)
    rg_gate_b = nrm((DEPTH, 2, 2, D_RG), 0.01)
    a_c = jax.random.uniform(next(ks), (DEPTH, 2, D_RG), f32, 0.9, 0.999)
    s = a_c ** (1.0 / RG_C)
    rg_lam = jnp.log(s) - jnp.log1p(-s)
    w_gate = nrm((DEPTH, D_MODEL, 3 * D_MODEL), D_MODEL ** -0.5)
    b_gate = nrm((DEPTH, 3 * D_MODEL), 0.02)
    w_br_a = nrm((DEPTH, D_HY, D_MODEL), D_HY ** -0.5)
    w_br_b = nrm((DEPTH, D_ATT_OUT, D_MODEL), D_ATT_OUT ** -0.5)
    w_br_c = nrm((DEPTH, D_RG, D_MODEL), D_RG ** -0.5)
    w_o = nrm((DEPTH, D_MODEL, D_MODEL), D_MODEL ** -0.5 * DN_BETA)
    ln1_g = 1.0 + nrm((DEPTH, D_MODEL), 0.02)
    ln1_b = nrm((DEPTH, D_MODEL), 0.02)
    w_up = nrm((DEPTH, D_MODEL, 2 * D_FF), D_MODEL ** -0.5)
    ffn_conv_w = nrm((DEPTH, FFN_CONV, D_FF), FFN_CONV ** -0.5)
    ffn_conv_b = nrm((DEPTH, D_FF), 0.02)
    w_down = nrm((DEPTH, D_FF, D_MODEL), D_FF ** -0.5 * DN_BETA)
    ln2_g = 1.0 + nrm((DEPTH, D_MODEL), 0.02)
    ln2_b = nrm((DEPTH, D_MODEL), 0.02)
    return {'x_prompt': x_prompt, 'x_sample': x_sample, 'w_in': w_in,
            'hy_conv_w': hy_conv_w, 'hy_conv_b': hy_conv_b,
            'hy_filt_w1': hy_filt_w1, 'hy_filt_b1': hy_filt_b1, 'hy_filt_w2': hy_filt_w2,
            'hy_filt_b2': hy_filt_b2, 'hy_filt_w3': hy_filt_w3, 'hy_filt_b3': hy_filt_b3,
            'hy_filt_freq': hy_filt_freq, 'hy_bias': hy_bias,
            'rg_conv_w': rg_conv_w, 'rg_conv_b': rg_conv_b, 'rg_gate_w': rg_gate_w,
            'rg_gate_b': rg_gate_b, 'rg_lam': rg_lam,
            'w_gate': w_gate, 'b_gate': b_gate, 'w_br_a': w_br_a, 'w_br_b': w_br_b, 'w_br_c': w_br_c,
            'w_o': w_o, 'ln1_g': ln1_g, 'ln1_b': ln1_b,
            'w_up': w_up, 'ffn_conv_w': ffn_conv_w, 'ffn_conv_b': ffn_conv_b, 'w_down': w_down,
            'ln2_g': ln2_g, 'ln2_b': ln2_b}


def reference(x_prompt, x_sample, w_in, hy_conv_w, hy_conv_b, hy_filt_w1, hy_filt_b1, hy_filt_w2,
              hy_filt_b2, hy_filt_w3, hy_filt_b3, hy_filt_freq, hy_bias, rg_conv_w, rg_conv_b,
              rg_gate_w, rg_gate_b, rg_lam, w_gate, b_gate, w_br_a, w_br_b, w_br_c, w_o, ln1_g, ln1_b,
              w_up, ffn_conv_w, ffn_conv_b, w_down, ln2_g, ln2_b):
    def layer_params(l):
        return (w_in[l], hy_conv_w[l], hy_conv_b[l], hy_filt_w1[l], hy_filt_b1[l], hy_filt_w2[l],
                hy_filt_b2[l], hy_filt_w3[l], hy_filt_b3[l], hy_filt_freq[l], hy_bias[l],
                rg_conv_w[l], rg_conv_b[l], rg_gate_w[l], rg_gate_b[l], rg_lam[l],
                w_gate[l], b_gate[l], w_br_a[l], w_br_b[l], w_br_c[l], w_o[l], ln1_g[l], ln1_b[l],
                w_up[l], ffn_conv_w[l], ffn_conv_b[l], w_down[l], ln2_g[l], ln2_b[l])

    def trunk(x):
        for l in range(DEPTH):
            x = encoder_layer(x, *layer_params(l))
        return x

    y_prompt = trunk(x_prompt)
    y_sample = trunk(x_sample)
    return (y_prompt, y_sample)
```

```python
import math
from contextlib import ExitStack

import numpy as np
import ml_dtypes
import concourse.bass as bass
import concourse.mybir as mybir
from concourse.bass_utils import run_bass_kernel_spmd

F32 = mybir.dt.float32
BF16 = mybir.dt.bfloat16
AF = mybir.ActivationFunctionType
ALU = mybir.AluOpType
AX = mybir.AxisListType

L_SEQ = 4096
D = 1024
D_HY = 512
D_IN = 4864
D_FF = 3072
NLAYER = 2
ALPHA = (2 * NLAYER) ** 0.25
LN_EPS = 1e-5
TB = 1024
NFFT = 8192


class Res:
    __slots__ = ("w", "r", "psum")

    def __init__(self, psum=False):
        self.w = None
        self.r = {}
        self.psum = psum


class Sched:
    LIM = 40000

    def __init__(self, nc, es):
        self.nc, self.es = nc, es
        self.eng = dict(pe=nc.tensor, act=nc.scalar, dve=nc.vector, pool=nc.gpsimd, sp=nc.sync)
        self.sems = {}
        self.epoch = {k: 0 for k in self.eng}
        self.cnt = {k: 0 for k in self.eng}
        self.waited = {k: {} for k in self.eng}
        self.ND = 40
        self.dcount = 0
        self.dlast = {}
        self.nops = 0

    def _sem(self, key):
        if key not in self.sems:
            self.sems[key] = self.es.enter_context(self.nc.semaphore("s%d" % len(self.sems)))
        return self.sems[key]

    def op(self, e, fn, reads=(), writes=(), dma=False, pe_chain=False):
        deps = {}

        def add(ev):
            if ev is None:
                return
            k, v = ev
            if deps.get(k, 0) < v:
                deps[k] = v

        for r in reads:
            add(r.w)
            if r.psum:
                for k, ev in r.r.items():
                    if k[0] != e:
                        add(ev)
        for w in writes:
            add(w.w)
            for ev in w.r.values():
                add(ev)
        if dma:
            j = self.dcount
            self.dcount += 1
            slot = j % self.ND
            val = 16 * (j // self.ND + 1)
            if j >= self.ND:
                add((("d", slot), val - 16))
            ev = (("d", slot), val)
            self.dlast[("d", slot)] = val
        else:
            if self.cnt[e] >= self.LIM:
                self.epoch[e] += 1
                self.cnt[e] = 0
            self.cnt[e] += 1
            ev = ((e, self.epoch[e]), self.cnt[e])
        E = self.eng[e]
        wd = self.waited[e]
        for k, v in deps.items():
            if pe_chain and k[0] == e:
                continue
            if wd.get(k, 0) >= v:
                continue
            E.wait_ge(self._sem(k), v)
            wd[k] = v
        ins = fn(E)
        ins.then_inc(self._sem(ev[0]), 16 if dma else 1)
        self.nops += 1
        for w in writes:
            w.w = ev
            w.r = {}
        for r in reads:
            if r.w is not ev:
                r.r[ev[0]] = ev
        return ev

    def dma(self, q, out, in_, reads=(), writes=(), **kw):
        return self.op(q, lambda E: E.dma_start(out=out, in_=in_, **kw), reads=reads, writes=writes, dma=True)

    def barrier(self):
        evs = {}
        for e in self.eng:
            if self.cnt[e] > 0:
                evs[(e, self.epoch[e])] = self.cnt[e]
        evs.update(self.dlast)
        for e, E in self.eng.items():
            wd = self.waited[e]
            for k, v in evs.items():
                if wd.get(k, 0) >= v:
                    continue
                E.wait_ge(self._sem(k), v)
                wd[k] = v


class T:
    def __init__(self, t, n=1, psum=False):
        self.t = t
        self.rs = [Res(psum) for _ in range(n)]

    @property
    def res(self):
        return self.rs[0]

    def __getitem__(self, idx):
        return self.t[idx]


class Ctx:
    pass


def build_program(nc, NS=2, NL=2, dbg=None, stop_after=None):
    dbg = dbg or {}
    es = ExitStack()
    S = Sched(nc, es)
    g = Ctx()

    def din(name, shape, dt=F32):
        return nc.dram_tensor(name, list(shape), dt, kind="ExternalInput").ap()

    def dscr(name, shape, dt):
        kind = "ExternalOutput" if name in dbg else "Internal"
        return nc.dram_tensor(name, list(shape), dt, kind=kind).ap()

    x_in = din("x", [NS, L_SEQ, D])
    y_out = nc.dram_tensor("y", [NS, L_SEQ, D], F32, kind="ExternalOutput").ap()
    w_in = din("w_in", [NL, D, D_IN])
    w_gate = din("w_gate", [NL, D, 3 * D])
    w_br_a = din("w_br_a", [NL, 512, D])
    w_br_b = din("w_br_b", [NL, 256, D])
    w_br_c = din("w_br_c", [NL, 512, D])
    w_o = din("w_o", [NL, D, D])
    w_up = din("w_up", [NL, D, 2 * D_FF])
    w_down = din("w_down", [NL, D_FF, D])
    hy_conv_w = din("hy_conv_w", [NL, 128, 12, 3])
    hy_conv_b = din("hy_conv_b", [NL, 128, 12])
    hy_bias = din("hy_bias", [NL, 128, 4])
    hy_w1 = din("hy_w1", [NL, 17, 64])
    hy_w2 = din("hy_w2", [NL, 64, 64])
    hy_w3 = din("hy_w3", [NL, 64, 1024])
    hy_b1 = din("hy_b1", [NL, 64, 1])
    hy_b2 = din("hy_b2", [NL, 64, 1])
    hy_fr = din("hy_fr", [NL, 64, 1])
    hy_b3 = din("hy_b3", [NL, 128, 8])
    rg_conv_w = din("rg_conv_w", [NL, 128, 4, 4])
    rg_conv_b = din("rg_conv_b", [NL, 128, 4])
    rg_gate_w = din("rg_gate_w", [NL, 2, 2, 4, 128, 128])
    rg_gate_b = din("rg_gate_b", [NL, 128, 2, 2, 4])
    rg_lam = din("rg_lam", [NL, 128, 2, 4])
    b_gate = din("b_gate", [NL, 128, 24])
    ffn_conv_w = din("ffn_conv_w", [NL, 128, 24, 3])
    ffn_conv_b = din("ffn_conv_b", [NL, 128, 24])
    ln1_g = din("ln1_g", [NL, 128, D])
    ln1_b = din("ln1_b", [NL, 128, D])
    ln2_g = din("ln2_g", [NL, 128, D])
    ln2_b = din("ln2_b", [NL, 128, D])
    c_identf = din("c_identf", [128, 128])
    c_identb = din("c_identb", [128, 128], BF16)
    c_ropec = din("c_ropec", [128, L_SEQ])
    c_ropes = din("c_ropes", [128, L_SEQ])
    c_zT = din("c_zT", [17, L_SEQ])
    c_tv = din("c_tv", [128, L_SEQ])
    c_ndelta = din("c_ndelta", [128, 4])
    c_f1 = din("c_f1", [128, 2, 128], BF16)
    c_tw = din("c_tw", [128, 2, 64])
    c_l2x = din("c_l2x", [128, 3, 128], BF16)
    c_i2x = din("c_i2x", [128, 2, 128], BF16)
    c_negm = din("c_negm", [128, 2, 2, 128], BF16)
    c_i1 = din("c_i1", [128, 2, 64], BF16)
    c_mask = din("c_mask", [128, 2, 128], BF16)

    WIN_b = dscr("WIN_b", [NL, 38, 128, 8, 128], BF16)
    WV_b = dscr("WV_b", [NL, 128, 8, 768], BF16)
    WG_b = dscr("WG_b", [NL, 24, 128, 8, 128], BF16)
    WBA_b = dscr("WBA_b", [NL, 8, 128, 4, 128], BF16)
    WBB_b = dscr("WBB_b", [NL, 8, 128, 2, 128], BF16)
    WBC_b = dscr("WBC_b", [NL, 8, 128, 4, 128], BF16)
    WO_b = dscr("WO_b", [NL, 128, 8, D], BF16)
    WUP_b = dscr("WUP_b", [NL, 48, 128, 8, 128], BF16)
    WDN_b = dscr("WDN_b", [NL, 128, 24, D], BF16)
    XA = dscr("XA", [NS, 1536, L_SEQ], F32)
    QK = dscr("QK", [NS, 2, 3, 2, 128, L_SEQ], BF16)
    VT = dscr("VT", [NS, L_SEQ, 768], BF16)
    XC = dscr("XC", [NS, 1024, L_SEQ], F32)
    X0 = dscr("X0", [NS, 512, L_SEQ], F32)
    UT = dscr("UT", [NS, 512, L_SEQ], BF16)
    UTOK = dscr("UTOK", [NS, L_SEQ, 512], BF16)
    KTOK = dscr("KTOK", [NFFT, 512], BF16)
    D1 = dscr("D1", [2, 64, 128, 512], BF16)
    KH = dscr("KH", [NL, 128, 128, 2, 512], BF16)
    D2 = dscr("D2", [2, 128, 64, 512], BF16)
    YA = dscr("YA", [NS, 512, L_SEQ], BF16)
    AO = dscr("AO", [NS, 3, L_SEQ, 260], F32)
    YB = dscr("YB", [NS, 256, L_SEQ], BF16)
    YC = dscr("YC", [NS, 512, L_SEQ], BF16)
    X1 = dscr("X1", [NS, L_SEQ, D], F32)
    GT = dscr("GT", [NS, 3 * D, L_SEQ], BF16)
    X2 = dscr("X2", [NS, L_SEQ, D], F32)
    g.HFB = dscr("HFB", [2, 512, L_SEQ], F32)
    g.ASUM = dscr("ASUM", [128, 16], F32)

    uid = [0]

    def sb(ph, name, shape, dt, n=1):
        uid[0] += 1
        return T(ph.enter_context(nc.sbuf_tensor("%s_%d" % (name, uid[0]), list(shape), dt)), n)

    def ps(ph, name, shape, dt=F32, n=1):
        uid[0] += 1
        esz = 4 if dt == F32 else 2
        per = int(np.prod(shape[1:]))
        nb = (per * esz + 2047) // 2048
        t = ph.enter_context(nc.psum_tensor("%s_%d" % (name, uid[0]), [128, nb * 2048 // esz], dt))
        ap = t[:shape[0], :per]
        if len(shape) == 3:
            ap = ap.rearrange("p (a b) -> p a b", a=shape[1])
        elif len(shape) == 4:
            ap = ap.rearrange("p (a b c) -> p a b c", a=shape[1], b=shape[2])
        return T(ap, n, psum=True)

    xT = sb(es, "xT", [128, 8, L_SEQ], BF16, n=32)
    identf = sb(es, "identf", [128, 128], F32)
    identb = sb(es, "identb", [128, 128], BF16)
    S.dma("sp", identf[:], c_identf[:, :], writes=[identf.res])
    S.dma("sp", identb[:], c_identb[:, :], writes=[identb.res])

    rr = {"i": 0}

    def evac_eng(choices=("act", "dve")):
        rr["i"] += 1
        return choices[rr["i"] % len(choices)]

    def copy_op(e, out, in_, reads, writes):
        if e == "act":
            return S.op("act", lambda E: E.activation(out=out, in_=in_, func=AF.Copy), reads=reads, writes=writes)
        return S.op(e, lambda E: E.tensor_copy(out=out, in_=in_), reads=reads, writes=writes)

    def mm(out, lhsT, rhs, start, stop, reads, pres):
        S.op("pe", lambda E: E.matmul(out, lhsT=lhsT, rhs=rhs, start=start, stop=stop),
             reads=reads, writes=[pres], pe_chain=True)

    def xt_res(t0, n):
        return xT.rs[t0 // 128:(t0 + n + 127) // 128]

    def cast_weights(l):
        with ExitStack() as ph:
            st = [sb(ph, "cst%d" % i, [128, 8, 512], F32) for i in range(2)]
            sbb = [sb(ph, "csb%d" % i, [128, 4, 8, 128], BF16) for i in range(2)]
            k = [0]

            def stat(W, dst, K, Dw):
                KC = K // 128
                Wv = W.rearrange("(kc k) d -> k kc d", k=128)
                for d0 in range(0, Dw, 512):
                    wd = min(512, Dw - d0)
                    nm = wd // 128
                    a, b = st[k[0] % 2], sbb[k[0] % 2]
                    k[0] += 1
                    S.dma("sp", a[:, :KC, :wd], Wv[:, :, d0:d0 + wd], writes=[a.res])
                    copy_op(evac_eng(("act", "dve", "pool")),
                            b[:, :nm, :KC, :], a[:, :KC, :wd].rearrange("p kc (m j) -> p m kc j", j=128),
                            [a.res], [b.res])
                    S.dma("pool", dst[d0 // 128:d0 // 128 + nm].rearrange("m k kc j -> k m kc j"),
                          b[:, :nm, :KC, :], reads=[b.res])

            def mov(W, dst, K, Dw):
                KC = K // 128
                Wv = W.rearrange("(kc k) d -> k kc d", k=128)
                per = max(1, 4096 // Dw)
                for c0 in range(0, KC, per):
                    n = min(per, KC - c0)
                    a, b = st[k[0] % 2], sbb[k[0] % 2]
                    k[0] += 1
                    av = a[:].rearrange("p a b -> p (a b)")[:, :n * Dw].rearrange("p (a b) -> p a b", b=Dw)
                    bv = b[:].rearrange("p a b c -> p (a b c)")[:, :n * Dw].rearrange("p (a b) -> p a b", b=Dw)
                    S.dma("sp", av, Wv[:, c0:c0 + n, :], writes=[a.res])
                    copy_op(evac_eng(("act", "dve", "pool")), bv, av, [a.res], [b.res])
                    S.dma("pool", dst[:, c0:c0 + n, :], bv, reads=[b.res])

            stat(w_in[l], WIN_b[l], D, D_IN)
            mov(w_in[l][:, 3072:3840], WV_b[l], D, 768)
            stat(w_gate[l], WG_b[l], D, 3 * D)
            stat(w_br_a[l], WBA_b[l], 512, D)
            stat(w_br_b[l], WBB_b[l], 256, D)
            stat(w_br_c[l], WBC_b[l], 512, D)
            mov(w_o[l], WO_b[l], D, D)
            stat(w_up[l], WUP_b[l], D, 2 * D_FF)
            mov(w_down[l], WDN_b[l], D_FF, D)
            S.barrier()

    def p1a_load_x(s, src=None):
        src = x_in[s] if src is None else src
        with ExitStack() as ph:
            xs = [sb(ph, "xs%d" % i, [128, D], F32) for i in range(3)]
            tp = [ps(ph, "tp%d" % i, [128, 8, 128], F32) for i in range(2)]
            for tt in range(32):
                a, p = xs[tt % 3], tp[tt % 2]
                S.dma("sp", a[:], src[tt * 128:(tt + 1) * 128, :], writes=[a.res])
                for kc in range(8):
                    S.op("pe", lambda E, kc=kc: E.transpose(p[:, kc, :], a[:, kc * 128:(kc + 1) * 128], identf[:]),
                         reads=[a.res, identf.res], writes=[p.res], pe_chain=True)
                copy_op(evac_eng(), xT[:, :, tt * 128:(tt + 1) * 128], p[:, :, :], [p.res], [xT.rs[tt]])
            S.barrier()

    def p1b_inproj(l, s):
        with ExitStack() as ph:
            ropec = sb(ph, "ropec", [128, L_SEQ], F32)
            ropes = sb(ph, "ropes", [128, L_SEQ], F32)
            S.dma("sp", ropec[:], c_ropec[:, :], writes=[ropec.res])
            S.dma("sp", ropes[:], c_ropes[:, :], writes=[ropes.res])
            wv = sb(ph, "wv", [128, 8, 768], BF16)
            S.dma("sp", wv[:], WV_b[l], writes=[wv.res])
            NW = 4
            wt = [sb(ph, "wt%d" % i, [128, 8, 128], BF16) for i in range(NW)]
            pp = [ps(ph, "pp%d" % i, [128, TB], F32) for i in range(4)]
            stg = [sb(ph, "stg%d" % i, [128, TB], F32) for i in range(2)]
            tmp = [sb(ph, "rtmp%d" % i, [128, TB], F32) for i in range(4)]
            qks = [sb(ph, "qks%d" % i, [128, 2, TB], BF16) for i in range(2)]
            vst = [sb(ph, "vst%d" % i, [128, 768], BF16) for i in range(2)]
            ms = [m for m in range(38) if not (24 <= m < 30)]
            cnt = {"w": 0, "p": 0, "s": 0, "q": 0, "v": 0}
            for tb in range(L_SEQ // TB):
                t0 = tb * TB
                xr = xt_res(t0, TB)
                loaded = {}

                def load(i):
                    w = wt[cnt["w"] % NW]
                    cnt["w"] += 1
                    S.dma("sp", w[:], WIN_b[l, ms[i]], writes=[w.res])
                    loaded[i] = w

                def compute(i):
                    m = ms[i]
                    w = loaded.pop(i)
                    p = pp[cnt["p"] % 4]
                    cnt["p"] += 1
                    for nt in range(TB // 512):
                        for kc in range(8):
                            mm(p[:, nt * 512:(nt + 1) * 512], w[:, kc, :], xT[:, kc, t0 + nt * 512:t0 + (nt + 1) * 512],
                               kc == 0, kc == 7, [w.res] + xr, p.res)
                    return m, p

                pend = {}
                D_PF = 2
                for i in range(len(ms) + D_PF):
                    if i < len(ms):
                        load(i)
                    j = i - D_PF
                    if j < 0:
                        continue
                    m, p = compute(j)
                    if m < 12 or m >= 30:
                        a = stg[cnt["s"] % 2]
                        cnt["s"] += 1
                        copy_op(evac_eng(), a[:], p[:], [p.res], [a.res])
                        if m < 12:
                            dst = XA[s, m * 128:(m + 1) * 128, t0:t0 + TB]
                        else:
                            dst = XC[s, (m - 30) * 128:(m - 29) * 128, t0:t0 + TB]
                        S.dma("pool", dst, a[:], reads=[a.res])
                    else:
                        jj = m - 12
                        if jj % 2 == 0:
                            pend["A"] = p
                        else:
                            pa, pb = pend.pop("A"), p
                            kind, gidx = (jj // 2) // 3, (jj // 2) % 3
                            o = qks[cnt["q"] % 2]
                            cnt["q"] += 1
                            c_, s_ = ropec[:, t0:t0 + TB], ropes[:, t0:t0 + TB]
                            t1, t2, t3, t4 = tmp
                            S.op("dve", lambda E: E.tensor_tensor(out=t1[:], in0=pa[:], in1=c_, op=ALU.mult), [pa.res, ropec.res], [t1.res])
                            S.op("dve", lambda E: E.tensor_tensor(out=t2[:], in0=pb[:], in1=s_, op=ALU.mult), [pb.res, ropes.res], [t2.res])
                            S.op("dve", lambda E: E.tensor_tensor(out=t3[:], in0=pb[:], in1=c_, op=ALU.mult), [pb.res, ropec.res], [t3.res])
                            S.op("dve", lambda E: E.tensor_tensor(out=t4[:], in0=pa[:], in1=s_, op=ALU.mult), [pa.res, ropes.res], [t4.res])
                            S.op("pool", lambda E: E.tensor_tensor(out=o[:, 0, :], in0=t1[:], in1=t2[:], op=ALU.subtract), [t1.res, t2.res], [o.res])
                            S.op("pool", lambda E: E.tensor_tensor(out=o[:, 1, :], in0=t3[:], in1=t4[:], op=ALU.add), [t3.res, t4.res], [o.res])
                            S.dma("pool", QK[s, kind, gidx].rearrange("h p t -> p h t")[:, :, t0:t0 + TB], o[:, :, :], reads=[o.res])
                for tt in range(TB // 128):
                    tk = t0 + tt * 128
                    p = pp[cnt["p"] % 4]
                    cnt["p"] += 1
                    for (c0, c1) in ((0, 512), (512, 768)):
                        for kc in range(8):
                            mm(p[:, c0:c1], xT[:, kc, tk:tk + 128], wv[:, kc, c0:c1], kc == 0, kc == 7,
                               [wv.res] + xt_res(tk, 128), p.res)
                    a = vst[cnt["v"] % 2]
                    cnt["v"] += 1
                    copy_op(evac_eng(), a[:], p[:, 0:768], [p.res], [a.res])
                    S.dma("pool", VT[s, tk:tk + 128, :], a[:], reads=[a.res])
            S.barrier()


    def to_tokmajor(ph, tiles, dst, tag):
        tp = [ps(ph, "tk_tp%s%d" % (tag, i), [128, 4, 128], BF16) for i in range(2)]
        st = [sb(ph, "tk_st%s%d" % (tag, i), [128, 512], BF16) for i in range(3)]
        for tt in range(32):
            p, a = tp[tt % 2], st[tt % 3]
            for cc in range(4):
                S.op("pe", lambda E, cc=cc: E.transpose(p[:, cc, :], tiles[cc][:, tt * 128:(tt + 1) * 128], identb[:]),
                     reads=[tiles[cc].res, identb.res], writes=[p.res], pe_chain=True)
            copy_op(evac_eng(), a[:].rearrange("p (a b) -> p a b", a=4), p[:, :, :], [p.res], [a.res])
            S.dma("sp", dst[tt * 128:(tt + 1) * 128, :], a[:], reads=[a.res])

    def pf_filter(l):
        HFB = g.HFB
        with ExitStack() as ph:
            zT = sb(ph, "zT", [17, L_SEQ], F32)
            tv = sb(ph, "tv", [128, L_SEQ], F32)
            w1 = sb(ph, "fw1", [17, 64], F32)
            w2 = sb(ph, "fw2", [64, 64], F32)
            w3 = sb(ph, "fw3", [64, 1024], F32)
            b1 = sb(ph, "fb1", [64, 1], F32)
            b2 = sb(ph, "fb2", [64, 1], F32)
            fr = sb(ph, "ffr", [64, 1], F32)
            frb1 = sb(ph, "frb1", [64, 1], F32)
            frb2 = sb(ph, "frb2", [64, 1], F32)
            b3 = sb(ph, "fb3", [128, 8], F32)
            nd = sb(ph, "fnd", [128, 4], F32)
            halfpi = sb(ph, "halfpi", [128, 1], F32)
            asum = sb(ph, "asum", [128, 16], F32)
            for (t_, src) in ((zT, c_zT), (tv, c_tv), (w1, hy_w1[l]), (w2, hy_w2[l]), (w3, hy_w3[l]), (b1, hy_b1[l]),
                              (b2, hy_b2[l]), (fr, hy_fr[l]), (b3, hy_b3[l]), (nd, c_ndelta)):
                S.dma("sp", t_[:], src, writes=[t_.res])
            S.op("dve", lambda E: E.memset(halfpi[:], math.pi / 2), [], [halfpi.res])
            S.op("dve", lambda E: E.tensor_tensor(out=frb1[:], in0=fr[:], in1=b1[:], op=ALU.mult), [fr.res, b1.res], [frb1.res])
            S.op("dve", lambda E: E.tensor_tensor(out=frb2[:], in0=fr[:], in1=b2[:], op=ALU.mult), [fr.res, b2.res], [frb2.res])
            HS = 2048
            h1 = sb(ph, "fh1", [64, HS], F32)
            h2 = sb(ph, "fh2", [64, HS], F32)
            dec = sb(ph, "fdec", [128, HS], F32)
            hk = [sb(ph, "fhk%d" % i, [128, HS], F32) for i in range(2)]
            ta = sb(ph, "fta", [64, 512], F32)
            tab = sb(ph, "ftab", [64, 512], F32)
            ts1 = sb(ph, "fts1", [64, 512], F32)
            ts2 = sb(ph, "fts2", [64, 512], F32)
            pq = [ps(ph, "fpq%d" % i, [128, 512], F32) for i in range(4)]
            pc = [0]

            def sin_layer(dst, wmat, K, rhs_t, rhs_res, hs0, frb):
                for nt in range(HS // 512):
                    p = pq[pc[0] % 4]
                    pc[0] += 1
                    c0 = nt * 512
                    off = hs0 + c0 if rhs_t is zT else c0
                    rhs = rhs_t[:K, off:off + 512]
                    mm(p[:64, :], wmat[:K, :], rhs, True, True, [wmat.res, rhs_res], p.res)
                    S.op("act", lambda E: E.activation(out=ta[:], in_=p[:64, :], func=AF.Identity, scale=fr[:], bias=frb[:]),
                         [p.res, fr.res, frb.res], [ta.res])
                    S.op("dve", lambda E: E.scalar_tensor_tensor(out=tab[:], in0=ta[:], scalar=-1.0, in1=ta[:], op0=ALU.mult, op1=ALU.max), [ta.res], [tab.res])
                    S.op("act", lambda E: E.activation(out=ts1[:], in_=ta[:], func=AF.Sin, scale=0.5), [ta.res], [ts1.res])
                    S.op("act", lambda E: E.activation(out=ts2[:], in_=tab[:], func=AF.Sin, scale=-0.5, bias=halfpi[:64, :]),
                         [tab.res, halfpi.res], [ts2.res])
                    S.op("dve", lambda E: E.scalar_tensor_tensor(out=dst[:, c0:c0 + 512], in0=ts1[:], scalar=2.0, in1=ts2[:],
                                                                 op0=ALU.mult, op1=ALU.mult), [ts1.res, ts2.res], [dst.res])

            S.op("dve", lambda E: E.memset(asum[:], 0.0), [], [asum.res])
            ki = 0
            for hs in range(2):
                hs0 = hs * HS
                sin_layer(h1, w1, 17, zT, zT.res, hs0, frb1)
                sin_layer(h2, w2, 64, h1, h1.res, hs0, frb2)
                for cc in range(4):
                    S.op("act", lambda E: E.activation(out=dec[:], in_=tv[:, hs0:hs0 + HS], func=AF.Exp, scale=nd[:, cc:cc + 1]),
                         [tv.res, nd.res], [dec.res])
                    for dirn in range(2):
                        mc = dirn * 4 + cc
                        o = hk[ki % 2]
                        ki += 1
                        for nt in range(HS // 512):
                            p = pq[pc[0] % 4]
                            pc[0] += 1
                            c0 = nt * 512
                            mm(p[:, :], w3[:, mc * 128:(mc + 1) * 128], h2[:, c0:c0 + 512], True, True, [w3.res, h2.res], p.res)
                            S.op("dve", lambda E: E.scalar_tensor_tensor(out=o[:, c0:c0 + 512], in0=p[:, :], scalar=b3[:, mc:mc + 1],
                                                                         in1=dec[:, c0:c0 + 512], op0=ALU.add, op1=ALU.mult),
                                 [p.res, b3.res, dec.res], [o.res])
                        if dirn == 1 and hs == 0:
                            S.op("dve", lambda E: E.memset(o[:, 0:1], 0.0), [], [o.res])
                        col = mc * 2 + hs
                        S.op("dve", lambda E: E.tensor_reduce(out=asum[:, col:col + 1], in_=o[:], axis=AX.X, op=ALU.add,
                                                              apply_absolute_value=True), [o.res], [asum.res])
                        S.dma("pool", HFB[dirn, cc * 128:(cc + 1) * 128, hs0:hs0 + HS], o[:], reads=[o.res])
            S.dma("pool", g.ASUM[:, :], asum[:], reads=[asum.res])
            S.barrier()
        with ExitStack() as ph:
            asum2 = sb(ph, "asum2", [128, 16], F32)
            nrm = sb(ph, "fnrm", [128, 4], F32)
            rinv = sb(ph, "frinv", [128, 4], F32)
            S.dma("sp", asum2[:], g.ASUM[:, :], writes=[asum2.res])
            for cc in range(4):
                S.op("dve", lambda E, cc=cc: E.tensor_tensor(out=nrm[:, cc:cc + 1], in0=asum2[:, 2 * cc:2 * cc + 1],
                                                             in1=asum2[:, 2 * cc + 1:2 * cc + 2], op=ALU.add), [asum2.res], [nrm.res])
                for extra in (2 * (4 + cc), 2 * (4 + cc) + 1):
                    S.op("dve", lambda E, cc=cc, extra=extra: E.tensor_tensor(out=nrm[:, cc:cc + 1], in0=nrm[:, cc:cc + 1],
                                                                              in1=asum2[:, extra:extra + 1], op=ALU.add),
                         [asum2.res, nrm.res], [nrm.res])
            S.op("dve", lambda E: E.reciprocal(out=rinv[:], in_=nrm[:]), [nrm.res], [rinv.res])
            hb = [sb(ph, "fhb%d" % i, [128, L_SEQ], F32) for i in range(2)]
            kb = [sb(ph, "fkb%d" % i, [128, L_SEQ], BF16) for i in range(4)]
            for dirn in range(2):
                for cc in range(4):
                    a = hb[cc % 2]
                    S.dma("sp", a[:], HFB[dirn, cc * 128:(cc + 1) * 128, :], writes=[a.res])
                    o = kb[cc]
                    if dirn == 0:
                        S.op("dve", lambda E: E.tensor_scalar(out=o[:], in0=a[:], scalar1=rinv[:, cc:cc + 1], scalar2=None, op0=ALU.mult),
                             [a.res, rinv.res], [o.res])
                    else:
                        S.op("dve", lambda E: E.memset(o[:, 0:1], 0.0), [], [o.res])
                        S.op("dve", lambda E: E.tensor_scalar(out=o[:, 1:L_SEQ], in0=a[:, L_SEQ - 1:0:-1], scalar1=rinv[:, cc:cc + 1],
                                                              scalar2=None, op0=ALU.mult), [a.res, rinv.res], [o.res])
                to_tokmajor(ph, kb, KTOK[dirn * L_SEQ:(dirn + 1) * L_SEQ, :], "f%d" % dirn)
            S.barrier()

    def fft_stage1(src, NR):
        with ExitStack() as ph:
            f1m = sb(ph, "f1m", [128, 2, 128], BF16)
            tw = sb(ph, "tw", [128, 2, 64], F32)
            S.dma("sp", f1m[:], c_f1[:, :, :], writes=[f1m.res])
            S.dma("sp", tw[:], c_tw[:, :, :], writes=[tw.res])
            NC2 = 8
            uh = [sb(ph, "uh%d" % i, [128, NC2, 512], BF16) for i in range(2)]
            pq = [ps(ph, "s1p%d" % i, [128, 512], F32) for i in range(6)]
            t1 = [sb(ph, "s1t1%d" % i, [128, 512], F32) for i in range(2)]
            t2 = [sb(ph, "s1t2%d" % i, [128, 512], F32) for i in range(2)]
            oo = [sb(ph, "s1o%d" % i, [128, 2, 512], BF16) for i in range(3)]
            srcv = src.rearrange("(a b) c -> a b c", b=64)
            pc = 0
            for ch in range(64 // NC2):
                u = uh[ch % 2]
                S.dma("sp", u[:NR], srcv[:, ch * NC2:(ch + 1) * NC2, :], writes=[u.res])
                for j in range(NC2):
                    n2 = ch * NC2 + j
                    pr, pi = pq[pc % 6], pq[(pc + 1) % 6]
                    pc += 2
                    mm(pr[:, :], f1m[:NR, 0, :], u[:NR, j, :], True, True, [f1m.res, u.res], pr.res)
                    mm(pi[:, :], f1m[:NR, 1, :], u[:NR, j, :], True, True, [f1m.res, u.res], pi.res)
                    a1, a2, o = t1[n2 % 2], t2[n2 % 2], oo[n2 % 3]
                    Tr, Ti = tw[:, 0, n2:n2 + 1], tw[:, 1, n2:n2 + 1]
                    S.op("act", lambda E: E.activation(out=a1[:], in_=pi[:, :], func=AF.Identity, scale=Ti), [pi.res, tw.res], [a1.res])
                    S.op("dve", lambda E: E.scalar_tensor_tensor(out=o[:, 0, :], in0=pr[:, :], scalar=Tr, in1=a1[:], op0=ALU.mult, op1=ALU.add),
                         [pr.res, tw.res, a1.res], [o.res])
                    S.op("act", lambda E: E.activation(out=a2[:], in_=pr[:, :], func=AF.Identity, scale=Ti), [pr.res, tw.res], [a2.res])
                    S.op("dve", lambda E: E.scalar_tensor_tensor(out=o[:, 1, :], in0=pi[:, :], scalar=Tr, in1=a2[:], op0=ALU.mult, op1=ALU.subtract),
                         [pi.res, tw.res, a2.res], [o.res])
                    S.dma("pool", D1[:, n2, :, :].rearrange("r f c -> f r c"), o[:, :, :], reads=[o.res])
            S.barrier()

    def fft_stage2(mode, l):
        with ExitStack() as ph:
            l2 = sb(ph, "l2m", [128, 3, 128], BF16)
            S.dma("sp", l2[:], c_l2x[:, :, :], writes=[l2.res])
            FC = 4
            ain = [sb(ph, "s2a%d" % i, [128, FC, 512], BF16) for i in range(3)]
            d1v = D1.rearrange("r n f c -> (r n) f c")
            if mode == "filter":
                pU = [ps(ph, "s2pu%d" % i, [128, 2, 512], F32) for i in range(3)]
                st = [sb(ph, "s2st%d" % i, [128, 2, 512], BF16) for i in range(3)]
            else:
                pU = [ps(ph, "s2pu%d" % i, [128, 512], F32) for i in range(4)]
                pB = [ps(ph, "s2pb%d" % i, [128, 512], F32) for i in range(4)]
                i2 = sb(ph, "i2m", [128, 2, 128], BF16)
                S.dma("sp", i2[:], c_i2x[:, :, :], writes=[i2.res])
                kh = [sb(ph, "s2kh%d" % i, [128, FC, 2, 512], BF16) for i in range(3)]
                p1 = [sb(ph, "s2p1%d" % i, [128, 512], BF16) for i in range(4)]
                p2 = [sb(ph, "s2p2%d" % i, [128, 512], BF16) for i in range(4)]
                oo = [sb(ph, "s2o%d" % i, [128, FC, 512], BF16) for i in range(3)]
            pendB = []

            def flushB(f1, j, q1, q2, o, ch):
                pb = pB[f1 % 4]
                mm(pb[:, :], i2[:, 0, :], q1[:], True, False, [i2.res, q1.res], pb.res)
                mm(pb[:, :], i2[:, 1, :], q2[:], False, True, [i2.res, q2.res], pb.res)
                copy_op("act", o[:, j, :], pb[:, :], [pb.res], [o.res])
                if j == FC - 1:
                    for r_ in range(2):
                        S.dma("pool", D2[r_, ch * FC:(ch + 1) * FC, :, :].rearrange("f t c -> t f c"), o[64 * r_:64 * r_ + 64, :, :], reads=[o.res])

            for ch in range(128 // FC):
                a = ain[ch % 3]
                S.dma("sp", a[:], d1v[:, ch * FC:(ch + 1) * FC, :], writes=[a.res])
                if mode != "filter":
                    k = kh[ch % 3]
                    S.dma("sp", k[:].rearrange("p f a c -> p f (a c)"),
                          KH[l, ch * FC:(ch + 1) * FC].rearrange("f p a c -> p f (a c)"), writes=[k.res])
                    o = oo[ch % 3]
                for j in range(FC):
                    f1 = ch * FC + j
                    if mode == "filter":
                        pu = pU[f1 % 3]
                        mm(pu[:, 0, :], l2[:, 1, :], a[:, j, :], True, True, [l2.res, a.res], pu.res)
                        mm(pu[:, 1, :], l2[:, 2, :], a[:, j, :], True, True, [l2.res, a.res], pu.res)
                        t_ = st[f1 % 3]
                        copy_op(evac_eng(), t_[:], pu[:, :, :], [pu.res], [t_.res])
                        S.dma("pool", KH[l, f1].rearrange("p a c -> p (a c)"), t_[:].rearrange("p a c -> p (a c)"), reads=[t_.res])
                        continue
                    pu = pU[f1 % 4]
                    q1, q2 = p1[f1 % 4], p2[f1 % 4]
                    mm(pu[:, :], l2[:, 0, :], a[:, j, :], True, True, [l2.res, a.res], pu.res)
                    S.op("dve", lambda E: E.tensor_tensor(out=q1[:], in0=pu[:, :], in1=k[:, j, 0, :], op=ALU.mult), [pu.res, k.res], [q1.res])
                    S.op("dve", lambda E: E.tensor_tensor(out=q2[:], in0=pu[:, :], in1=k[:, j, 1, :], op=ALU.mult), [pu.res, k.res], [q2.res])
                    if pendB:
                        flushB(*pendB.pop())
                    pendB.append((f1, j, q1, q2, o, ch))
            if mode != "filter" and pendB:
                flushB(*pendB.pop())
            S.barrier()

    def conv3_rows(raw, out, w_t, b_ap, wcol, n):
        S.op("pool", lambda E: E.tensor_scalar(out=out[:, :n], in0=raw[:, 1:n + 1], scalar1=w_t[:, wcol, 1:2], scalar2=b_ap,
                                               op0=ALU.mult, op1=ALU.add), [raw.res, w_t.res], [out.res])
        S.op("dve", lambda E: E.scalar_tensor_tensor(out=out[:, :n], in0=raw[:, 0:n], scalar=w_t[:, wcol, 0:1], in1=out[:, :n],
                                                     op0=ALU.mult, op1=ALU.add), [raw.res, w_t.res, out.res], [out.res])
        S.op("dve", lambda E: E.scalar_tensor_tensor(out=out[:, :n], in0=raw[:, 2:n + 2], scalar=w_t[:, wcol, 2:3], in1=out[:, :n],
                                                     op0=ALU.mult, op1=ALU.add), [raw.res, w_t.res, out.res], [out.res])

    def load_halo(raw, src_rows, t0, n):
        lo, hi = max(t0 - 1, 0), min(t0 + n + 1, L_SEQ)
        wr = [raw.res]
        if t0 == 0:
            S.op("pool", lambda E: E.memset(raw[:, 0:1], 0.0), [], wr)
        if t0 + n >= L_SEQ:
            S.op("pool", lambda E: E.memset(raw[:, n + 1:n + 2], 0.0), [], wr)
        S.dma("sp", raw[:, lo - (t0 - 1):hi - (t0 - 1)], src_rows[:, lo:hi], writes=wr)

    def p2a_hyconv(l, s):
        with ExitStack() as ph:
            cw = sb(ph, "hcw", [128, 12, 3], F32)
            cb = sb(ph, "hcb", [128, 12], F32)
            S.dma("sp", cw[:], hy_conv_w[l], writes=[cw.res])
            S.dma("sp", cb[:], hy_conv_b[l], writes=[cb.res])
            HS = 2048
            raw = [sb(ph, "hraw%d" % i, [128, HS + 2], F32) for i in range(3)]
            cv = [sb(ph, "hcv%d" % i, [128, HS], F32) for i in range(3)]
            ut = [sb(ph, "hut%d" % i, [128, L_SEQ], BF16) for i in range(4)]
            ri = 0
            bgt = sb(ph, "hbg", [128, 24], F32)
            S.dma("sp", bgt[:], b_gate[l], writes=[bgt.res])
            gwt = [sb(ph, "hgw%d" % i, [128, 8, 128], BF16) for i in range(4)]
            gpp = [ps(ph, "hgp%d" % i, [128, TB], F32) for i in range(3)]
            gst = [sb(ph, "hgs%d" % i, [128, TB], BF16) for i in range(2)]
            gitems = [(tb, m) for tb in range(L_SEQ // TB) for m in range(24)]
            gstate = {"next": 0, "loaded": {}, "lw": 0}

            def gate_load(i):
                if i >= len(gitems) or i in gstate["loaded"]:
                    return
                w = gwt[gstate["lw"] % 4]
                gstate["lw"] += 1
                S.dma("sp", w[:], WG_b[l, gitems[i][1]], writes=[w.res])
                gstate["loaded"][i] = w

            def gate_items(n):
                for _ in range(n):
                    i = gstate["next"]
                    if i >= len(gitems):
                        return
                    gstate["next"] += 1
                    gate_load(i)
                    gate_load(i + 1)
                    gate_load(i + 2)
                    tb, m = gitems[i]
                    w = gstate["loaded"].pop(i)
                    t0g = tb * TB
                    p = gpp[i % 3]
                    for nt in range(TB // 512):
                        for kc in range(8):
                            mm(p[:, nt * 512:(nt + 1) * 512], w[:, kc, :], xT[:, kc, t0g + nt * 512:t0g + (nt + 1) * 512],
                               kc == 0, kc == 7, [w.res] + xt_res(t0g, TB), p.res)
                    o = gst[i % 2]
                    S.op("act", lambda E: E.activation(out=o[:], in_=p[:, :], func=AF.Sigmoid, bias=bgt[:, m:m + 1]), [p.res, bgt.res], [o.res])
                    S.dma("act", GT[s, m * 128:(m + 1) * 128, t0g:t0g + TB], o[:], reads=[o.res])

            for cc in range(4):
                for hs in range(2):
                    t0 = hs * HS
                    outs = []
                    for part in range(3):
                        ch = part * 4 + cc
                        r_, c_ = raw[ri % 3], cv[ri % 3]
                        ri += 1
                        load_halo(r_, XA[s, ch * 128:(ch + 1) * 128, :], t0, HS)
                        conv3_rows(r_, c_, cw, cb[:, ch:ch + 1], ch, HS)
                        outs.append(c_)
                        gate_items(4)
                    S.dma("pool", X0[s, cc * 128:(cc + 1) * 128, t0:t0 + HS], outs[0][:], reads=[outs[0].res])
                    S.op("pool", lambda E: E.tensor_tensor(out=ut[cc][:, t0:t0 + HS], in0=outs[1][:], in1=outs[2][:], op=ALU.mult),
                         [outs[1].res, outs[2].res], [ut[cc].res])
                S.dma("pool", UT[s, cc * 128:(cc + 1) * 128, :], ut[cc][:], reads=[ut[cc].res])
            gate_items(len(gitems))
            to_tokmajor(ph, ut, UTOK[s], "u")
            S.barrier()

    def p2d_hyout(l, s):
        with ExitStack() as ph:
            i1 = sb(ph, "i1m", [128, 2, 64], BF16)
            hb_ = sb(ph, "hbias", [128, 4], F32)
            S.dma("sp", i1[:], c_i1[:, :, :], writes=[i1.res])
            S.dma("sp", hb_[:], hy_bias[l], writes=[hb_.res])
            zT_ = sb(ph, "zTt", [128, 4, L_SEQ], BF16)
            TC = 4
            bin_ = [sb(ph, "bin%d" % i, [128, 2, TC, 512], BF16) for i in range(2)]
            d2v = D2.rearrange("r f t c -> f r t c")
            py = [ps(ph, "p2dy%d" % i, [64, 512], F32) for i in range(3)]
            pz = [ps(ph, "p2dz%d" % i, [128, 4, 64], BF16) for i in range(3)]
            ysb = [sb(ph, "ysb%d" % i, [64, 512], BF16) for i in range(3)]
            tw = sb(ph, "twd", [128, 2, 64], F32)
            S.dma("sp", tw[:], c_tw[:, :, :], writes=[tw.res])
            tw1 = [sb(ph, "tw1%d" % i, [128, 512], F32) for i in range(2)]
            tw2 = [sb(ph, "tw2%d" % i, [128, 512], F32) for i in range(2)]
            btw = [sb(ph, "btw%d" % i, [128, 2, 512], BF16) for i in range(4)]
            pendz = []
            pendm = []

            def flush_m(bt, t2):
                p, y, z = py[t2 % 3], ysb[t2 % 3], pz[t2 % 3]
                mm(p[:, :], i1[:, 0, :], bt[:, 0, :], True, False, [i1.res, bt.res], p.res)
                mm(p[:, :], i1[:, 1, :], bt[:, 1, :], False, True, [i1.res, bt.res], p.res)
                copy_op("act", y[:], p[:, :], [p.res], [y.res])
                if pendz:
                    flush_z(*pendz.pop())
                pendz.append((y, z, t2))

            def flush_z(y, z, t2):
                for cc in range(4):
                    S.op("pe", lambda E, cc=cc: E.transpose(z[:, cc, :], y[:, cc * 128:(cc + 1) * 128], identb[:64, :64]),
                         reads=[y.res, identb.res], writes=[z.res], pe_chain=True)
                copy_op("dve", zT_[:, :, t2:L_SEQ:64], z[:, :, :], [z.res], [zT_.res])

            for ch in range(64 // TC):
                b = bin_[ch % 2]
                S.dma("sp", b[:], d2v[:, :, ch * TC:(ch + 1) * TC, :], writes=[b.res])
                for j in range(TC):
                    t2 = ch * TC + j
                    a1, a2, bt = tw1[t2 % 2], tw2[t2 % 2], btw[t2 % 4]
                    Tr, Ti = tw[:, 0, t2:t2 + 1], tw[:, 1, t2:t2 + 1]
                    S.op("act", lambda E: E.activation(out=a1[:], in_=b[:, 1, j, :], func=AF.Identity, scale=Ti), [b.res, tw.res], [a1.res])
                    S.op("dve", lambda E: E.scalar_tensor_tensor(out=bt[:, 0, :], in0=b[:, 0, j, :], scalar=Tr, in1=a1[:], op0=ALU.mult, op1=ALU.subtract),
                         [b.res, tw.res, a1.res], [bt.res])
                    S.op("act", lambda E: E.activation(out=a2[:], in_=b[:, 0, j, :], func=AF.Identity, scale=Ti), [b.res, tw.res], [a2.res])
                    S.op("dve", lambda E: E.scalar_tensor_tensor(out=bt[:, 1, :], in0=b[:, 1, j, :], scalar=Tr, in1=a2[:], op0=ALU.mult, op1=ALU.add),
                         [b.res, tw.res, a2.res], [bt.res])
                    if pendm:
                        flush_m(*pendm.pop())
                    pendm.append((bt, t2))
            if pendm:
                flush_m(*pendm.pop())
            if pendz:
                flush_z(*pendz.pop())
            HS = 1024
            utl = [sb(ph, "utl%d" % i, [128, HS], BF16) for i in range(2)]
            x0l = [sb(ph, "x0l%d" % i, [128, HS], F32) for i in range(2)]
            tm = [sb(ph, "ytm%d" % i, [128, HS], F32) for i in range(2)]
            yo = [sb(ph, "yo%d" % i, [128, HS], BF16) for i in range(2)]
            k = 0
            for cc in range(4):
                for hs in range(L_SEQ // HS):
                    t0 = hs * HS
                    u_, x_, t_, o_ = utl[k % 2], x0l[k % 2], tm[k % 2], yo[k % 2]
                    k += 1
                    S.dma("sp", u_[:], UT[s, cc * 128:(cc + 1) * 128, t0:t0 + HS], writes=[u_.res])
                    S.dma("sp", x_[:], X0[s, cc * 128:(cc + 1) * 128, t0:t0 + HS], writes=[x_.res])
                    S.op("dve", lambda E: E.scalar_tensor_tensor(out=t_[:], in0=u_[:], scalar=hb_[:, cc:cc + 1], in1=zT_[:, cc, t0:t0 + HS],
                                                                 op0=ALU.mult, op1=ALU.add), [u_.res, hb_.res, zT_.res], [t_.res])
                    S.op("pool", lambda E: E.tensor_tensor(out=o_[:], in0=t_[:], in1=x_[:], op=ALU.mult), [t_.res, x_.res], [o_.res])
                    S.dma("pool", YA[s, cc * 128:(cc + 1) * 128, t0:t0 + HS], o_[:], reads=[o_.res])
            S.barrier()


    def p3_attention(l, s):
        with ExitStack() as ph:
            mask = sb(ph, "amask", [128, 2, 128], BF16)
            S.dma("sp", mask[:], c_mask[:, :, :], writes=[mask.res])
            raw = [sb(ph, "araw%d" % i, [64, L_SEQ], BF16) for i in range(2)]
            Qs = [sb(ph, "aQ%d" % i, [64, L_SEQ], BF16) for i in range(4)]
            Ks = [sb(ph, "aK%d" % i, [64, L_SEQ + 128 * 16], BF16) for i in range(4)]
            NV = 8
            vraw = [sb(ph, "avr%d" % i, [128, 256], BF16) for i in range(NV)]
            vint = [sb(ph, "avx%d" % i, [128, 4, 65], BF16) for i in range(NV)]
            vfirsts = [sb(ph, "avf%d" % i, [128, 4, 65], BF16) for i in range(4)]
            vlasts = [sb(ph, "avl%d" % i, [128, 4, 65], BF16) for i in range(4)]
            for v_ in vint + vfirsts + vlasts:
                S.op("pool", lambda E: E.memset(v_[:], 1.0), [], [v_.res])
            for v_ in vfirsts:
                S.op("pool", lambda E: E.memset(v_[0:64], 0.0), [], [v_.res])
            for v_ in vlasts:
                S.op("pool", lambda E: E.memset(v_[64:128], 0.0), [], [v_.res])
            negm = sb(ph, "anegm", [128, 2, 2, 128], BF16)
            S.dma("sp", negm[:], c_negm[:, :, :, :], writes=[negm.res])
            psc = [ps(ph, "asc%d" % i, [128, 4, 2, 128], F32) for i in range(3)]
            ppo = [ps(ph, "apo%d" % i, [128, 4, 65], F32) for i in range(2)]
            pe_ = [sb(ph, "ape%d" % i, [128, 4, 2, 128], BF16) for i in range(3)]
            aos = [sb(ph, "aos%d" % i, [128, 260], F32) for i in range(3)]
            cn = {"raw": 0, "v": 0, "vi": 0, "vf": 0, "vl": 0, "sc": 0, "po": 0, "ao": 0}
            for gi, d in enumerate((1, 4, 16)):
                n = L_SEQ // d
                W = n + 128
                for h in range(4):
                    for kind, dstt in ((0, Qs[h]), (1, Ks[h])):
                        r_ = raw[cn["raw"] % 2]
                        cn["raw"] += 1
                        S.dma("sp", r_[0:32, :], QK[s, kind, gi, 0, 32 * h:32 * h + 32, :], writes=[r_.res])
                        S.dma("sp", r_[32:64, :], QK[s, kind, gi, 1, 32 * h:32 * h + 32, :], writes=[r_.res])
                        src = r_[:, :].rearrange("p (i r) -> p r i", r=d)
                        if kind == 0:
                            dv = dstt[:, :].rearrange("p (r i) -> p r i", r=d)
                            copy_op(evac_eng(("dve", "act")), dv, src, [r_.res], [dstt.res])
                        else:
                            kv = dstt[:, :d * W].rearrange("p (r w) -> p r w", r=d)
                            S.op("pool", lambda E: E.memset(kv[:, :, 0:64], 0.0), [], [dstt.res])
                            S.op("pool", lambda E: E.memset(kv[:, :, 64 + n:W], 0.0), [], [dstt.res])
                            copy_op(evac_eng(("dve", "act")), kv[:, :, 64:64 + n], src, [r_.res], [dstt.res])
                nqb = n // 128
                blocks = [(r, qb) for r in range(d) for qb in range(nqb)]
                vt = {}
                st1 = {}

                def get_v(r, m):
                    if (r, m) in vt:
                        return vt[(r, m)]
                    vr = vraw[cn["v"] % NV]
                    if m == 0:
                        ve, p0, p1 = vfirsts[cn["vf"] % 4], 64, 128
                        cn["vf"] += 1
                    elif m == nqb:
                        ve, p0, p1 = vlasts[cn["vl"] % 4], 0, 64
                        cn["vl"] += 1
                    else:
                        ve, p0, p1 = vint[cn["vi"] % NV], 0, 128
                        cn["vi"] += 1
                    cn["v"] += 1
                    j0 = 128 * m - 64 + p0
                    npos = p1 - p0
                    pos0 = r + d * j0
                    srcv = VT[s, pos0:pos0 + d * (npos - 1) + 1:d, gi * 256:(gi + 1) * 256]
                    S.dma("sp", vr[p0:p1, :], srcv, writes=[vr.res])
                    S.op("pool", lambda E: E.tensor_copy(out=ve[p0:p1, :, 0:64], in_=vr[p0:p1, :].rearrange("p (h e) -> p h e", h=4)),
                         [vr.res], [ve.res])
                    vt[(r, m)] = ve
                    return ve

                def prefetch_v(i):
                    if i < len(blocks):
                        r_, qb_ = blocks[i]
                        get_v(r_, qb_)
                        get_v(r_, qb_ + 1)

                prefetch_v(0)
                prefetch_v(1)

                def stage1(i):
                    r, qb = blocks[i]
                    prefetch_v(i + 2)
                    va, vb = get_v(r, qb), get_v(r, qb + 1)
                    for key in [k_ for k_ in vt if (k_[0] < r) or (k_[0] == r and k_[1] < qb)]:
                        vt.pop(key)
                    sc = psc[cn["sc"] % 3]
                    pe1 = pe_[cn["sc"] % 3]
                    cn["sc"] += 1
                    for hp in range(2):
                        mm(sc[:, 2 * hp:2 * hp + 2, :, :], identb[:], negm[:, :, :, :], True, False, [identb.res, negm.res], sc.res)
                        for h in (2 * hp, 2 * hp + 1):
                            kv = Ks[h][:, :d * W].rearrange("p (r w) -> p r w", r=d)
                            qv = Qs[h][:, :].rearrange("p (r i) -> p r i", r=d)
                            q_ap = qv[:, r, 128 * qb:128 * qb + 128]
                            mm(sc[:, h, 0, :], kv[:, r, 128 * qb:128 * qb + 128], q_ap, False, False, [Ks[h].res, Qs[h].res], sc.res)
                            mm(sc[:, h, 1, :], kv[:, r, 128 * qb + 128:128 * qb + 256], q_ap, False, h == 2 * hp + 1, [Ks[h].res, Qs[h].res], sc.res)
                    S.op("act", lambda E: E.activation(out=pe1[:], in_=sc[:, :, :, :], func=AF.Exp, scale=0.125), [sc.res], [pe1.res])
                    st1[i] = (pe1, va, vb)

                def stage2(i):
                    r, qb = blocks[i]
                    pe1, va, vb = st1.pop(i)
                    po = ppo[cn["po"] % 2]
                    cn["po"] += 1
                    for h in range(4):
                        mm(po[:, h, :], pe1[:, h, 0, :], va[:, h, :], True, False, [pe1.res, va.res], po.res)
                        mm(po[:, h, :], pe1[:, h, 1, :], vb[:, h, :], False, True, [pe1.res, vb.res], po.res)
                    a = aos[cn["ao"] % 3]
                    cn["ao"] += 1
                    copy_op("dve", a[:].rearrange("p (h e) -> p h e", h=4), po[:, :, :], [po.res], [a.res])
                    pos0 = r + d * 128 * qb
                    S.dma("sp", AO[s, gi, pos0:pos0 + d * 127 + 1:d, :], a[:], reads=[a.res])

                for i in range(len(blocks) + 1):
                    if i < len(blocks):
                        stage1(i)
                    if i >= 1:
                        stage2(i - 1)
            S.barrier()

    def p3b_attn_merge(l, s):
        with ExitStack() as ph:
            a3 = [sb(ph, "m3a%d" % i, [128, 3, 260], F32) for i in range(2)]
            acc = [sb(ph, "m3acc%d" % i, [128, 260], F32) for i in range(2)]
            rd = [sb(ph, "m3rd%d" % i, [128, 4], F32) for i in range(2)]
            yb = [sb(ph, "m3yb%d" % i, [128, 256], BF16) for i in range(2)]
            pt = [ps(ph, "m3pt%d" % i, [128, 2, 128], BF16) for i in range(2)]
            ybT = [sb(ph, "m3ybT%d" % i, [128, 2, 1024], BF16) for i in range(2)]
            for tt in range(32):
                a, c, r_, y, p = a3[tt % 2], acc[tt % 2], rd[tt % 2], yb[tt % 2], pt[tt % 2]
                o = ybT[(tt // 8) % 2]
                S.dma("sp", a[:], AO[s, :, tt * 128:(tt + 1) * 128, :].rearrange("g t c -> t g c"), writes=[a.res])
                S.op("dve", lambda E: E.tensor_tensor(out=c[:], in0=a[:, 0, :], in1=a[:, 1, :], op=ALU.add), [a.res], [c.res])
                S.op("dve", lambda E: E.tensor_tensor(out=c[:], in0=c[:], in1=a[:, 2, :], op=ALU.add), [a.res, c.res], [c.res])
                cv = c[:].rearrange("p (h e) -> p h e", h=4)
                S.op("dve", lambda E: E.reciprocal(out=r_[:], in_=cv[:, :, 64]), [c.res], [r_.res])
                for h in range(4):
                    S.op("pool", lambda E, h=h: E.tensor_scalar(out=y[:, h * 64:(h + 1) * 64], in0=cv[:, h, 0:64], scalar1=r_[:, h:h + 1],
                                                                 scalar2=None, op0=ALU.mult), [c.res, r_.res], [y.res])
                for j in range(2):
                    S.op("pe", lambda E, j=j: E.transpose(p[:, j, :], y[:, j * 128:(j + 1) * 128], identb[:]),
                         reads=[y.res, identb.res], writes=[p.res], pe_chain=True)
                copy_op("act", o[:, :, (tt % 8) * 128:(tt % 8 + 1) * 128], p[:, :, :], [p.res], [o.res])
                if tt % 8 == 7:
                    t0 = (tt // 8) * 1024
                    S.dma("pool", YB[s].rearrange("(a p) t -> p a t", p=128)[:, :, t0:t0 + 1024], o[:, :, :], reads=[o.res])
            S.barrier()

    def p4_rglru(l, s):
        with ExitStack() as ph:
            cw = sb(ph, "rcw", [128, 4, 4], F32)
            cb = sb(ph, "rcb", [128, 4], F32)
            gb = sb(ph, "rgb", [128, 2, 2, 4], F32)
            lam = sb(ph, "rlam", [128, 8], F32)
            c8 = sb(ph, "rc8", [128, 8], F32)
            c16 = sb(ph, "rc16", [128, 8], F32)
            one = sb(ph, "rone", [128, 1], F32)
            S.dma("sp", cw[:], rg_conv_w[l], writes=[cw.res])
            S.dma("sp", cb[:], rg_conv_b[l], writes=[cb.res])
            S.dma("sp", gb[:], rg_gate_b[l], writes=[gb.res])
            S.dma("sp", lam[:], rg_lam[l].rearrange("p a b -> p (a b)"), writes=[lam.res])
            S.op("dve", lambda E: E.memset(one[:], 1.0), [], [one.res])
            S.op("act", lambda E: E.activation(out=c8[:], in_=lam[:], func=AF.Exp, scale=-1.0), [lam.res], [c8.res])
            S.op("act", lambda E: E.activation(out=c8[:], in_=c8[:], func=AF.Ln, bias=one[:]), [c8.res, one.res], [c8.res])
            S.op("dve", lambda E: E.tensor_scalar(out=c16[:], in0=c8[:], scalar1=-16.0, scalar2=None, op0=ALU.mult), [c8.res], [c16.res])
            S.op("dve", lambda E: E.tensor_scalar(out=c8[:], in0=c8[:], scalar1=-8.0, scalar2=None, op0=ALU.mult), [c8.res, c16.res], [c8.res])
            gw = [sb(ph, "rgw%d" % i, [128, 128], F32) for i in range(4)]
            raw = sb(ph, "rraw", [128, L_SEQ + 3], F32)
            xr = sb(ph, "rxr", [128, L_SEQ], F32)
            gt = sb(ph, "rgt", [128, L_SEQ], F32)
            a_t = sb(ph, "rat", [128, L_SEQ], F32)
            xn = sb(ph, "rxn", [128, L_SEQ], F32)
            hf = sb(ph, "rhf", [128, L_SEQ], F32)
            yo = sb(ph, "ryo", [128, L_SEQ], BF16)
            CH = 1024
            r_t = sb(ph, "rrt", [128, L_SEQ], F32)
            xrb = sb(ph, "rxrb", [128, L_SEQ], BF16)
            gwb = [sb(ph, "rgwb%d" % i, [128, 128], BF16) for i in range(4)]
            pg = [ps(ph, "rpg%d" % i, [128, CH], F32) for i in range(4)]
            pc = 0
            for cc in range(4):
                S.op("pool", lambda E: E.memset(raw[:, 0:2], 0.0), [], [raw.res])
                S.op("pool", lambda E: E.memset(raw[:, L_SEQ + 2:L_SEQ + 3], 0.0), [], [raw.res])
                S.dma("sp", raw[:, 2:L_SEQ + 2], XC[s, cc * 128:(cc + 1) * 128, :], writes=[raw.res])
                S.dma("sp", gt[:], XC[s, 512 + cc * 128:512 + (cc + 1) * 128, :], writes=[gt.res])
                S.op("pool", lambda E: E.tensor_scalar(out=xr[:], in0=raw[:, 2:L_SEQ + 2], scalar1=cw[:, cc, 2:3], scalar2=cb[:, cc:cc + 1],
                                                       op0=ALU.mult, op1=ALU.add), [raw.res, cw.res, cb.res], [xr.res])
                for k in (0, 1, 3):
                    S.op("dve", lambda E, k=k: E.scalar_tensor_tensor(out=xr[:], in0=raw[:, k:k + L_SEQ], scalar=cw[:, cc, k:k + 1], in1=xr[:],
                                                                      op0=ALU.mult, op1=ALU.add), [raw.res, cw.res, xr.res], [xr.res])
                for dirn in range(2):
                    for gate in range(2):
                        S.dma("sp", gw[dirn * 2 + gate][:], rg_gate_w[l, dirn, gate, cc], writes=[gw[dirn * 2 + gate].res])
                        copy_op("dve", gwb[dirn * 2 + gate][:], gw[dirn * 2 + gate][:], [gw[dirn * 2 + gate].res], [gwb[dirn * 2 + gate].res])
                copy_op("act", xrb[:], xr[:], [xr.res], [xrb.res])
                for dirn in range(2):
                    idx = dirn * 4 + cc
                    for c0 in range(0, L_SEQ, CH):
                        pr_, pi_ = pg[pc % 4], pg[(pc + 1) % 4]
                        pc += 2
                        for nt in range(CH // 512):
                            sl = slice(c0 + nt * 512, c0 + (nt + 1) * 512)
                            mm(pr_[:, nt * 512:(nt + 1) * 512], gwb[dirn * 2][:], xrb[:, sl], True, True, [gwb[dirn * 2].res, xrb.res], pr_.res)
                            mm(pi_[:, nt * 512:(nt + 1) * 512], gwb[dirn * 2 + 1][:], xrb[:, sl], True, True, [gwb[dirn * 2 + 1].res, xrb.res], pi_.res)
                        S.op("act", lambda E: E.activation(out=r_t[:, c0:c0 + CH], in_=pr_[:], func=AF.Sigmoid, bias=gb[:, dirn, 0, cc:cc + 1]), [pr_.res, gb.res], [r_t.res])
                        S.op("act", lambda E: E.activation(out=xn[:, c0:c0 + CH], in_=pi_[:], func=AF.Sigmoid, bias=gb[:, dirn, 1, cc:cc + 1]), [pi_.res, gb.res], [xn.res])
                    S.op("pool", lambda E: E.tensor_tensor(out=xn[:], in0=xn[:], in1=xr[:], op=ALU.mult), [xn.res, xr.res], [xn.res])
                    S.op("act", lambda E: E.activation(out=a_t[:], in_=r_t[:], func=AF.Exp, scale=c8[:, idx:idx + 1]), [r_t.res, c8.res], [a_t.res])
                    S.op("act", lambda E: E.activation(out=r_t[:], in_=r_t[:], func=AF.Exp, scale=c16[:, idx:idx + 1]), [r_t.res, c16.res], [r_t.res])
                    S.op("act", lambda E: E.activation(out=r_t[:], in_=r_t[:], func=AF.Sqrt, scale=-1.0, bias=one[:]), [r_t.res, one.res], [r_t.res])
                    bcol = 0 if dirn == 0 else L_SEQ - 1
                    S.op("dve", lambda E: E.memset(r_t[:, bcol:bcol + 1], 1.0), [r_t.res], [r_t.res])
                    S.op("dve", lambda E: E.tensor_tensor(out=xn[:], in0=xn[:], in1=r_t[:], op=ALU.mult), [xn.res, r_t.res], [xn.res])
                    if dirn == 0:
                        S.op("dve", lambda E: E.tensor_tensor_scan(out=hf[:, :], data0=a_t[:, :], data1=xn[:, :], initial=0.0, op0=ALU.mult, op1=ALU.add),
                             [a_t.res, xn.res], [hf.res])
                    else:
                        hb = raw
                        S.op("dve", lambda E: E.tensor_tensor_scan(out=hb[:, L_SEQ - 1::-1], data0=a_t[:, ::-1], data1=xn[:, ::-1], initial=0.0,
                                                                  op0=ALU.mult, op1=ALU.add), [a_t.res, xn.res], [hb.res])
                        S.op("pool", lambda E: E.tensor_tensor(out=hf[:], in0=hf[:], in1=hb[:, 0:L_SEQ], op=ALU.add), [hf.res, hb.res], [hf.res])
                S.op("act", lambda E: E.activation(out=gt[:], in_=gt[:], func=AF.Gelu_apprx_tanh), [gt.res], [gt.res])
                S.op("pool", lambda E: E.tensor_tensor(out=yo[:], in0=hf[:], in1=gt[:], op=ALU.mult), [hf.res, gt.res], [yo.res])
                S.dma("pool", YC[s, cc * 128:(cc + 1) * 128, :], yo[:], reads=[yo.res])
            S.barrier()


    def ln_epilogue(lnbuf, po, xres, gB, bB, epsT, k):
        ysb, st, ti, out = lnbuf["y"][k % 2], lnbuf["st"][k % 2], lnbuf["ti"][k % 2], lnbuf["o"][k % 2]
        I32 = mybir.dt.int32
        S.op("dve", lambda E: E.memset(st[:, 0:2], 0.0), [st.res], [st.res])
        S.op("dve", lambda E: E.scalar_tensor_tensor(out=ysb[:], in0=xres[:], scalar=float(ALPHA), in1=po[:, :], op0=ALU.mult, op1=ALU.add,
                                                     accum_out=st[:, 0:1]), [xres.res, po.res, st.res], [ysb.res, st.res])
        S.op("dve", lambda E: E.scalar_tensor_tensor(out=out[:], in0=ysb[:], scalar=1.0, in1=ysb[:], op0=ALU.mult, op1=ALU.mult,
                                                     accum_out=st[:, 1:2]), [ysb.res, st.res], [out.res, st.res])
        S.op("dve", lambda E: E.tensor_scalar(out=st[:, 2:3], in0=st[:, 0:1], scalar1=1.0 / D, scalar2=None, op0=ALU.mult), [st.res], [st.res])
        S.op("dve", lambda E: E.tensor_tensor(out=st[:, 3:4], in0=st[:, 2:3], in1=st[:, 2:3], op=ALU.mult), [st.res], [st.res])
        S.op("dve", lambda E: E.scalar_tensor_tensor(out=st[:, 4:5], in0=st[:, 1:2], scalar=1.0 / D, in1=st[:, 3:4], op0=ALU.mult, op1=ALU.subtract),
             [st.res], [st.res])
        S.op("dve", lambda E: E.tensor_scalar(out=st[:, 4:5], in0=st[:, 4:5], scalar1=float(LN_EPS), scalar2=None, op0=ALU.add), [st.res], [st.res])
        S.op("dve", lambda E: E.tensor_single_scalar(out=ti[:, 0:1], in_=st[:, 4:5].bitcast(I32), scalar=1, op=ALU.logical_shift_right),
             [st.res], [ti.res])
        S.op("dve", lambda E: E.tensor_scalar(out=ti[:, 1:2], in0=ti[:, 0:1], scalar1=-1.0, scalar2=1597463007.0, op0=ALU.mult, op1=ALU.add),
             [ti.res], [ti.res])
        S.op("dve", lambda E: E.tensor_copy(out=st[:, 6:7], in_=ti[:, 1:2].bitcast(F32)), [ti.res], [st.res])
        for _ in range(3):
            S.op("dve", lambda E: E.scalar_tensor_tensor(out=st[:, 5:6], in0=st[:, 6:7], scalar=st[:, 4:5], in1=st[:, 6:7], op0=ALU.mult, op1=ALU.mult),
                 [st.res], [st.res])
            S.op("dve", lambda E: E.tensor_scalar(out=st[:, 5:6], in0=st[:, 5:6], scalar1=-0.5, scalar2=1.5, op0=ALU.mult, op1=ALU.add), [st.res], [st.res])
            S.op("dve", lambda E: E.tensor_tensor(out=st[:, 6:7], in0=st[:, 6:7], in1=st[:, 5:6], op=ALU.mult), [st.res], [st.res])
        S.op("dve", lambda E: E.tensor_scalar(out=ysb[:], in0=ysb[:], scalar1=st[:, 2:3], scalar2=st[:, 6:7], op0=ALU.subtract, op1=ALU.mult),
             [ysb.res, st.res], [ysb.res])
        S.op("pool", lambda E: E.tensor_tensor(out=out[:], in0=ysb[:], in1=gB[:], op=ALU.mult), [ysb.res, gB.res], [out.res])
        S.op("pool", lambda E: E.tensor_tensor(out=out[:], in0=out[:], in1=bB[:], op=ALU.add), [out.res, bB.res], [out.res])
        return out

    def ln_epilogue_act(lnbuf, po, xres, gB, bB, epsT, k):
        ysb, st, out = lnbuf["y"][k % 2], lnbuf["st"][k % 2], lnbuf["o"][k % 2]
        S.op("dve", lambda E: E.scalar_tensor_tensor(out=ysb[:], in0=xres[:], scalar=float(ALPHA), in1=po[:, :], op0=ALU.mult, op1=ALU.add),
             [xres.res, po.res], [ysb.res])
        S.op("act", lambda E: E.activation(out=out[:], in_=ysb[:], func=AF.Identity, accum_out=st[:, 0:1]), [ysb.res], [out.res, st.res])
        S.op("act", lambda E: E.activation(out=out[:], in_=ysb[:], func=AF.Square, accum_out=st[:, 1:2]), [ysb.res, out.res], [out.res, st.res])
        S.op("dve", lambda E: E.tensor_scalar(out=st[:, 2:3], in0=st[:, 0:1], scalar1=1.0 / D, scalar2=None, op0=ALU.mult), [st.res], [st.res])
        S.op("dve", lambda E: E.tensor_tensor(out=st[:, 3:4], in0=st[:, 2:3], in1=st[:, 2:3], op=ALU.mult), [st.res], [st.res])
        S.op("dve", lambda E: E.scalar_tensor_tensor(out=st[:, 4:5], in0=st[:, 1:2], scalar=1.0 / D, in1=st[:, 3:4], op0=ALU.mult, op1=ALU.subtract),
             [st.res], [st.res])
        S.op("act", lambda E: E.activation(out=st[:, 5:6], in_=st[:, 4:5], func=AF.Sqrt, bias=epsT[:]), [st.res, epsT.res], [st.res])
        S.op("dve", lambda E: E.reciprocal(out=st[:, 6:7], in_=st[:, 5:6]), [st.res], [st.res])
        S.op("dve", lambda E: E.tensor_scalar(out=ysb[:], in0=ysb[:], scalar1=st[:, 2:3], scalar2=st[:, 6:7], op0=ALU.subtract, op1=ALU.mult),
             [ysb.res, st.res], [ysb.res])
        S.op("pool", lambda E: E.tensor_tensor(out=out[:], in0=ysb[:], in1=gB[:], op=ALU.mult), [ysb.res, gB.res], [out.res])
        S.op("pool", lambda E: E.tensor_tensor(out=out[:], in0=out[:], in1=bB[:], op=ALU.add), [out.res, bB.res], [out.res])
        return out

    def ln_bufs(ph, tag):
        return {"y": [sb(ph, "lny%s%d" % (tag, i), [128, D], F32) for i in range(2)],
                "st": [sb(ph, "lnst%s%d" % (tag, i), [128, 8], F32) for i in range(2)],
                "ti": [sb(ph, "lnti%s%d" % (tag, i), [128, 2], mybir.dt.int32) for i in range(2)],
                "o": [sb(ph, "lno%s%d" % (tag, i), [128, D], F32) for i in range(2)]}

    def to_xT(bufs, xo, tok0, k):
        xb, pt = bufs["xb"][k % 2], bufs["pt"][k % 2]
        copy_op("act", xb[:], xo[:], [xo.res], [xb.res])
        for kc in range(8):
            S.op("pe", lambda E, kc=kc: E.transpose(pt[:, kc, :], xb[:, kc * 128:(kc + 1) * 128], identb[:]),
                 reads=[xb.res, identb.res], writes=[pt.res], pe_chain=True)
        copy_op("dve", xT[:, :, tok0:tok0 + 128], pt[:, :, :], [pt.res], [xT.rs[tok0 // 128]])

    def p5_merge(l, s):
        with ExitStack() as ph:
            wo = sb(ph, "wo", [128, 8, D], BF16)
            S.dma("sp", wo[:], WO_b[l], writes=[wo.res])
            bg = sb(ph, "bg", [128, 24], F32)
            S.dma("sp", bg[:], b_gate[l], writes=[bg.res])
            gB = sb(ph, "ln1g", [128, D], F32)
            bB = sb(ph, "ln1b", [128, D], F32)
            S.dma("sp", gB[:], ln1_g[l], writes=[gB.res])
            S.dma("sp", bB[:], ln1_b[l], writes=[bB.res])
            epsT = sb(ph, "eps1", [128, 1], F32)
            S.op("dve", lambda E: E.memset(epsT[:], LN_EPS), [], [epsT.res])
            TBm = 512
            NG = 6
            gtl = [sb(ph, "p5gt%d" % i, [128, TBm], BF16) for i in range(NG)]
            yin = [sb(ph, "p5y%d" % i, [128, 10, TBm], BF16) for i in range(2)]
            NWB = NG
            wb = [sb(ph, "p5wb%d" % i, [128, 4, 128], BF16) for i in range(NWB)]
            pp = [ps(ph, "p5pp%d" % i, [128, TBm], F32) for i in range(4)]
            tmpm = [sb(ph, "p5tm%d" % i, [128, TBm], F32) for i in range(1)] * 2
            maccs = [sb(ph, "p5macc%d" % i, [128, TBm], F32) for i in range(2)]
            tmps = [sb(ph, "p5tmps%d" % i, [128, TBm], F32) for i in range(4)]
            mixT = [sb(ph, "p5mix%d" % i, [128, 8, TBm], BF16) for i in range(2)]
            po = [ps(ph, "p5po%d" % i, [128, D], F32) for i in range(2)]
            xres = [sb(ph, "p5xr%d" % i, [128, D], F32) for i in range(2)]
            lnb = ln_bufs(ph, "a")
            KC = (4, 2, 4)
            WB = (WBA_b, WBB_b, WBC_b)
            yoff = (0, 4, 6)
            cn = {"w": 0, "p": 0, "g": 0, "k": 0}
            pend = []
            x_src = x_in[s] if l == 0 else X2[s]
            NB = L_SEQ // TBm
            items = [(m, br) for m in range(8) for br in range(3)]

            def wo_tile(tb, tt):
                mx = mixT[tb % 2]
                tok0 = tb * TBm + tt * 128
                k = cn["k"]
                cn["k"] += 1
                p_ = po[k % 2]
                xr_ = xres[k % 2]
                S.dma("sp", xr_[:], x_src[tok0:tok0 + 128, :], writes=[xr_.res])
                for nh in range(2):
                    for kc in range(8):
                        mm(p_[:, nh * 512:(nh + 1) * 512], mx[:, kc, tt * 128:(tt + 1) * 128], wo[:, kc, nh * 512:(nh + 1) * 512],
                           kc == 0, kc == 7, [mx.res, wo.res], p_.res)
                o_ = ln_epilogue(lnb, p_, xr_, gB, bB, epsT, k)
                S.dma("pool", X1[s, tok0:tok0 + 128, :], o_[:], reads=[o_.res])

            def load_y(tb_):
                yy_ = yin[tb_ % 2]
                ta = tb_ * TBm
                S.dma("sp", yy_[:, 0:4, :], YA[s].rearrange("(a p) t -> p a t", p=128)[:, :, ta:ta + TBm], writes=[yy_.res])
                S.dma("sp", yy_[:, 4:6, :], YB[s].rearrange("(a p) t -> p a t", p=128)[:, :, ta:ta + TBm], writes=[yy_.res])
                S.dma("sp", yy_[:, 6:10, :], YC[s].rearrange("(a p) t -> p a t", p=128)[:, :, ta:ta + TBm], writes=[yy_.res])

            for tb in range(NB + 1):
                if tb == NB:
                    for tt in range(TBm // 128):
                        wo_tile(tb - 1, tt)
                    break
                t0 = tb * TBm
                if tb == 0:
                    load_y(0)
                if tb + 1 < NB:
                    load_y(tb + 1)
                yi = yin[tb % 2]
                mx = mixT[tb % 2]
                loaded = {}

                def load(i):
                    m, br = items[i]
                    b = wb[cn["w"] % NWB]
                    gt_ = gtl[cn["w"] % NG]
                    cn["w"] += 1
                    col = br * 8 + m
                    S.dma("sp", b[:, :KC[br], :], WB[br][l, m], writes=[b.res])
                    S.dma("sp", gt_[:], GT[s, col * 128:(col + 1) * 128, t0:t0 + TBm], writes=[gt_.res])
                    loaded[i] = (b, gt_)

                PF = 3
                for i in range(len(items) + PF):
                    if i < len(items):
                        load(i)
                    j = i - PF
                    if j < 0:
                        continue
                    m, br = items[j]
                    b, gt_ = loaded.pop(j)
                    pt_ = pp[cn["p"] % 4]
                    cn["p"] += 1
                    col = br * 8 + m
                    for kc in range(KC[br]):
                        mm(pt_[:, :], b[:, kc, :], yi[:, yoff[br] + kc, :], kc == 0, kc == KC[br] - 1, [b.res, yi.res], pt_.res)
                    mac_ = maccs[m % 2]
                    if br == 0:
                        S.op("dve", lambda E: E.tensor_tensor(out=mac_[:], in0=gt_[:], in1=pt_[:, :], op=ALU.mult), [gt_.res, pt_.res], [mac_.res])
                    else:
                        t_ = tmps[(m % 2) * 2 + (br - 1)]
                        S.op("dve", lambda E: E.tensor_tensor(out=t_[:], in0=gt_[:], in1=pt_[:, :], op=ALU.mult), [gt_.res, pt_.res], [t_.res])
                        if br == 1:
                            S.op("dve", lambda E: E.tensor_tensor(out=mac_[:], in0=mac_[:], in1=t_[:], op=ALU.add), [mac_.res, t_.res], [mac_.res])
                        else:
                            S.op("dve", lambda E: E.tensor_tensor(out=mx[:, m, :], in0=mac_[:], in1=t_[:], op=ALU.add), [mac_.res, t_.res], [mx.res])
                    if tb >= 1 and j % 6 == 5:
                        wo_tile(tb - 1, j // 6)
            S.barrier()

    def p6_ffn(l, s, last):
        with ExitStack() as ph:
            wdn = sb(ph, "wdn", [128, 24, D], BF16)
            S.dma("sp", wdn[:, 0:12, :], WDN_b[l, :, 0:12, :], writes=[wdn.res])
            S.dma("sp", wdn[:, 12:24, :], WDN_b[l, :, 12:24, :], writes=[wdn.res])
            fw = sb(ph, "ffw", [128, 24, 3], F32)
            fb = sb(ph, "ffb", [128, 24], F32)
            S.dma("sp", fw[:], ffn_conv_w[l], writes=[fw.res])
            S.dma("sp", fb[:], ffn_conv_b[l], writes=[fb.res])
            gB = sb(ph, "ln2g", [128, D], F32)
            bB = sb(ph, "ln2b", [128, D], F32)
            S.dma("sp", gB[:], ln2_g[l], writes=[gB.res])
            S.dma("sp", bB[:], ln2_b[l], writes=[bB.res])
            epsT = sb(ph, "eps2", [128, 1], F32)
            S.op("dve", lambda E: E.memset(epsT[:], LN_EPS), [], [epsT.res])
            TBf = 512
            wu = [sb(ph, "p6wu%d" % i, [128, 2, 8, 128], BF16) for i in range(4)]
            pgt = [ps(ph, "p6pg%d" % i, [128, 1024], F32) for i in range(2)]
            put = [ps(ph, "p6pu%d" % i, [128, 512], F32) for i in range(2)]
            po = ps(ph, "p6po", [128, D], F32)
            yv = [sb(ph, "p6y%d" % i, [128, TBf], F32) for i in range(2)]
            gl = [sb(ph, "p6g%d" % i, [128, TBf], F32) for i in range(2)]
            actT = [sb(ph, "p6act%d" % i, [128, 24, TBf], BF16) for i in range(1)]
            xres = [sb(ph, "p6xr%d" % i, [128, D], F32) for i in range(2)]
            lnb = ln_bufs(ph, "b")
            cn = {"w": 0, "p": 0, "k": 0}
            for tb in range(L_SEQ // TBf):
                t0 = tb * TBf
                at = actT[0]
                first, lastb = (t0 == 0), (t0 + TBf == L_SEQ)
                loaded = {}

                def load(i):
                    w = wu[cn["w"] % 4]
                    cn["w"] += 1
                    S.dma("sp", w[:, 0, :, :], WUP_b[l, i], writes=[w.res])
                    S.dma("sp", w[:, 1, :, :], WUP_b[l, 24 + i], writes=[w.res])
                    loaded[i] = w

                PF = 2
                for i in range(24 + PF):
                    if i < 24:
                        load(i)
                    m = i - PF
                    if m < 0:
                        continue
                    w = loaded.pop(m)
                    pg_, pu_ = pgt[cn["p"] % 2], put[cn["p"] % 2]
                    y_, g_ = yv[cn["p"] % 2], gl[cn["p"] % 2]
                    cn["p"] += 1
                    c_lo = 1 if first else 0
                    c_hi = 513 if lastb else 514
                    for (ca, cb_) in ((c_lo, 512), (512, c_hi)):
                        ta, tb_ = t0 - 1 + ca, t0 - 1 + cb_
                        for kc in range(8):
                            mm(pg_[:, ca:cb_], w[:, 0, kc, :], xT[:, kc, ta:tb_], kc == 0, kc == 7, [w.res] + xt_res(ta, tb_ - ta), pg_.res)
                    for kc in range(8):
                        mm(pu_[:, :], w[:, 1, kc, :], xT[:, kc, t0:t0 + TBf], kc == 0, kc == 7, [w.res] + xt_res(t0, TBf), pu_.res)
                    S.op("act", lambda E: E.activation(out=y_[:], in_=pg_[:, 1:513], func=AF.Identity, scale=fw[:, m, 1:2], bias=fb[:, m:m + 1]),
                         [pg_.res, fw.res, fb.res], [y_.res])
                    S.op("dve", lambda E: E.scalar_tensor_tensor(out=y_[:, c_lo:512], in0=pg_[:, c_lo:512], scalar=fw[:, m, 0:1], in1=y_[:, c_lo:512],
                                                                 op0=ALU.mult, op1=ALU.add), [pg_.res, fw.res, y_.res], [y_.res])
                    nh = c_hi - 2
                    S.op("dve", lambda E: E.scalar_tensor_tensor(out=y_[:, 0:nh], in0=pg_[:, 2:2 + nh], scalar=fw[:, m, 2:3], in1=y_[:, 0:nh],
                                                                 op0=ALU.mult, op1=ALU.add), [pg_.res, fw.res, y_.res], [y_.res])
                    S.op("act", lambda E: E.activation(out=g_[:], in_=y_[:], func=AF.Gelu_apprx_tanh), [y_.res], [g_.res])
                    S.op("dve", lambda E: E.tensor_tensor(out=at[:, m, :], in0=g_[:], in1=pu_[:, :], op=ALU.mult), [g_.res, pu_.res], [at.res])
                for tt in range(TBf // 128):
                    tok0 = t0 + tt * 128
                    k = cn["k"]
                    cn["k"] += 1
                    xr_ = xres[k % 2]
                    S.dma("sp", xr_[:], X1[s, tok0:tok0 + 128, :], writes=[xr_.res])
                    for nh in range(2):
                        for kc in range(24):
                            mm(po[:, nh * 512:(nh + 1) * 512], at[:, kc, tt * 128:(tt + 1) * 128], wdn[:, kc, nh * 512:(nh + 1) * 512],
                               kc == 0, kc == 23, [at.res, wdn.res], po.res)
                    o_ = ln_epilogue_act(lnb, po, xr_, gB, bB, epsT, k)
                    if last:
                        S.dma("pool", y_out[s, tok0:tok0 + 128, :], o_[:], reads=[o_.res])
                    else:
                        S.dma("pool", X2[s, tok0:tok0 + 128, :], o_[:], reads=[o_.res])
            S.barrier()

    g.p5_merge = p5_merge
    g.p6_ffn = p6_ffn

    def hyena_filter_all(l):
        pf_filter(l)
        fft_stage1(KTOK, 128)
        fft_stage2("filter", l)

    def hyena_seq(l, s):
        p2a_hyconv(l, s)
        fft_stage1(UTOK[s], 64)
        fft_stage2("signal", l)
        p2d_hyout(l, s)

    g.hyena_filter_all = hyena_filter_all
    g.hyena_seq = hyena_seq
    g.pf_filter = pf_filter
    g.p3_attention = p3_attention
    g.p3b_attn_merge = p3b_attn_merge
    g.p4_rglru = p4_rglru

    g.cast_weights = cast_weights
    g.p1a_load_x = p1a_load_x
    g.p1b_inproj = p1b_inproj
    return S, g, es, locals()


def _pcol(v):
    v = np.asarray(v)
    C = v.shape[-1]
    lead = v.shape[:-1]
    v = v.reshape(lead + (C // 128, 128))
    v = np.moveaxis(v, -1, 0)
    v = np.moveaxis(v, -1, 1)
    return np.ascontiguousarray(v)


def _qk_perm():
    cols = list(range(1536))
    for kind in range(2):
        base = 1536 + kind * 768
        for gi in range(3):
            for half in range(2):
                for h in range(4):
                    for e in range(32):
                        cols.append(base + (gi * 4 + h) * 64 + half * 32 + e)
    cols += list(range(3072, 4864))
    return np.array(cols)


_CONST_CACHE = {}


def make_consts():
    if _CONST_CACHE:
        return _CONST_CACHE
    bf = ml_dtypes.bfloat16
    f32 = np.float32
    c = {}
    c["c_identf"] = np.eye(128, dtype=f32)
    c["c_identb"] = np.eye(128).astype(bf)
    inv = (np.float32(10000.0) ** (-np.arange(0, 64, 2, dtype=f32) / np.float32(64))).astype(f32)
    ang = (np.arange(L_SEQ, dtype=f32)[:, None] * inv[None, :]).astype(f32)
    c["c_ropec"] = np.ascontiguousarray(np.tile(np.cos(ang).T.astype(f32), (4, 1)))
    c["c_ropes"] = np.ascontiguousarray(np.tile(np.sin(ang).T.astype(f32), (4, 1)))
    t = np.linspace(0.0, 1.0, L_SEQ, dtype=f32)[:, None]
    w = (np.float32(2.0 * math.pi) * np.arange(L_SEQ, dtype=f32)[:, None] / np.float32(L_SEQ)).astype(f32)
    bands = np.linspace(1e-4, 7, 8, dtype=f32)[None, :]
    z = np.concatenate([t, np.cos(bands * w), -np.sin(bands * w)], axis=-1).astype(f32)
    c["c_zT"] = np.ascontiguousarray(z.T)
    c["c_tv"] = np.ascontiguousarray(np.tile(t.T, (128, 1)).astype(f32))
    deltas = np.abs(np.linspace(math.log(1e-2) / 1.5, math.log(1e-2) / 0.3, D_HY, dtype=f32))
    c["c_ndelta"] = _pcol(-deltas).astype(f32)
    n1 = np.arange(128)[:, None]
    f1 = np.arange(128)[None, :]
    a = 2 * np.pi * n1 * f1 / 128.0
    c["c_f1"] = np.stack([np.cos(a), -np.sin(a)], axis=1).astype(bf)
    f1c = np.arange(128)[:, None]
    n2 = np.arange(64)[None, :]
    a = 2 * np.pi * f1c * n2 / NFFT
    c["c_tw"] = np.stack([np.cos(a), np.sin(a)], axis=1).astype(f32)
    n2c = np.arange(64)[:, None]
    f2 = np.arange(64)[None, :]
    a = 2 * np.pi * n2c * f2 / 64.0
    C2, S2 = np.cos(a), np.sin(a)
    l2re = np.concatenate([C2, S2], axis=0)
    l2im = np.concatenate([-S2, C2], axis=0)
    c["c_l2x"] = np.stack([np.concatenate([l2re, l2im], axis=1), np.concatenate([l2re, l2re], axis=1),
                           np.concatenate([-l2im, l2im], axis=1)], axis=1).astype(bf)
    m0 = np.concatenate([np.concatenate([C2, S2], axis=1), np.concatenate([-S2, C2], axis=1)], axis=0)
    m1 = np.concatenate([np.concatenate([S2, -C2], axis=1), np.concatenate([-C2, -S2], axis=1)], axis=0)
    c["c_i2x"] = np.stack([m0, m1], axis=1).astype(bf)
    f1c = np.arange(128)[:, None]
    t1 = np.arange(64)[None, :]
    a = 2 * np.pi * f1c * t1 / 128.0
    c["c_i1"] = np.stack([np.cos(a) / NFFT, -np.sin(a) / NFFT], axis=1).astype(bf)
    p = np.arange(128)[:, None]
    j = np.arange(128)[None, :]
    c["c_mask"] = np.stack([(p >= j), (p <= j)], axis=1).astype(bf)
    neg = np.where(np.stack([(p >= j), (p <= j)], axis=1), 0.0, -30000.0)
    c["c_negm"] = np.stack([neg, neg], axis=1).astype(bf)
    _CONST_CACHE.update(c)
    return c


def prep_shared(inputs, NL=NLAYER):
    f32 = np.float32
    g = {}
    perm = _qk_perm()
    g["w_in"] = np.ascontiguousarray(np.asarray(inputs["w_in"], f32)[:NL][:, :, perm])
    for k in ("w_gate", "w_br_a", "w_br_b", "w_br_c", "w_o", "w_up", "w_down"):
        g[k] = np.ascontiguousarray(np.asarray(inputs[k], f32)[:NL])
    A = lambda k: np.asarray(inputs[k], f32)[:NL]
    g["hy_conv_w"] = np.stack([_pcol(A("hy_conv_w")[l]) for l in range(NL)])
    g["hy_conv_b"] = np.stack([_pcol(A("hy_conv_b")[l]) for l in range(NL)])
    g["hy_bias"] = np.stack([_pcol(A("hy_bias")[l]) for l in range(NL)])
    g["hy_w1"] = A("hy_filt_w1")
    g["hy_w2"] = A("hy_filt_w2")
    g["hy_w3"] = A("hy_filt_w3")
    g["hy_b1"] = A("hy_filt_b1")[:, :, None]
    g["hy_b2"] = A("hy_filt_b2")[:, :, None]
    g["hy_fr"] = A("hy_filt_freq")[:, :, None]
    g["hy_b3"] = np.stack([_pcol(A("hy_filt_b3")[l]) for l in range(NL)])
    g["rg_conv_w"] = np.stack([_pcol(A("rg_conv_w")[l]) for l in range(NL)])
    g["rg_conv_b"] = np.stack([_pcol(A("rg_conv_b")[l]) for l in range(NL)])
    gw = A("rg_gate_w")
    bd = np.zeros((NL, 2, 2, 4, 128, 128), f32)
    for cc in range(4):
        bd[:, :, :, cc, 0:64, 0:64] = gw[:, :, :, 2 * cc]
        bd[:, :, :, cc, 64:128, 64:128] = gw[:, :, :, 2 * cc + 1]
    g["rg_gate_w"] = bd
    g["rg_gate_b"] = np.stack([_pcol(A("rg_gate_b")[l]) for l in range(NL)])
    g["rg_gate_b"] = np.ascontiguousarray(np.transpose(g["rg_gate_b"], (0, 1, 3, 4, 2)))
    g["rg_lam"] = np.ascontiguousarray(np.transpose(np.stack([_pcol(A("rg_lam")[l]) for l in range(NL)]), (0, 1, 3, 2)))
    g["b_gate"] = np.stack([_pcol(A("b_gate")[l]) for l in range(NL)])
    g["ffn_conv_w"] = np.stack([_pcol(A("ffn_conv_w")[l]) for l in range(NL)])
    g["ffn_conv_b"] = np.stack([_pcol(A("ffn_conv_b")[l]) for l in range(NL)])
    for k in ("ln1_g", "ln1_b", "ln2_g", "ln2_b"):
        g[k] = np.ascontiguousarray(np.broadcast_to(A(k)[:, None, :], (NL, 128, D)))
    g.update(make_consts())
    return {k: np.ascontiguousarray(v) for k, v in g.items()}


def build_full(nc, NS=2, NL=NLAYER, dbg=None):
    S, g, es, loc = build_program(nc, NS=NS, NL=NL, dbg=dbg)
    for l in range(NL):
        g.cast_weights(l)
        g.hyena_filter_all(l)
    for s in range(NS):
        g.p1a_load_x(s)
        for l in range(NL):
            last = (l == NL - 1)
            g.p1b_inproj(l, s)
            g.hyena_seq(l, s)
            g.p3_attention(l, s)
            g.p3b_attn_merge(l, s)
            g.p4_rglru(l, s)
            g.p5_merge(l, s)
            g.p1a_load_x(s, loc["X1"][s])
            g.p6_ffn(l, s, last)
            if not last:
                g.p1a_load_x(s, loc["X2"][s])
    S.barrier()
    es.close()
    return S


def kernel(**inputs):
    n_cores = 8
    xp = np.asarray(inputs["x_prompt"], np.float32)
    xs = np.asarray(inputs["x_sample"], np.float32)
    shared = prep_shared(inputs, NL=NLAYER)
    nc = bass.Bass("TRN2", target_bir_lowering=False)
    build_full(nc, NS=2, NL=NLAYER)
    in_maps = []
    for c in range(n_cores):
        m = dict(shared)
        m["x"] = np.ascontiguousarray(np.stack([xp[c], xs[c % 4]]))
        in_maps.append(m)
    res = run_bass_kernel_spmd(nc, in_maps, core_ids=list(range(n_cores)))
    y_prompt = np.stack([np.asarray(res.results[c]["y"][0], np.float32) for c in range(8)])
    y_sample = np.stack([np.asarray(res.results[c]["y"][1], np.float32) for c in range(4)])
    return (y_prompt, y_sample)
```

```python
import math
from contextlib import ExitStack

import numpy as np
import ml_dtypes
import concourse.bass as bass
import concourse.mybir as mybir
from concourse.bass_utils import run_bass_kernel_spmd

F32 = mybir.dt.float32
BF16 = mybir.dt.bfloat16
AF = mybir.ActivationFunctionType
ALU = mybir.AluOpType
AX = mybir.AxisListType

L_SEQ = 4096
D = 1024
D_HY = 512
D_IN = 4864
D_FF = 3072
NLAYER = 2
ALPHA = (2 * NLAYER) ** 0.25
LN_EPS = 1e-5
TB = 1024
NFFT = 8192


class Res:
    __slots__ = ("w", "r", "psum")

    def __init__(self, psum=False):
        self.w = None
        self.r = {}
        self.psum = psum


class Sched:
    LIM = 40000

    def __init__(self, nc, es):
        self.nc, self.es = nc, es
        self.eng = dict(pe=nc.tensor, act=nc.scalar, dve=nc.vector, pool=nc.gpsimd, sp=nc.sync)
        self.sems = {}
        self.epoch = {k: 0 for k in self.eng}
        self.cnt = {k: 0 for k in self.eng}
        self.waited = {k: {} for k in self.eng}
        self.ND = 40
        self.dcount = 0
        self.dlast = {}
        self.nops = 0

    def _sem(self, key):
        if key not in self.sems:
            self.sems[key] = self.es.enter_context(self.nc.semaphore("s%d" % len(self.sems)))
        return self.sems[key]

    def op(self, e, fn, reads=(), writes=(), dma=False, pe_chain=False):
        deps = {}

        def add(ev):
            if ev is None:
                return
            k, v = ev
            if deps.get(k, 0) < v:
                deps[k] = v

        for r in reads:
            add(r.w)
            if r.psum:
                for k, ev in r.r.items():
                    if k[0] != e:
                        add(ev)
        for w in writes:
            add(w.w)
            for ev in w.r.values():
                add(ev)
        if dma:
            j = self.dcount
            self.dcount += 1
            slot = j % self.ND
            val = 16 * (j // self.ND + 1)
            if j >= self.ND:
                add((("d", slot), val - 16))
            ev = (("d", slot), val)
            self.dlast[("d", slot)] = val
        else:
            if self.cnt[e] >= self.LIM:
                self.epoch[e] += 1
                self.cnt[e] = 0
            self.cnt[e] += 1
            ev = ((e, self.epoch[e]), self.cnt[e])
        E = self.eng[e]
        wd = self.waited[e]
        for k, v in deps.items():
            if pe_chain and k[0] == e:
                continue
            if wd.get(k, 0) >= v:
                continue
            E.wait_ge(self._sem(k), v)
            wd[k] = v
        ins = fn(E)
        ins.then_inc(self._sem(ev[0]), 16 if dma else 1)
        self.nops += 1
        for w in writes:
            w.w = ev
            w.r = {}
        for r in reads:
            if r.w is not ev:
                r.r[ev[0]] = ev
        return ev

    def dma(self, q, out, in_, reads=(), writes=(), **kw):
        return self.op(q, lambda E: E.dma_start(out=out, in_=in_, **kw), reads=reads, writes=writes, dma=True)

    def barrier(self):
        evs = {}
        for e in self.eng:
            if self.cnt[e] > 0:
                evs[(e, self.epoch[e])] = self.cnt[e]
        evs.update(self.dlast)
        for e, E in self.eng.items():
            wd = self.waited[e]
            for k, v in evs.items():
                if wd.get(k, 0) >= v:
                    continue
                E.wait_ge(self._sem(k), v)
                wd[k] = v


class T:
    def __init__(self, t, n=1, psum=False):
        self.t = t
        self.rs = [Res(psum) for _ in range(n)]

    @property
    def res(self):
        return self.rs[0]

    def __getitem__(self, idx):
        return self.t[idx]


class Ctx:
    pass


def build_program(nc, NS=2, NL=2, dbg=None, stop_after=None):
    dbg = dbg or {}
    es = ExitStack()
    S = Sched(nc, es)
    g = Ctx()

    def din(name, shape, dt=F32):
        return nc.dram_tensor(name, list(shape), dt, kind="ExternalInput").ap()

    def dscr(name, shape, dt):
        kind = "ExternalOutput" if name in dbg else "Internal"
        return nc.dram_tensor(name, list(shape), dt, kind=kind).ap()

    x_in = din("x", [NS, L_SEQ, D])
    y_out = nc.dram_tensor("y", [NS, L_SEQ, D], F32, kind="ExternalOutput").ap()
    w_in = din("w_in", [NL, D, D_IN])
    w_gate = din("w_gate", [NL, D, 3 * D])
    w_br_a = din("w_br_a", [NL, 512, D])
    w_br_b = din("w_br_b", [NL, 256, D])
    w_br_c = din("w_br_c", [NL, 512, D])
    w_o = din("w_o", [NL, D, D])
    w_up = din("w_up", [NL, D, 2 * D_FF])
    w_down = din("w_down", [NL, D_FF, D])
    hy_conv_w = din("hy_conv_w", [NL, 128, 12, 3])
    hy_conv_b = din("hy_conv_b", [NL, 128, 12])
    hy_bias = din("hy_bias", [NL, 128, 4])
    hy_w1 = din("hy_w1", [NL, 17, 64])
    hy_w2 = din("hy_w2", [NL, 64, 64])
    hy_w3 = din("hy_w3", [NL, 64, 1024])
    hy_b1 = din("hy_b1", [NL, 64, 1])
    hy_b2 = din("hy_b2", [NL, 64, 1])
    hy_fr = din("hy_fr", [NL, 64, 1])
    hy_b3 = din("hy_b3", [NL, 128, 8])
    rg_conv_w = din("rg_conv_w", [NL, 128, 4, 4])
    rg_conv_b = din("rg_conv_b", [NL, 128, 4])
    rg_gate_w = din("rg_gate_w", [NL, 2, 2, 4, 128, 128])
    rg_gate_b = din("rg_gate_b", [NL, 128, 2, 2, 4])
    rg_lam = din("rg_lam", [NL, 128, 2, 4])
    b_gate = din("b_gate", [NL, 128, 24])
    ffn_conv_w = din("ffn_conv_w", [NL, 128, 24, 3])
    ffn_conv_b = din("ffn_conv_b", [NL, 128, 24])
    ln1_g = din("ln1_g", [NL, 128, D])
    ln1_b = din("ln1_b", [NL, 128, D])
    ln2_g = din("ln2_g", [NL, 128, D])
    ln2_b = din("ln2_b", [NL, 128, D])
    c_identf = din("c_identf", [128, 128])
    c_identb = din("c_identb", [128, 128], BF16)
    c_ropec = din("c_ropec", [128, L_SEQ])
    c_ropes = din("c_ropes", [128, L_SEQ])
    c_zT = din("c_zT", [17, L_SEQ])
    c_tv = din("c_tv", [128, L_SEQ])
    c_ndelta = din("c_ndelta", [128, 4])
    c_f1 = din("c_f1", [128, 2, 128], BF16)
    c_tw = din("c_tw", [128, 2, 64])
    c_l2x = din("c_l2x", [128, 3, 128], BF16)
    c_i2x = din("c_i2x", [128, 2, 128], BF16)
    c_negm = din("c_negm", [128, 2, 2, 128], BF16)
    c_i1 = din("c_i1", [128, 2, 64], BF16)
    c_mask = din("c_mask", [128, 2, 128], BF16)

    WIN_b = dscr("WIN_b", [NL, 38, 128, 8, 128], BF16)
    WV_b = dscr("WV_b", [NL, 128, 8, 768], BF16)
    WG_b = dscr("WG_b", [NL, 24, 128, 8, 128], BF16)
    WBA_b = dscr("WBA_b", [NL, 8, 128, 4, 128], BF16)
    WBB_b = dscr("WBB_b", [NL, 8, 128, 2, 128], BF16)
    WBC_b = dscr("WBC_b", [NL, 8, 128, 4, 128], BF16)
    WO_b = dscr("WO_b", [NL, 128, 8, D], BF16)
    WUP_b = dscr("WUP_b", [NL, 48, 128, 8, 128], BF16)
    WDN_b = dscr("WDN_b", [NL, 128, 24, D], BF16)
    XA = dscr("XA", [NS, 1536, L_SEQ], F32)
    QK = dscr("QK", [NS, 2, 3, 2, 128, L_SEQ], BF16)
    VT = dscr("VT", [NS, L_SEQ, 768], BF16)
    XC = dscr("XC", [NS, 1024, L_SEQ], F32)
    X0 = dscr("X0", [NS, 512, L_SEQ], F32)
    UT = dscr("UT", [NS, 512, L_SEQ], BF16)
    UTOK = dscr("UTOK", [NS, L_SEQ, 512], BF16)
    KTOK = dscr("KTOK", [NFFT, 512], BF16)
    D1 = dscr("D1", [2, 64, 128, 512], BF16)
    KH = dscr("KH", [NL, 128, 128, 2, 512], BF16)
    D2 = dscr("D2", [2, 128, 64, 512], BF16)
    YA = dscr("YA", [NS, 512, L_SEQ], BF16)
    AO = dscr("AO", [NS, 3, L_SEQ, 260], F32)
    YB = dscr("YB", [NS, 256, L_SEQ], BF16)
    YC = dscr("YC", [NS, 512, L_SEQ], BF16)
    X1 = dscr("X1", [NS, L_SEQ, D], F32)
    GT = dscr("GT", [NS, 3 * D, L_SEQ], BF16)
    X2 = dscr("X2", [NS, L_SEQ, D], F32)
    g.HFB = dscr("HFB", [2, 512, L_SEQ], F32)
    g.ASUM = dscr("ASUM", [128, 16], F32)

    uid = [0]

    def sb(ph, name, shape, dt, n=1):
        uid[0] += 1
        return T(ph.enter_context(nc.sbuf_tensor("%s_%d" % (name, uid[0]), list(shape), dt)), n)

    def ps(ph, name, shape, dt=F32, n=1):
        uid[0] += 1
        esz = 4 if dt == F32 else 2
        per = int(np.prod(shape[1:]))
        nb = (per * esz + 2047) // 2048
        t = ph.enter_context(nc.psum_tensor("%s_%d" % (name, uid[0]), [128, nb * 2048 // esz], dt))
        ap = t[:shape[0], :per]
        if len(shape) == 3:
            ap = ap.rearrange("p (a b) -> p a b", a=shape[1])
        elif len(shape) == 4:
            ap = ap.rearrange("p (a b c) -> p a b c", a=shape[1], b=shape[2])
        return T(ap, n, psum=True)

    xT = sb(es, "xT", [128, 8, L_SEQ], BF16, n=32)
    identf = sb(es, "identf", [128, 128], F32)
    identb = sb(es, "identb", [128, 128], BF16)
    S.dma("sp", identf[:], c_identf[:, :], writes=[identf.res])
    S.dma("sp", identb[:], c_identb[:, :], writes=[identb.res])

    rr = {"i": 0}

    def evac_eng(choices=("act", "dve")):
        rr["i"] += 1
        return choices[rr["i"] % len(choices)]

    def copy_op(e, out, in_, reads, writes):
        if e == "act":
            return S.op("act", lambda E: E.activation(out=out, in_=in_, func=AF.Copy), reads=reads, writes=writes)
        return S.op(e, lambda E: E.tensor_copy(out=out, in_=in_), reads=reads, writes=writes)

    def mm(out, lhsT, rhs, start, stop, reads, pres):
        S.op("pe", lambda E: E.matmul(out, lhsT=lhsT, rhs=rhs, start=start, stop=stop),
             reads=reads, writes=[pres], pe_chain=True)

    def xt_res(t0, n):
        return xT.rs[t0 // 128:(t0 + n + 127) // 128]

    def cast_weights(l):
        with ExitStack() as ph:
            st = [sb(ph, "cst%d" % i, [128, 8, 512], F32) for i in range(2)]
            sbb = [sb(ph, "csb%d" % i, [128, 4, 8, 128], BF16) for i in range(2)]
            k = [0]

            def stat(W, dst, K, Dw):
                KC = K // 128
                Wv = W.rearrange("(kc k) d -> k kc d", k=128)
                for d0 in range(0, Dw, 512):
                    wd = min(512, Dw - d0)
                    nm = wd // 128
                    a, b = st[k[0] % 2], sbb[k[0] % 2]
                    k[0] += 1
                    S.dma("sp", a[:, :KC, :wd], Wv[:, :, d0:d0 + wd], writes=[a.res])
                    copy_op(evac_eng(("act", "dve", "pool")),
                            b[:, :nm, :KC, :], a[:, :KC, :wd].rearrange("p kc (m j) -> p m kc j", j=128),
                            [a.res], [b.res])
                    S.dma("pool", dst[d0 // 128:d0 // 128 + nm].rearrange("m k kc j -> k m kc j"),
                          b[:, :nm, :KC, :], reads=[b.res])

            def mov(W, dst, K, Dw):
                KC = K // 128
                Wv = W.rearrange("(kc k) d -> k kc d", k=128)
                per = max(1, 4096 // Dw)
                for c0 in range(0, KC, per):
                    n = min(per, KC - c0)
                    a, b = st[k[0] % 2], sbb[k[0] % 2]
                    k[0] += 1
                    av = a[:].rearrange("p a b -> p (a b)")[:, :n * Dw].rearrange("p (a b) -> p a b", b=Dw)
                    bv = b[:].rearrange("p a b c -> p (a b c)")[:, :n * Dw].rearrange("p (a b) -> p a b", b=Dw)
                    S.dma("sp", av, Wv[:, c0:c0 + n, :], writes=[a.res])
                    copy_op(evac_eng(("act", "dve", "pool")), bv, av, [a.res], [b.res])
                    S.dma("pool", dst[:, c0:c0 + n, :], bv, reads=[b.res])

            stat(w_in[l], WIN_b[l], D, D_IN)
            mov(w_in[l][:, 3072:3840], WV_b[l], D, 768)
            stat(w_gate[l], WG_b[l], D, 3 * D)
            stat(w_br_a[l], WBA_b[l], 512, D)
            stat(w_br_b[l], WBB_b[l], 256, D)
            stat(w_br_c[l], WBC_b[l], 512, D)
            mov(w_o[l], WO_b[l], D, D)
            stat(w_up[l], WUP_b[l], D, 2 * D_FF)
            mov(w_down[l], WDN_b[l], D_FF, D)
            S.barrier()

    def p1a_load_x(s, src=None):
        src = x_in[s] if src is None else src
        with ExitStack() as ph:
            xs = [sb(ph, "xs%d" % i, [128, D], F32) for i in range(3)]
            tp = [ps(ph, "tp%d" % i, [128, 8, 128], F32) for i in range(2)]
            for tt in range(32):
                a, p = xs[tt % 3], tp[tt % 2]
                S.dma("sp", a[:], src[tt * 128:(tt + 1) * 128, :], writes=[a.res])
                for kc in range(8):
                    S.op("pe", lambda E, kc=kc: E.transpose(p[:, kc, :], a[:, kc * 128:(kc + 1) * 128], identf[:]),
                         reads=[a.res, identf.res], writes=[p.res], pe_chain=True)
                copy_op(evac_eng(), xT[:, :, tt * 128:(tt + 1) * 128], p[:, :, :], [p.res], [xT.rs[tt]])
            S.barrier()

    def p1b_inproj(l, s):
        with ExitStack() as ph:
            ropec = sb(ph, "ropec", [128, L_SEQ], F32)
            ropes = sb(ph, "ropes", [128, L_SEQ], F32)
            S.dma("sp", ropec[:], c_ropec[:, :], writes=[ropec.res])
            S.dma("sp", ropes[:], c_ropes[:, :], writes=[ropes.res])
            wv = sb(ph, "wv", [128, 8, 768], BF16)
            S.dma("sp", wv[:], WV_b[l], writes=[wv.res])
            NW = 4
            wt = [sb(ph, "wt%d" % i, [128, 8, 128], BF16) for i in range(NW)]
            pp = [ps(ph, "pp%d" % i, [128, TB], F32) for i in range(4)]
            stg = [sb(ph, "stg%d" % i, [128, TB], F32) for i in range(2)]
            tmp = [sb(ph, "rtmp%d" % i, [128, TB], F32) for i in range(4)]
            qks = [sb(ph, "qks%d" % i, [128, 2, TB], BF16) for i in range(2)]
            vst = [sb(ph, "vst%d" % i, [128, 768], BF16) for i in range(2)]
            ms = [m for m in range(38) if not (24 <= m < 30)]
            cnt = {"w": 0, "p": 0, "s": 0, "q": 0, "v": 0}
            for tb in range(L_SEQ // TB):
                t0 = tb * TB
                xr = xt_res(t0, TB)
                loaded = {}

                def load(i):
                    w = wt[cnt["w"] % NW]
                    cnt["w"] += 1
                    S.dma("sp", w[:], WIN_b[l, ms[i]], writes=[w.res])
                    loaded[i] = w

                def compute(i):
                    m = ms[i]
                    w = loaded.pop(i)
                    p = pp[cnt["p"] % 4]
                    cnt["p"] += 1
                    for nt in range(TB // 512):
                        for kc in range(8):
                            mm(p[:, nt * 512:(nt + 1) * 512], w[:, kc, :], xT[:, kc, t0 + nt * 512:t0 + (nt + 1) * 512],
                               kc == 0, kc == 7, [w.res] + xr, p.res)
                    return m, p

                pend = {}
                D_PF = 2
                for i in range(len(ms) + D_PF):
                    if i < len(ms):
                        load(i)
                    j = i - D_PF
                    if j < 0:
                        continue
                    m, p = compute(j)
                    if m < 12 or m >= 30:
                        a = stg[cnt["s"] % 2]
                        cnt["s"] += 1
                        copy_op(evac_eng(), a[:], p[:], [p.res], [a.res])
                        if m < 12:
                            dst = XA[s, m * 128:(m + 1) * 128, t0:t0 + TB]
                        else:
                            dst = XC[s, (m - 30) * 128:(m - 29) * 128, t0:t0 + TB]
                        S.dma("pool", dst, a[:], reads=[a.res])
                    else:
                        jj = m - 12
                        if jj % 2 == 0:
                            pend["A"] = p
                        else:
                            pa, pb = pend.pop("A"), p
                            kind, gidx = (jj // 2) // 3, (jj // 2) % 3
                            o = qks[cnt["q"] % 2]
                            cnt["q"] += 1
                            c_, s_ = ropec[:, t0:t0 + TB], ropes[:, t0:t0 + TB]
                            t1, t2, t3, t4 = tmp
                            S.op("dve", lambda E: E.tensor_tensor(out=t1[:], in0=pa[:], in1=c_, op=ALU.mult), [pa.res, ropec.res], [t1.res])
                            S.op("dve", lambda E: E.tensor_tensor(out=t2[:], in0=pb[:], in1=s_, op=ALU.mult), [pb.res, ropes.res], [t2.res])
                            S.op("dve", lambda E: E.tensor_tensor(out=t3[:], in0=pb[:], in1=c_, op=ALU.mult), [pb.res, ropec.res], [t3.res])
                            S.op("dve", lambda E: E.tensor_tensor(out=t4[:], in0=pa[:], in1=s_, op=ALU.mult), [pa.res, ropes.res], [t4.res])
                            S.op("pool", lambda E: E.tensor_tensor(out=o[:, 0, :], in0=t1[:], in1=t2[:], op=ALU.subtract), [t1.res, t2.res], [o.res])
                            S.op("pool", lambda E: E.tensor_tensor(out=o[:, 1, :], in0=t3[:], in1=t4[:], op=ALU.add), [t3.res, t4.res], [o.res])
                            S.dma("pool", QK[s, kind, gidx].rearrange("h p t -> p h t")[:, :, t0:t0 + TB], o[:, :, :], reads=[o.res])
                for tt in range(TB // 128):
                    tk = t0 + tt * 128
                    p = pp[cnt["p"] % 4]
                    cnt["p"] += 1
                    for (c0, c1) in ((0, 512), (512, 768)):
                        for kc in range(8):
                            mm(p[:, c0:c1], xT[:, kc, tk:tk + 128], wv[:, kc, c0:c1], kc == 0, kc == 7,
                               [wv.res] + xt_res(tk, 128), p.res)
                    a = vst[cnt["v"] % 2]
                    cnt["v"] += 1
                    copy_op(evac_eng(), a[:], p[:, 0:768], [p.res], [a.res])
                    S.dma("pool", VT[s, tk:tk + 128, :], a[:], reads=[a.res])
            S.barrier()


    def to_tokmajor(ph, tiles, dst, tag):
        tp = [ps(ph, "tk_tp%s%d" % (tag, i), [128, 4, 128], BF16) for i in range(2)]
        st = [sb(ph, "tk_st%s%d" % (tag, i), [128, 512], BF16) for i in range(3)]
        for tt in range(32):
            p, a = tp[tt % 2], st[tt % 3]
            for cc in range(4):
                S.op("pe", lambda E, cc=cc: E.transpose(p[:, cc, :], tiles[cc][:, tt * 128:(tt + 1) * 128], identb[:]),
                     reads=[tiles[cc].res, identb.res], writes=[p.res], pe_chain=True)
            copy_op(evac_eng(), a[:].rearrange("p (a b) -> p a b", a=4), p[:, :, :], [p.res], [a.res])
            S.dma("sp", dst[tt * 128:(tt + 1) * 128, :], a[:], reads=[a.res])

    def pf_filter(l):
        HFB = g.HFB
        with ExitStack() as ph:
            zT = sb(ph, "zT", [17, L_SEQ], F32)
            tv = sb(ph, "tv", [128, L_SEQ], F32)
            w1 = sb(ph, "fw1", [17, 64], F32)
            w2 = sb(ph, "fw2", [64, 64], F32)
            w3 = sb(ph, "fw3", [64, 1024], F32)
            b1 = sb(ph, "fb1", [64, 1], F32)
            b2 = sb(ph, "fb2", [64, 1], F32)
            fr = sb(ph, "ffr", [64, 1], F32)
            frb1 = sb(ph, "frb1", [64, 1], F32)
            frb2 = sb(ph, "frb2", [64, 1], F32)
            b3 = sb(ph, "fb3", [128, 8], F32)
            nd = sb(ph, "fnd", [128, 4], F32)
            halfpi = sb(ph, "halfpi", [128, 1], F32)
            asum = sb(ph, "asum", [128, 16], F32)
            for (t_, src) in ((zT, c_zT), (tv, c_tv), (w1, hy_w1[l]), (w2, hy_w2[l]), (w3, hy_w3[l]), (b1, hy_b1[l]),
                              (b2, hy_b2[l]), (fr, hy_fr[l]), (b3, hy_b3[l]), (nd, c_ndelta)):
                S.dma("sp", t_[:], src, writes=[t_.res])
            S.op("dve", lambda E: E.memset(halfpi[:], math.pi / 2), [], [halfpi.res])
            S.op("dve", lambda E: E.tensor_tensor(out=frb1[:], in0=fr[:], in1=b1[:], op=ALU.mult), [fr.res, b1.res], [frb1.res])
            S.op("dve", lambda E: E.tensor_tensor(out=frb2[:], in0=fr[:], in1=b2[:], op=ALU.mult), [fr.res, b2.res], [frb2.res])
            HS = 2048
            h1 = sb(ph, "fh1", [64, HS], F32)
            h2 = sb(ph, "fh2", [64, HS], F32)
            dec = sb(ph, "fdec", [128, HS], F32)
            hk = [sb(ph, "fhk%d" % i, [128, HS], F32) for i in range(2)]
            ta = sb(ph, "fta", [64, 512], F32)
            tab = sb(ph, "ftab", [64, 512], F32)
            ts1 = sb(ph, "fts1", [64, 512], F32)
            ts2 = sb(ph, "fts2", [64, 512], F32)
            pq = [ps(ph, "fpq%d" % i, [128, 512], F32) for i in range(4)]
            pc = [0]

            def sin_layer(dst, wmat, K, rhs_t, rhs_res, hs0, frb):
                for nt in range(HS // 512):
                    p = pq[pc[0] % 4]
                    pc[0] += 1
                    c0 = nt * 512
                    off = hs0 + c0 if rhs_t is zT else c0
                    rhs = rhs_t[:K, off:off + 512]
                    mm(p[:64, :], wmat[:K, :], rhs, True, True, [wmat.res, rhs_res], p.res)
                    S.op("act", lambda E: E.activation(out=ta[:], in_=p[:64, :], func=AF.Identity, scale=fr[:], bias=frb[:]),
                         [p.res, fr.res, frb.res], [ta.res])
                    S.op("dve", lambda E: E.scalar_tensor_tensor(out=tab[:], in0=ta[:], scalar=-1.0, in1=ta[:], op0=ALU.mult, op1=ALU.max), [ta.res], [tab.res])
                    S.op("act", lambda E: E.activation(out=ts1[:], in_=ta[:], func=AF.Sin, scale=0.5), [ta.res], [ts1.res])
                    S.op("act", lambda E: E.activation(out=ts2[:], in_=tab[:], func=AF.Sin, scale=-0.5, bias=halfpi[:64, :]),
                         [tab.res, halfpi.res], [ts2.res])
                    S.op("dve", lambda E: E.scalar_tensor_tensor(out=dst[:, c0:c0 + 512], in0=ts1[:], scalar=2.0, in1=ts2[:],
                                                                 op0=ALU.mult, op1=ALU.mult), [ts1.res, ts2.res], [dst.res])

            S.op("dve", lambda E: E.memset(asum[:], 0.0), [], [asum.res])
            ki = 0
            for hs in range(2):
                hs0 = hs * HS
                sin_layer(h1, w1, 17, zT, zT.res, hs0, frb1)
                sin_layer(h2, w2, 64, h1, h1.res, hs0, frb2)
                for cc in range(4):
                    S.op("act", lambda E: E.activation(out=dec[:], in_=tv[:, hs0:hs0 + HS], func=AF.Exp, scale=nd[:, cc:cc + 1]),
                         [tv.res, nd.res], [dec.res])
                    for dirn in range(2):
                        mc = dirn * 4 + cc
                        o = hk[ki % 2]
                        ki += 1
                        for nt in range(HS // 512):
                            p = pq[pc[0] % 4]
                            pc[0] += 1
                            c0 = nt * 512
                            mm(p[:, :], w3[:, mc * 128:(mc + 1) * 128], h2[:, c0:c0 + 512], True, True, [w3.res, h2.res], p.res)
                            S.op("dve", lambda E: E.scalar_tensor_tensor(out=o[:, c0:c0 + 512], in0=p[:, :], scalar=b3[:, mc:mc + 1],
                                                                         in1=dec[:, c0:c0 + 512], op0=ALU.add, op1=ALU.mult),
                                 [p.res, b3.res, dec.res], [o.res])
                        if dirn == 1 and hs == 0:
                            S.op("dve", lambda E: E.memset(o[:, 0:1], 0.0), [], [o.res])
                        col = mc * 2 + hs
                        S.op("dve", lambda E: E.tensor_reduce(out=asum[:, col:col + 1], in_=o[:], axis=AX.X, op=ALU.add,
                                                              apply_absolute_value=True), [o.res], [asum.res])
                        S.dma("pool", HFB[dirn, cc * 128:(cc + 1) * 128, hs0:hs0 + HS], o[:], reads=[o.res])
            S.dma("pool", g.ASUM[:, :], asum[:], reads=[asum.res])
            S.barrier()
        with ExitStack() as ph:
            asum2 = sb(ph, "asum2", [128, 16], F32)
            nrm = sb(ph, "fnrm", [128, 4], F32)
            rinv = sb(ph, "frinv", [128, 4], F32)
            S.dma("sp", asum2[:], g.ASUM[:, :], writes=[asum2.res])
            for cc in range(4):
                S.op("dve", lambda E, cc=cc: E.tensor_tensor(out=nrm[:, cc:cc + 1], in0=asum2[:, 2 * cc:2 * cc + 1],
                                                             in1=asum2[:, 2 * cc + 1:2 * cc + 2], op=ALU.add), [asum2.res], [nrm.res])
                for extra in (2 * (4 + cc), 2 * (4 + cc) + 1):
                    S.op("dve", lambda E, cc=cc, extra=extra: E.tensor_tensor(out=nrm[:, cc:cc + 1], in0=nrm[:, cc:cc + 1],
                                                                              in1=asum2[:, extra:extra + 1], op=ALU.add),
                         [asum2.res, nrm.res], [nrm.res])
            S.op("dve", lambda E: E.reciprocal(out=rinv[:], in_=nrm[:]), [nrm.res], [rinv.res])
            hb = [sb(ph, "fhb%d" % i, [128, L_SEQ], F32) for i in range(2)]
            kb = [sb(ph, "fkb%d" % i, [128, L_SEQ], BF16) for i in range(4)]
            for dirn in range(2):
                for cc in range(4):
                    a = hb[cc % 2]
                    S.dma("sp", a[:], HFB[dirn, cc * 128:(cc + 1) * 128, :], writes=[a.res])
                    o = kb[cc]
                    if dirn == 0:
                        S.op("dve", lambda E: E.tensor_scalar(out=o[:], in0=a[:], scalar1=rinv[:, cc:cc + 1], scalar2=None, op0=ALU.mult),
                             [a.res, rinv.res], [o.res])
                    else:
                        S.op("dve", lambda E: E.memset(o[:, 0:1], 0.0), [], [o.res])
                        S.op("dve", lambda E: E.tensor_scalar(out=o[:, 1:L_SEQ], in0=a[:, L_SEQ - 1:0:-1], scalar1=rinv[:, cc:cc + 1],
                                                              scalar2=None, op0=ALU.mult), [a.res, rinv.res], [o.res])
                to_tokmajor(ph, kb, KTOK[dirn * L_SEQ:(dirn + 1) * L_SEQ, :], "f%d" % dirn)
            S.barrier()

    def fft_stage1(src, NR):
        with ExitStack() as ph:
            f1m = sb(ph, "f1m", [128, 2, 128], BF16)
            tw = sb(ph, "tw", [128, 2, 64], F32)
            S.dma("sp", f1m[:], c_f1[:, :, :], writes=[f1m.res])
            S.dma("sp", tw[:], c_tw[:, :, :], writes=[tw.res])
            NC2 = 8
            uh = [sb(ph, "uh%d" % i, [128, NC2, 512], BF16) for i in range(2)]
            pq = [ps(ph, "s1p%d" % i, [128, 512], F32) for i in range(6)]
            t1 = [sb(ph, "s1t1%d" % i, [128, 512], F32) for i in range(2)]
            t2 = [sb(ph, "s1t2%d" % i, [128, 512], F32) for i in range(2)]
            oo = [sb(ph, "s1o%d" % i, [128, 2, 512], BF16) for i in range(3)]
            srcv = src.rearrange("(a b) c -> a b c", b=64)
            pc = 0
            for ch in range(64 // NC2):
                u = uh[ch % 2]
                S.dma("sp", u[:NR], srcv[:, ch * NC2:(ch + 1) * NC2, :], writes=[u.res])
                for j in range(NC2):
                    n2 = ch * NC2 + j
                    pr, pi = pq[pc % 6], pq[(pc + 1) % 6]
                    pc += 2
                    mm(pr[:, :], f1m[:NR, 0, :], u[:NR, j, :], True, True, [f1m.res, u.res], pr.res)
                    mm(pi[:, :], f1m[:NR, 1, :], u[:NR, j, :], True, True, [f1m.res, u.res], pi.res)
                    a1, a2, o = t1[n2 % 2], t2[n2 % 2], oo[n2 % 3]
                    Tr, Ti = tw[:, 0, n2:n2 + 1], tw[:, 1, n2:n2 + 1]
                    S.op("act", lambda E: E.activation(out=a1[:], in_=pi[:, :], func=AF.Identity, scale=Ti), [pi.res, tw.res], [a1.res])
                    S.op("act", lambda E: E.activation(out=a2[:], in_=pr[:, :], func=AF.Identity, scale=Ti), [pr.res, tw.res], [a2.res])
                    S.op("dve", lambda E: E.scalar_tensor_tensor(out=o[:, 0, :], in0=pr[:, :], scalar=Tr, in1=a1[:], op0=ALU.mult, op1=ALU.add),
                         [pr.res, tw.res, a1.res], [o.res])
                    S.op("dve", lambda E: E.scalar_tensor_tensor(out=o[:, 1, :], in0=pi[:, :], scalar=Tr, in1=a2[:], op0=ALU.mult, op1=ALU.subtract),
                         [pi.res, tw.res, a2.res], [o.res])
                    S.dma("pool", D1[:, n2, :, :].rearrange("r f c -> f r c"), o[:, :, :], reads=[o.res])
            S.barrier()

    def fft_stage2(mode, l):
        with ExitStack() as ph:
            l2 = sb(ph, "l2m", [128, 3, 128], BF16)
            S.dma("sp", l2[:], c_l2x[:, :, :], writes=[l2.res])
            FC = 4
            ain = [sb(ph, "s2a%d" % i, [128, FC, 512], BF16) for i in range(3)]
            d1v = D1.rearrange("r n f c -> (r n) f c")
            if mode == "filter":
                pU = [ps(ph, "s2pu%d" % i, [128, 2, 512], F32) for i in range(3)]
                st = [sb(ph, "s2st%d" % i, [128, 2, 512], BF16) for i in range(3)]
            else:
                pU = [ps(ph, "s2pu%d" % i, [128, 512], F32) for i in range(4)]
                pB = [ps(ph, "s2pb%d" % i, [128, 512], F32) for i in range(4)]
                i2 = sb(ph, "i2m", [128, 2, 128], BF16)
                S.dma("sp", i2[:], c_i2x[:, :, :], writes=[i2.res])
                kh = [sb(ph, "s2kh%d" % i, [128, FC, 2, 512], BF16) for i in range(3)]
                p1 = [sb(ph, "s2p1%d" % i, [128, 512], BF16) for i in range(4)]
                p2 = [sb(ph, "s2p2%d" % i, [128, 512], BF16) for i in range(4)]
                oo = [sb(ph, "s2o%d" % i, [128, FC, 512], BF16) for i in range(3)]
            pendB = []

            def flushB(f1, j, q1, q2, o, ch):
                pb = pB[f1 % 4]
                mm(pb[:, :], i2[:, 0, :], q1[:], True, False, [i2.res, q1.res], pb.res)
                mm(pb[:, :], i2[:, 1, :], q2[:], False, True, [i2.res, q2.res], pb.res)
                copy_op("act", o[:, j, :], pb[:, :], [pb.res], [o.res])
                if j == FC - 1:
                    for r_ in range(2):
                        S.dma("pool", D2[r_, ch * FC:(ch + 1) * FC, :, :].rearrange("f t c -> t f c"), o[64 * r_:64 * r_ + 64, :, :], reads=[o.res])

            for ch in range(128 // FC):
                a = ain[ch % 3]
                S.dma("sp", a[:], d1v[:, ch * FC:(ch + 1) * FC, :], writes=[a.res])
                if mode != "filter":
                    k = kh[ch % 3]
                    S.dma("sp", k[:].rearrange("p f a c -> p f (a c)"),
                          KH[l, ch * FC:(ch + 1) * FC].rearrange("f p a c -> p f (a c)"), writes=[k.res])
                    o = oo[ch % 3]
                for j in range(FC):
                    f1 = ch * FC + j
                    if mode == "filter":
                        pu = pU[f1 % 3]
                        mm(pu[:, 0, :], l2[:, 1, :], a[:, j, :], True, True, [l2.res, a.res], pu.res)
                        mm(pu[:, 1, :], l2[:, 2, :], a[:, j, :], True, True, [l2.res, a.res], pu.res)
                        t_ = st[f1 % 3]
                        copy_op(evac_eng(), t_[:], pu[:, :, :], [pu.res], [t_.res])
                        S.dma("pool", KH[l, f1].rearrange("p a c -> p (a c)"), t_[:].rearrange("p a c -> p (a c)"), reads=[t_.res])
                        continue
                    pu = pU[f1 % 4]
                    q1, q2 = p1[f1 % 4], p2[f1 % 4]
                    mm(pu[:, :], l2[:, 0, :], a[:, j, :], True, True, [l2.res, a.res], pu.res)
                    S.op("dve", lambda E: E.tensor_tensor(out=q1[:], in0=pu[:, :], in1=k[:, j, 0, :], op=ALU.mult), [pu.res, k.res], [q1.res])
                    S.op("dve", lambda E: E.tensor_tensor(out=q2[:], in0=pu[:, :], in1=k[:, j, 1, :], op=ALU.mult), [pu.res, k.res], [q2.res])
                    if pendB:
                        flushB(*pendB.pop())
                    pendB.append((f1, j, q1, q2, o, ch))
            if mode != "filter" and pendB:
                flushB(*pendB.pop())
            S.barrier()

    def conv3_rows(raw, out, w_t, b_ap, wcol, n):
        S.op("pool", lambda E: E.tensor_scalar(out=out[:, :n], in0=raw[:, 1:n + 1], scalar1=w_t[:, wcol, 1:2], scalar2=b_ap,
                                               op0=ALU.mult, op1=ALU.add), [raw.res, w_t.res], [out.res])
        S.op("dve", lambda E: E.scalar_tensor_tensor(out=out[:, :n], in0=raw[:, 0:n], scalar=w_t[:, wcol, 0:1], in1=out[:, :n],
                                                     op0=ALU.mult, op1=ALU.add), [raw.res, w_t.res, out.res], [out.res])
        S.op("dve", lambda E: E.scalar_tensor_tensor(out=out[:, :n], in0=raw[:, 2:n + 2], scalar=w_t[:, wcol, 2:3], in1=out[:, :n],
                                                     op0=ALU.mult, op1=ALU.add), [raw.res, w_t.res, out.res], [out.res])

    def load_halo(raw, src_rows, t0, n):
        lo, hi = max(t0 - 1, 0), min(t0 + n + 1, L_SEQ)
        wr = [raw.res]
        if t0 == 0:
            S.op("pool", lambda E: E.memset(raw[:, 0:1], 0.0), [], wr)
        if t0 + n >= L_SEQ:
            S.op("pool", lambda E: E.memset(raw[:, n + 1:n + 2], 0.0), [], wr)
        S.dma("sp", raw[:, lo - (t0 - 1):hi - (t0 - 1)], src_rows[:, lo:hi], writes=wr)

    def p2a_hyconv(l, s):
        with ExitStack() as ph:
            cw = sb(ph, "hcw", [128, 12, 3], F32)
            cb = sb(ph, "hcb", [128, 12], F32)
            S.dma("sp", cw[:], hy_conv_w[l], writes=[cw.res])
            S.dma("sp", cb[:], hy_conv_b[l], writes=[cb.res])
            HS = 2048
            raw = [sb(ph, "hraw%d" % i, [128, HS + 2], F32) for i in range(3)]
            cv = [sb(ph, "hcv%d" % i, [128, HS], F32) for i in range(3)]
            ut = [sb(ph, "hut%d" % i, [128, L_SEQ], BF16) for i in range(4)]
            ri = 0
            bgt = sb(ph, "hbg", [128, 24], F32)
            S.dma("sp", bgt[:], b_gate[l], writes=[bgt.res])
            gwt = [sb(ph, "hgw%d" % i, [128, 8, 128], BF16) for i in range(4)]
            gpp = [ps(ph, "hgp%d" % i, [128, TB], F32) for i in range(3)]
            gst = [sb(ph, "hgs%d" % i, [128, TB], BF16) for i in range(2)]
            gitems = [(tb, m) for tb in range(L_SEQ // TB) for m in range(24)]
            gstate = {"next": 0, "loaded": {}, "lw": 0}

            def gate_load(i):
                if i >= len(gitems) or i in gstate["loaded"]:
                    return
                w = gwt[gstate["lw"] % 4]
                gstate["lw"] += 1
                S.dma("sp", w[:], WG_b[l, gitems[i][1]], writes=[w.res])
                gstate["loaded"][i] = w

            def gate_items(n):
                for _ in range(n):
                    i = gstate["next"]
                    if i >= len(gitems):
                        return
                    gstate["next"] += 1
                    gate_load(i)
                    gate_load(i + 1)
                    gate_load(i + 2)
                    tb, m = gitems[i]
                    w = gstate["loaded"].pop(i)
                    t0g = tb * TB
                    p = gpp[i % 3]
                    for nt in range(TB // 512):
                        for kc in range(8):
                            mm(p[:, nt * 512:(nt + 1) * 512], w[:, kc, :], xT[:, kc, t0g + nt * 512:t0g + (nt + 1) * 512],
                               kc == 0, kc == 7, [w.res] + xt_res(t0g, TB), p.res)
                    o = gst[i % 2]
                    S.op("act", lambda E: E.activation(out=o[:], in_=p[:, :], func=AF.Sigmoid, bias=bgt[:, m:m + 1]), [p.res, bgt.res], [o.res])
                    S.dma("act", GT[s, m * 128:(m + 1) * 128, t0g:t0g + TB], o[:], reads=[o.res])

            for cc in range(4):
                for hs in range(2):
                    t0 = hs * HS
                    outs = []
                    for part in range(3):
                        ch = part * 4 + cc
                        r_, c_ = raw[ri % 3], cv[ri % 3]
                        ri += 1
                        load_halo(r_, XA[s, ch * 128:(ch + 1) * 128, :], t0, HS)
                        conv3_rows(r_, c_, cw, cb[:, ch:ch + 1], ch, HS)
                        outs.append(c_)
                        gate_items(4)
                    S.dma("pool", X0[s, cc * 128:(cc + 1) * 128, t0:t0 + HS], outs[0][:], reads=[outs[0].res])
                    S.op("pool", lambda E: E.tensor_tensor(out=ut[cc][:, t0:t0 + HS], in0=outs[1][:], in1=outs[2][:], op=ALU.mult),
                         [outs[1].res, outs[2].res], [ut[cc].res])
                S.dma("pool", UT[s, cc * 128:(cc + 1) * 128, :], ut[cc][:], reads=[ut[cc].res])
            gate_items(len(gitems))
            to_tokmajor(ph, ut, UTOK[s], "u")
            S.barrier()

    def p2d_hyout(l, s):
        with ExitStack() as ph:
            i1 = sb(ph, "i1m", [128, 2, 64], BF16)
            hb_ = sb(ph, "hbias", [128, 4], F32)
            S.dma("sp", i1[:], c_i1[:, :, :], writes=[i1.res])
            S.dma("sp", hb_[:], hy_bias[l], writes=[hb_.res])
            zT_ = sb(ph, "zTt", [128, 4, L_SEQ], BF16)
            TC = 4
            bin_ = [sb(ph, "bin%d" % i, [128, 2, TC, 512], BF16) for i in range(2)]
            d2v = D2.rearrange("r f t c -> f r t c")
            py = [ps(ph, "p2dy%d" % i, [64, 512], F32) for i in range(3)]
            pz = [ps(ph, "p2dz%d" % i, [128, 4, 64], BF16) for i in range(3)]
            ysb = [sb(ph, "ysb%d" % i, [64, 512], BF16) for i in range(3)]
            tw = sb(ph, "twd", [128, 2, 64], F32)
            S.dma("sp", tw[:], c_tw[:, :, :], writes=[tw.res])
            tw1 = [sb(ph, "tw1%d" % i, [128, 512], F32) for i in range(2)]
            tw2 = [sb(ph, "tw2%d" % i, [128, 512], F32) for i in range(2)]
            btw = [sb(ph, "btw%d" % i, [128, 2, 512], BF16) for i in range(4)]
            pendz = []
            pendm = []

            def flush_m(bt, t2):
                p, y, z = py[t2 % 3], ysb[t2 % 3], pz[t2 % 3]
                mm(p[:, :], i1[:, 0, :], bt[:, 0, :], True, False, [i1.res, bt.res], p.res)
                mm(p[:, :], i1[:, 1, :], bt[:, 1, :], False, True, [i1.res, bt.res], p.res)
                copy_op("act", y[:], p[:, :], [p.res], [y.res])
                if pendz:
                    flush_z(*pendz.pop())
                pendz.append((y, z, t2))

            def flush_z(y, z, t2):
                for cc in range(4):
                    S.op("pe", lambda E, cc=cc: E.transpose(z[:, cc, :], y[:, cc * 128:(cc + 1) * 128], identb[:64, :64]),
                         reads=[y.res, identb.res], writes=[z.res], pe_chain=True)
                copy_op("dve", zT_[:, :, t2:L_SEQ:64], z[:, :, :], [z.res], [zT_.res])

            for ch in range(64 // TC):
                b = bin_[ch % 2]
                S.dma("sp", b[:], d2v[:, :, ch * TC:(ch + 1) * TC, :], writes=[b.res])
                for j in range(TC):
                    t2 = ch * TC + j
                    a1, a2, bt = tw1[t2 % 2], tw2[t2 % 2], btw[t2 % 4]
                    Tr, Ti = tw[:, 0, t2:t2 + 1], tw[:, 1, t2:t2 + 1]
                    S.op("act", lambda E: E.activation(out=a1[:], in_=b[:, 1, j, :], func=AF.Identity, scale=Ti), [b.res, tw.res], [a1.res])
                    S.op("act", lambda E: E.activation(out=a2[:], in_=b[:, 0, j, :], func=AF.Identity, scale=Ti), [b.res, tw.res], [a2.res])
                    S.op("dve", lambda E: E.scalar_tensor_tensor(out=bt[:, 0, :], in0=b[:, 0, j, :], scalar=Tr, in1=a1[:], op0=ALU.mult, op1=ALU.subtract),
                         [b.res, tw.res, a1.res], [bt.res])
                    S.op("dve", lambda E: E.scalar_tensor_tensor(out=bt[:, 1, :], in0=b[:, 1, j, :], scalar=Tr, in1=a2[:], op0=ALU.mult, op1=ALU.add),
                         [b.res, tw.res, a2.res], [bt.res])
                    if pendm:
                        flush_m(*pendm.pop())
                    pendm.append((bt, t2))
            if pendm:
                flush_m(*pendm.pop())
            if pendz:
                flush_z(*pendz.pop())
            HS = 1024
            utl = [sb(ph, "utl%d" % i, [128, HS], BF16) for i in range(2)]
            x0l = [sb(ph, "x0l%d" % i, [128, HS], F32) for i in range(2)]
            tm = [sb(ph, "ytm%d" % i, [128, HS], F32) for i in range(2)]
            yo = [sb(ph, "yo%d" % i, [128, HS], BF16) for i in range(2)]
            k = 0
            for cc in range(4):
                for hs in range(L_SEQ // HS):
                    t0 = hs * HS
                    u_, x_, t_, o_ = utl[k % 2], x0l[k % 2], tm[k % 2], yo[k % 2]
                    k += 1
                    S.dma("sp", u_[:], UT[s, cc * 128:(cc + 1) * 128, t0:t0 + HS], writes=[u_.res])
                    S.dma("sp", x_[:], X0[s, cc * 128:(cc + 1) * 128, t0:t0 + HS], writes=[x_.res])
                    S.op("dve", lambda E: E.scalar_tensor_tensor(out=t_[:], in0=u_[:], scalar=hb_[:, cc:cc + 1], in1=zT_[:, cc, t0:t0 + HS],
                                                                 op0=ALU.mult, op1=ALU.add), [u_.res, hb_.res, zT_.res], [t_.res])
                    S.op("pool", lambda E: E.tensor_tensor(out=o_[:], in0=t_[:], in1=x_[:], op=ALU.mult), [t_.res, x_.res], [o_.res])
                    S.dma("pool", YA[s, cc * 128:(cc + 1) * 128, t0:t0 + HS], o_[:], reads=[o_.res])
            S.barrier()


    def p3_attention(l, s):
        with ExitStack() as ph:
            mask = sb(ph, "amask", [128, 2, 128], BF16)
            S.dma("sp", mask[:], c_mask[:, :, :], writes=[mask.res])
            raw = [sb(ph, "araw%d" % i, [64, L_SEQ], BF16) for i in range(2)]
            Qs = [sb(ph, "aQ%d" % i, [64, L_SEQ], BF16) for i in range(4)]
            Ks = [sb(ph, "aK%d" % i, [64, L_SEQ + 128 * 16], BF16) for i in range(4)]
            NV = 8
            vraw = [sb(ph, "avr%d" % i, [128, 256], BF16) for i in range(NV)]
            vint = [sb(ph, "avx%d" % i, [128, 4, 65], BF16) for i in range(NV)]
            vfirsts = [sb(ph, "avf%d" % i, [128, 4, 65], BF16) for i in range(4)]
            vlasts = [sb(ph, "avl%d" % i, [128, 4, 65], BF16) for i in range(4)]
            for v_ in vint + vfirsts + vlasts:
                S.op("pool", lambda E: E.memset(v_[:], 1.0), [], [v_.res])
            for v_ in vfirsts:
                S.op("pool", lambda E: E.memset(v_[0:64], 0.0), [], [v_.res])
            for v_ in vlasts:
                S.op("pool", lambda E: E.memset(v_[64:128], 0.0), [], [v_.res])
            negm = sb(ph, "anegm", [128, 2, 2, 128], BF16)
            S.dma("sp", negm[:], c_negm[:, :, :, :], writes=[negm.res])
            psc = [ps(ph, "asc%d" % i, [128, 4, 2, 128], F32) for i in range(3)]
            ppo = [ps(ph, "apo%d" % i, [128, 4, 65], F32) for i in range(2)]
            pe_ = [sb(ph, "ape%d" % i, [128, 4, 2, 128], BF16) for i in range(3)]
            aos = [sb(ph, "aos%d" % i, [128, 260], F32) for i in range(3)]
            cn = {"raw": 0, "v": 0, "vi": 0, "vf": 0, "vl": 0, "sc": 0, "po": 0, "ao": 0}
            for gi, d in enumerate((1, 4, 16)):
                n = L_SEQ // d
                W = n + 128
                for h in range(4):
                    for kind, dstt in ((0, Qs[h]), (1, Ks[h])):
                        r_ = raw[cn["raw"] % 2]
                        cn["raw"] += 1
                        S.dma("sp", r_[0:32, :], QK[s, kind, gi, 0, 32 * h:32 * h + 32, :], writes=[r_.res])
                        S.dma("sp", r_[32:64, :], QK[s, kind, gi, 1, 32 * h:32 * h + 32, :], writes=[r_.res])
                        src = r_[:, :].rearrange("p (i r) -> p r i", r=d)
                        if kind == 0:
                            dv = dstt[:, :].rearrange("p (r i) -> p r i", r=d)
                            copy_op(evac_eng(("dve", "act")), dv, src, [r_.res], [dstt.res])
                        else:
                            kv = dstt[:, :d * W].rearrange("p (r w) -> p r w", r=d)
                            S.op("pool", lambda E: E.memset(kv[:, :, 0:64], 0.0), [], [dstt.res])
                            S.op("pool", lambda E: E.memset(kv[:, :, 64 + n:W], 0.0), [], [dstt.res])
                            copy_op(evac_eng(("dve", "act")), kv[:, :, 64:64 + n], src, [r_.res], [dstt.res])
                nqb = n // 128
                blocks = [(r, qb) for r in range(d) for qb in range(nqb)]
                vt = {}
                st1 = {}

                def get_v(r, m):
                    if (r, m) in vt:
                        return vt[(r, m)]
                    vr = vraw[cn["v"] % NV]
                    if m == 0:
                        ve, p0, p1 = vfirsts[cn["vf"] % 4], 64, 128
                        cn["vf"] += 1
                    elif m == nqb:
                        ve, p0, p1 = vlasts[cn["vl"] % 4], 0, 64
                        cn["vl"] += 1
                    else:
                        ve, p0, p1 = vint[cn["vi"] % NV], 0, 128
                        cn["vi"] += 1
                    cn["v"] += 1
                    j0 = 128 * m - 64 + p0
                    npos = p1 - p0
                    pos0 = r + d * j0
                    srcv = VT[s, pos0:pos0 + d * (npos - 1) + 1:d, gi * 256:(gi + 1) * 256]
                    S.dma("sp", vr[p0:p1, :], srcv, writes=[vr.res])
                    S.op("pool", lambda E: E.tensor_copy(out=ve[p0:p1, :, 0:64], in_=vr[p0:p1, :].rearrange("p (h e) -> p h e", h=4)),
                         [vr.res], [ve.res])
                    vt[(r, m)] = ve
                    return ve

                def prefetch_v(i):
                    if i < len(blocks):
                        r_, qb_ = blocks[i]
                        get_v(r_, qb_)
                        get_v(r_, qb_ + 1)

                prefetch_v(0)
                prefetch_v(1)

                def stage1(i):
                    r, qb = blocks[i]
                    prefetch_v(i + 2)
                    va, vb = get_v(r, qb), get_v(r, qb + 1)
                    for key in [k_ for k_ in vt if (k_[0] < r) or (k_[0] == r and k_[1] < qb)]:
                        vt.pop(key)
                    sc = psc[cn["sc"] % 3]
                    pe1 = pe_[cn["sc"] % 3]
                    cn["sc"] += 1
                    for hp in range(2):
                        mm(sc[:, 2 * hp:2 * hp + 2, :, :], identb[:], negm[:, :, :, :], True, False, [identb.res, negm.res], sc.res)
                        for h in (2 * hp, 2 * hp + 1):
                            kv = Ks[h][:, :d * W].rearrange("p (r w) -> p r w", r=d)
                            qv = Qs[h][:, :].rearrange("p (r i) -> p r i", r=d)
                            q_ap = qv[:, r, 128 * qb:128 * qb + 128]
                            mm(sc[:, h, 0, :], kv[:, r, 128 * qb:128 * qb + 128], q_ap, False, False, [Ks[h].res, Qs[h].res], sc.res)
                            mm(sc[:, h, 1, :], kv[:, r, 128 * qb + 128:128 * qb + 256], q_ap, False, h == 2 * hp + 1, [Ks[h].res, Qs[h].res], sc.res)
                    S.op("act", lambda E: E.activation(out=pe1[:], in_=sc[:, :, :, :], func=AF.Exp, scale=0.125), [sc.res], [pe1.res])
                    st1[i] = (pe1, va, vb)

                def stage2(i):
                    r, qb = blocks[i]
                    pe1, va, vb = st1.pop(i)
                    po = ppo[cn["po"] % 2]
                    cn["po"] += 1
                    for h in range(4):
                        mm(po[:, h, :], pe1[:, h, 0, :], va[:, h, :], True, False, [pe1.res, va.res], po.res)
                        mm(po[:, h, :], pe1[:, h, 1, :], vb[:, h, :], False, True, [pe1.res, vb.res], po.res)
                    a = aos[cn["ao"] % 3]
                    cn["ao"] += 1
                    copy_op("dve", a[:].rearrange("p (h e) -> p h e", h=4), po[:, :, :], [po.res], [a.res])
                    pos0 = r + d * 128 * qb
                    S.dma("sp", AO[s, gi, pos0:pos0 + d * 127 + 1:d, :], a[:], reads=[a.res])

                for i in range(len(blocks) + 1):
                    if i < len(blocks):
                        stage1(i)
                    if i >= 1:
                        stage2(i - 1)
            S.barrier()

    def p3b_attn_merge(l, s):
        with ExitStack() as ph:
            a3 = [sb(ph, "m3a%d" % i, [128, 3, 260], F32) for i in range(2)]
            acc = [sb(ph, "m3acc%d" % i, [128, 260], F32) for i in range(2)]
            rd = [sb(ph, "m3rd%d" % i, [128, 4], F32) for i in range(2)]
            yb = [sb(ph, "m3yb%d" % i, [128, 256], BF16) for i in range(2)]
            pt = [ps(ph, "m3pt%d" % i, [128, 2, 128], BF16) for i in range(2)]
            ybT = [sb(ph, "m3ybT%d" % i, [128, 2, 1024], BF16) for i in range(2)]
            for tt in range(32):
                a, c, r_, y, p = a3[tt % 2], acc[tt % 2], rd[tt % 2], yb[tt % 2], pt[tt % 2]
                o = ybT[(tt // 8) % 2]
                S.dma("sp", a[:], AO[s, :, tt * 128:(tt + 1) * 128, :].rearrange("g t c -> t g c"), writes=[a.res])
                S.op("dve", lambda E: E.tensor_tensor(out=c[:], in0=a[:, 0, :], in1=a[:, 1, :], op=ALU.add), [a.res], [c.res])
                S.op("dve", lambda E: E.tensor_tensor(out=c[:], in0=c[:], in1=a[:, 2, :], op=ALU.add), [a.res, c.res], [c.res])
                cv = c[:].rearrange("p (h e) -> p h e", h=4)
                S.op("dve", lambda E: E.reciprocal(out=r_[:], in_=cv[:, :, 64]), [c.res], [r_.res])
                for h in range(4):
                    S.op("act", lambda E, h=h: E.activation(out=y[:, h * 64:(h + 1) * 64], in_=cv[:, h, 0:64], func=AF.Identity,
                                                            scale=r_[:, h:h + 1]), [c.res, r_.res], [y.res])
                for j in range(2):
                    S.op("pe", lambda E, j=j: E.transpose(p[:, j, :], y[:, j * 128:(j + 1) * 128], identb[:]),
                         reads=[y.res, identb.res], writes=[p.res], pe_chain=True)
                copy_op("act", o[:, :, (tt % 8) * 128:(tt % 8 + 1) * 128], p[:, :, :], [p.res], [o.res])
                if tt % 8 == 7:
                    t0 = (tt // 8) * 1024
                    S.dma("pool", YB[s].rearrange("(a p) t -> p a t", p=128)[:, :, t0:t0 + 1024], o[:, :, :], reads=[o.res])
            S.barrier()

    def p4_rglru(l, s):
        with ExitStack() as ph:
            cw = sb(ph, "rcw", [128, 4, 4], F32)
            cb = sb(ph, "rcb", [128, 4], F32)
            gb = sb(ph, "rgb", [128, 2, 2, 4], F32)
            lam = sb(ph, "rlam", [128, 8], F32)
            c8 = sb(ph, "rc8", [128, 8], F32)
            c16 = sb(ph, "rc16", [128, 8], F32)
            one = sb(ph, "rone", [128, 1], F32)
            S.dma("sp", cw[:], rg_conv_w[l], writes=[cw.res])
            S.dma("sp", cb[:], rg_conv_b[l], writes=[cb.res])
            S.dma("sp", gb[:], rg_gate_b[l], writes=[gb.res])
            S.dma("sp", lam[:], rg_lam[l].rearrange("p a b -> p (a b)"), writes=[lam.res])
            S.op("dve", lambda E: E.memset(one[:], 1.0), [], [one.res])
            S.op("act", lambda E: E.activation(out=c8[:], in_=lam[:], func=AF.Exp, scale=-1.0), [lam.res], [c8.res])
            S.op("act", lambda E: E.activation(out=c8[:], in_=c8[:], func=AF.Ln, bias=one[:]), [c8.res, one.res], [c8.res])
            S.op("dve", lambda E: E.tensor_scalar(out=c16[:], in0=c8[:], scalar1=-16.0, scalar2=None, op0=ALU.mult), [c8.res], [c16.res])
            S.op("dve", lambda E: E.tensor_scalar(out=c8[:], in0=c8[:], scalar1=-8.0, scalar2=None, op0=ALU.mult), [c8.res, c16.res], [c8.res])
            gw = [sb(ph, "rgw%d" % i, [128, 128], F32) for i in range(4)]
            raw = sb(ph, "rraw", [128, L_SEQ + 3], F32)
            xr = sb(ph, "rxr", [128, L_SEQ], F32)
            gt = sb(ph, "rgt", [128, L_SEQ], F32)
            a_t = sb(ph, "rat", [128, L_SEQ], F32)
            xn = sb(ph, "rxn", [128, L_SEQ], F32)
            hf = sb(ph, "rhf", [128, L_SEQ], F32)
            yo = sb(ph, "ryo", [128, L_SEQ], BF16)
            CH = 1024
            r_t = sb(ph, "rrt", [128, L_SEQ], F32)
            xrb = sb(ph, "rxrb", [128, L_SEQ], BF16)
            gwb = [sb(ph, "rgwb%d" % i, [128, 128], BF16) for i in range(4)]
            pg = [ps(ph, "rpg%d" % i, [128, CH], F32) for i in range(4)]
            pc = 0
            for cc in range(4):
                S.op("pool", lambda E: E.memset(raw[:, 0:2], 0.0), [], [raw.res])
                S.op("pool", lambda E: E.memset(raw[:, L_SEQ + 2:L_SEQ + 3], 0.0), [], [raw.res])
                S.dma("sp", raw[:, 2:L_SEQ + 2], XC[s, cc * 128:(cc + 1) * 128, :], writes=[raw.res])
                S.dma("sp", gt[:], XC[s, 512 + cc * 128:512 + (cc + 1) * 128, :], writes=[gt.res])
                S.op("pool", lambda E: E.tensor_scalar(out=xr[:], in0=raw[:, 2:L_SEQ + 2], scalar1=cw[:, cc, 2:3], scalar2=cb[:, cc:cc + 1],
                                                       op0=ALU.mult, op1=ALU.add), [raw.res, cw.res, cb.res], [xr.res])
                for k in (0, 1, 3):
                    S.op("dve", lambda E, k=k: E.scalar_tensor_tensor(out=xr[:], in0=raw[:, k:k + L_SEQ], scalar=cw[:, cc, k:k + 1], in1=xr[:],
                                                                      op0=ALU.mult, op1=ALU.add), [raw.res, cw.res, xr.res], [xr.res])
                for dirn in range(2):
                    for gate in range(2):
                        S.dma("sp", gw[dirn * 2 + gate][:], rg_gate_w[l, dirn, gate, cc], writes=[gw[dirn * 2 + gate].res])
                        copy_op("dve", gwb[dirn * 2 + gate][:], gw[dirn * 2 + gate][:], [gw[dirn * 2 + gate].res], [gwb[dirn * 2 + gate].res])
                copy_op("act", xrb[:], xr[:], [xr.res], [xrb.res])
                for dirn in range(2):
                    idx = dirn * 4 + cc
                    for c0 in range(0, L_SEQ, CH):
                        pr_, pi_ = pg[pc % 4], pg[(pc + 1) % 4]
                        pc += 2
                        for nt in range(CH // 512):
                            sl = slice(c0 + nt * 512, c0 + (nt + 1) * 512)
                            mm(pr_[:, nt * 512:(nt + 1) * 512], gwb[dirn * 2][:], xrb[:, sl], True, True, [gwb[dirn * 2].res, xrb.res], pr_.res)
                            mm(pi_[:, nt * 512:(nt + 1) * 512], gwb[dirn * 2 + 1][:], xrb[:, sl], True, True, [gwb[dirn * 2 + 1].res, xrb.res], pi_.res)
                        S.op("act", lambda E: E.activation(out=r_t[:, c0:c0 + CH], in_=pr_[:], func=AF.Sigmoid, bias=gb[:, dirn, 0, cc:cc + 1]), [pr_.res, gb.res], [r_t.res])
                        S.op("act", lambda E: E.activation(out=xn[:, c0:c0 + CH], in_=pi_[:], func=AF.Sigmoid, bias=gb[:, dirn, 1, cc:cc + 1]), [pi_.res, gb.res], [xn.res])
                    S.op("pool", lambda E: E.tensor_tensor(out=xn[:], in0=xn[:], in1=xr[:], op=ALU.mult), [xn.res, xr.res], [xn.res])
                    S.op("act", lambda E: E.activation(out=a_t[:], in_=r_t[:], func=AF.Exp, scale=c8[:, idx:idx + 1]), [r_t.res, c8.res], [a_t.res])
                    S.op("act", lambda E: E.activation(out=r_t[:], in_=r_t[:], func=AF.Exp, scale=c16[:, idx:idx + 1]), [r_t.res, c16.res], [r_t.res])
                    S.op("act", lambda E: E.activation(out=r_t[:], in_=r_t[:], func=AF.Sqrt, scale=-1.0, bias=one[:]), [r_t.res, one.res], [r_t.res])
                    bcol = 0 if dirn == 0 else L_SEQ - 1
                    S.op("dve", lambda E: E.memset(r_t[:, bcol:bcol + 1], 1.0), [r_t.res], [r_t.res])
                    S.op("dve", lambda E: E.tensor_tensor(out=xn[:], in0=xn[:], in1=r_t[:], op=ALU.mult), [xn.res, r_t.res], [xn.res])
                    if dirn == 0:
                        S.op("dve", lambda E: E.tensor_tensor_scan(out=hf[:, :], data0=a_t[:, :], data1=xn[:, :], initial=0.0, op0=ALU.mult, op1=ALU.add),
                             [a_t.res, xn.res], [hf.res])
                    else:
                        hb = raw
                        S.op("dve", lambda E: E.tensor_tensor_scan(out=hb[:, L_SEQ - 1::-1], data0=a_t[:, ::-1], data1=xn[:, ::-1], initial=0.0,
                                                                  op0=ALU.mult, op1=ALU.add), [a_t.res, xn.res], [hb.res])
                        S.op("pool", lambda E: E.tensor_tensor(out=hf[:], in0=hf[:], in1=hb[:, 0:L_SEQ], op=ALU.add), [hf.res, hb.res], [hf.res])
                S.op("act", lambda E: E.activation(out=gt[:], in_=gt[:], func=AF.Gelu_apprx_tanh), [gt.res], [gt.res])
                S.op("pool", lambda E: E.tensor_tensor(out=yo[:], in0=hf[:], in1=gt[:], op=ALU.mult), [hf.res, gt.res], [yo.res])
                S.dma("pool", YC[s, cc * 128:(cc + 1) * 128, :], yo[:], reads=[yo.res])
            S.barrier()


    def ln_epilogue(lnbuf, po, xres, gB, bB, epsT, k):
        ysb, st, ti, out = lnbuf["y"][k % 2], lnbuf["st"][k % 2], lnbuf["ti"][k % 2], lnbuf["o"][k % 2]
        I32 = mybir.dt.int32
        S.op("dve", lambda E: E.memset(st[:, 0:2], 0.0), [st.res], [st.res])
        S.op("dve", lambda E: E.scalar_tensor_tensor(out=ysb[:], in0=xres[:], scalar=float(ALPHA), in1=po[:, :], op0=ALU.mult, op1=ALU.add,
                                                     accum_out=st[:, 0:1]), [xres.res, po.res, st.res], [ysb.res, st.res])
        S.op("dve", lambda E: E.scalar_tensor_tensor(out=out[:], in0=ysb[:], scalar=1.0, in1=ysb[:], op0=ALU.mult, op1=ALU.mult,
                                                     accum_out=st[:, 1:2]), [ysb.res, st.res], [out.res, st.res])
        S.op("dve", lambda E: E.tensor_scalar(out=st[:, 2:3], in0=st[:, 0:1], scalar1=1.0 / D, scalar2=None, op0=ALU.mult), [st.res], [st.res])
        S.op("dve", lambda E: E.tensor_tensor(out=st[:, 3:4], in0=st[:, 2:3], in1=st[:, 2:3], op=ALU.mult), [st.res], [st.res])
        S.op("dve", lambda E: E.scalar_tensor_tensor(out=st[:, 4:5], in0=st[:, 1:2], scalar=1.0 / D, in1=st[:, 3:4], op0=ALU.mult, op1=ALU.subtract),
             [st.res], [st.res])
        S.op("dve", lambda E: E.tensor_scalar(out=st[:, 4:5], in0=st[:, 4:5], scalar1=float(LN_EPS), scalar2=None, op0=ALU.add), [st.res], [st.res])
        S.op("dve", lambda E: E.tensor_single_scalar(out=ti[:, 0:1], in_=st[:, 4:5].bitcast(I32), scalar=1, op=ALU.logical_shift_right),
             [st.res], [ti.res])
        S.op("dve", lambda E: E.tensor_scalar(out=ti[:, 1:2], in0=ti[:, 0:1], scalar1=-1.0, scalar2=1597463007.0, op0=ALU.mult, op1=ALU.add),
             [ti.res], [ti.res])
        S.op("dve", lambda E: E.tensor_copy(out=st[:, 6:7], in_=ti[:, 1:2].bitcast(F32)), [ti.res], [st.res])
        for _ in range(3):
            S.op("dve", lambda E: E.scalar_tensor_tensor(out=st[:, 5:6], in0=st[:, 6:7], scalar=st[:, 4:5], in1=st[:, 6:7], op0=ALU.mult, op1=ALU.mult),
                 [st.res], [st.res])
            S.op("dve", lambda E: E.tensor_scalar(out=st[:, 5:6], in0=st[:, 5:6], scalar1=-0.5, scalar2=1.5, op0=ALU.mult, op1=ALU.add), [st.res], [st.res])
            S.op("dve", lambda E: E.tensor_tensor(out=st[:, 6:7], in0=st[:, 6:7], in1=st[:, 5:6], op=ALU.mult), [st.res], [st.res])
        S.op("dve", lambda E: E.tensor_scalar(out=ysb[:], in0=ysb[:], scalar1=st[:, 2:3], scalar2=st[:, 6:7], op0=ALU.subtract, op1=ALU.mult),
             [ysb.res, st.res], [ysb.res])
        S.op("pool", lambda E: E.tensor_tensor(out=out[:], in0=ysb[:], in1=gB[:], op=ALU.mult), [ysb.res, gB.res], [out.res])
        S.op("pool", lambda E: E.tensor_tensor(out=out[:], in0=out[:], in1=bB[:], op=ALU.add), [out.res, bB.res], [out.res])
        return out

    def ln_epilogue_act(lnbuf, po, xres, gB, bB, epsT, k):
        ysb, st, out = lnbuf["y"][k % 2], lnbuf["st"][k % 2], lnbuf["o"][k % 2]
        S.op("dve", lambda E: E.scalar_tensor_tensor(out=ysb[:], in0=xres[:], scalar=float(ALPHA), in1=po[:, :], op0=ALU.mult, op1=ALU.add),
             [xres.res, po.res], [ysb.res])
        S.op("act", lambda E: E.activation(out=out[:], in_=ysb[:], func=AF.Identity, accum_out=st[:, 0:1]), [ysb.res], [out.res, st.res])
        S.op("act", lambda E: E.activation(out=out[:], in_=ysb[:], func=AF.Square, accum_out=st[:, 1:2]), [ysb.res, out.res], [out.res, st.res])
        S.op("dve", lambda E: E.tensor_scalar(out=st[:, 2:3], in0=st[:, 0:1], scalar1=1.0 / D, scalar2=None, op0=ALU.mult), [st.res], [st.res])
        S.op("dve", lambda E: E.tensor_tensor(out=st[:, 3:4], in0=st[:, 2:3], in1=st[:, 2:3], op=ALU.mult), [st.res], [st.res])
        S.op("dve", lambda E: E.scalar_tensor_tensor(out=st[:, 4:5], in0=st[:, 1:2], scalar=1.0 / D, in1=st[:, 3:4], op0=ALU.mult, op1=ALU.subtract),
             [st.res], [st.res])
        S.op("act", lambda E: E.activation(out=st[:, 5:6], in_=st[:, 4:5], func=AF.Sqrt, bias=epsT[:]), [st.res, epsT.res], [st.res])
        S.op("dve", lambda E: E.reciprocal(out=st[:, 6:7], in_=st[:, 5:6]), [st.res], [st.res])
        S.op("dve", lambda E: E.tensor_scalar(out=ysb[:], in0=ysb[:], scalar1=st[:, 2:3], scalar2=st[:, 6:7], op0=ALU.subtract, op1=ALU.mult),
             [ysb.res, st.res], [ysb.res])
        S.op("pool", lambda E: E.tensor_tensor(out=out[:], in0=ysb[:], in1=gB[:], op=ALU.mult), [ysb.res, gB.res], [out.res])
        S.op("pool", lambda E: E.tensor_tensor(out=out[:], in0=out[:], in1=bB[:], op=ALU.add), [out.res, bB.res], [out.res])
        return out

    def ln_bufs(ph, tag):
        return {"y": [sb(ph, "lny%s%d" % (tag, i), [128, D], F32) for i in range(2)],
                "st": [sb(ph, "lnst%s%d" % (tag, i), [128, 8], F32) for i in range(2)],
                "ti": [sb(ph, "lnti%s%d" % (tag, i), [128, 2], mybir.dt.int32) for i in range(2)],
                "o": [sb(ph, "lno%s%d" % (tag, i), [128, D], F32) for i in range(2)]}

    def to_xT(bufs, xo, tok0, k):
        xb, pt = bufs["xb"][k % 2], bufs["pt"][k % 2]
        copy_op("act", xb[:], xo[:], [xo.res], [xb.res])
        for kc in range(8):
            S.op("pe", lambda E, kc=kc: E.transpose(pt[:, kc, :], xb[:, kc * 128:(kc + 1) * 128], identb[:]),
                 reads=[xb.res, identb.res], writes=[pt.res], pe_chain=True)
        copy_op("dve", xT[:, :, tok0:tok0 + 128], pt[:, :, :], [pt.res], [xT.rs[tok0 // 128]])

    def p5_merge(l, s):
        with ExitStack() as ph:
            wo = sb(ph, "wo", [128, 8, D], BF16)
            S.dma("sp", wo[:], WO_b[l], writes=[wo.res])
            bg = sb(ph, "bg", [128, 24], F32)
            S.dma("sp", bg[:], b_gate[l], writes=[bg.res])
            gB = sb(ph, "ln1g", [128, D], F32)
            bB = sb(ph, "ln1b", [128, D], F32)
            S.dma("sp", gB[:], ln1_g[l], writes=[gB.res])
            S.dma("sp", bB[:], ln1_b[l], writes=[bB.res])
            epsT = sb(ph, "eps1", [128, 1], F32)
            S.op("dve", lambda E: E.memset(epsT[:], LN_EPS), [], [epsT.res])
            TBm = 512
            NG = 6
            gtl = [sb(ph, "p5gt%d" % i, [128, TBm], BF16) for i in range(NG)]
            yin = [sb(ph, "p5y%d" % i, [128, 10, TBm], BF16) for i in range(2)]
            NWB = NG
            wb = [sb(ph, "p5wb%d" % i, [128, 4, 128], BF16) for i in range(NWB)]
            pp = [ps(ph, "p5pp%d" % i, [128, TBm], F32) for i in range(4)]
            tmpm = [sb(ph, "p5tm%d" % i, [128, TBm], F32) for i in range(1)] * 2
            maccs = [sb(ph, "p5macc%d" % i, [128, TBm], F32) for i in range(2)]
            tmps = [sb(ph, "p5tmps%d" % i, [128, TBm], F32) for i in range(4)]
            mixT = [sb(ph, "p5mix%d" % i, [128, 8, TBm], BF16) for i in range(2)]
            po = [ps(ph, "p5po%d" % i, [128, D], F32) for i in range(2)]
            xres = [sb(ph, "p5xr%d" % i, [128, D], F32) for i in range(2)]
            lnb = ln_bufs(ph, "a")
            KC = (4, 2, 4)
            WB = (WBA_b, WBB_b, WBC_b)
            yoff = (0, 4, 6)
            cn = {"w": 0, "p": 0, "g": 0, "k": 0}
            pend = []
            x_src = x_in[s] if l == 0 else X2[s]
            NB = L_SEQ // TBm
            items = [(m, br) for m in range(8) for br in range(3)]

            def wo_tile(tb, tt):
                mx = mixT[tb % 2]
                tok0 = tb * TBm + tt * 128
                k = cn["k"]
                cn["k"] += 1
                p_ = po[k % 2]
                xr_ = xres[k % 2]
                S.dma("sp", xr_[:], x_src[tok0:tok0 + 128, :], writes=[xr_.res])
                for nh in range(2):
                    for kc in range(8):
                        mm(p_[:, nh * 512:(nh + 1) * 512], mx[:, kc, tt * 128:(tt + 1) * 128], wo[:, kc, nh * 512:(nh + 1) * 512],
                           kc == 0, kc == 7, [mx.res, wo.res], p_.res)
                o_ = ln_epilogue(lnb, p_, xr_, gB, bB, epsT, k)
                S.dma("pool", X1[s, tok0:tok0 + 128, :], o_[:], reads=[o_.res])

            def load_y(tb_):
                yy_ = yin[tb_ % 2]
                ta = tb_ * TBm
                S.dma("sp", yy_[:, 0:4, :], YA[s].rearrange("(a p) t -> p a t", p=128)[:, :, ta:ta + TBm], writes=[yy_.res])
                S.dma("sp", yy_[:, 4:6, :], YB[s].rearrange("(a p) t -> p a t", p=128)[:, :, ta:ta + TBm], writes=[yy_.res])
                S.dma("sp", yy_[:, 6:10, :], YC[s].rearrange("(a p) t -> p a t", p=128)[:, :, ta:ta + TBm], writes=[yy_.res])

            for tb in range(NB + 1):
                if tb == NB:
                    for tt in range(TBm // 128):
                        wo_tile(tb - 1, tt)
                    break
                t0 = tb * TBm
                if tb == 0:
                    load_y(0)
                if tb + 1 < NB:
                    load_y(tb + 1)
                yi = yin[tb % 2]
                mx = mixT[tb % 2]
                loaded = {}

                def load(i):
                    m, br = items[i]
                    b = wb[cn["w"] % NWB]
                    gt_ = gtl[cn["w"] % NG]
                    cn["w"] += 1
                    col = br * 8 + m
                    S.dma("sp", b[:, :KC[br], :], WB[br][l, m], writes=[b.res])
                    S.dma("sp", gt_[:], GT[s, col * 128:(col + 1) * 128, t0:t0 + TBm], writes=[gt_.res])
                    loaded[i] = (b, gt_)

                PF = 3
                for i in range(len(items) + PF):
                    if i < len(items):
                        load(i)
                    j = i - PF
                    if j < 0:
                        continue
                    m, br = items[j]
                    b, gt_ = loaded.pop(j)
                    pt_ = pp[cn["p"] % 4]
                    cn["p"] += 1
                    col = br * 8 + m
                    for kc in range(KC[br]):
                        mm(pt_[:, :], b[:, kc, :], yi[:, yoff[br] + kc, :], kc == 0, kc == KC[br] - 1, [b.res, yi.res], pt_.res)
                    mac_ = maccs[m % 2]
                    if br == 0:
                        S.op("dve", lambda E: E.tensor_tensor(out=mac_[:], in0=gt_[:], in1=pt_[:, :], op=ALU.mult), [gt_.res, pt_.res], [mac_.res])
                    else:
                        t_ = tmps[(m % 2) * 2 + (br - 1)]
                        S.op("dve", lambda E: E.tensor_tensor(out=t_[:], in0=gt_[:], in1=pt_[:, :], op=ALU.mult), [gt_.res, pt_.res], [t_.res])
                        if br == 1:
                            S.op("dve", lambda E: E.tensor_tensor(out=mac_[:], in0=mac_[:], in1=t_[:], op=ALU.add), [mac_.res, t_.res], [mac_.res])
                        else:
                            S.op("dve", lambda E: E.tensor_tensor(out=mx[:, m, :], in0=mac_[:], in1=t_[:], op=ALU.add), [mac_.res, t_.res], [mx.res])
                    if tb >= 1 and j % 6 == 5:
                        wo_tile(tb - 1, j // 6)
            S.barrier()

    def p6_ffn(l, s, last):
        with ExitStack() as ph:
            wdn = sb(ph, "wdn", [128, 24, D], BF16)
            S.dma("sp", wdn[:, 0:12, :], WDN_b[l, :, 0:12, :], writes=[wdn.res])
            S.dma("sp", wdn[:, 12:24, :], WDN_b[l, :, 12:24, :], writes=[wdn.res])
            fw = sb(ph, "ffw", [128, 24, 3], F32)
            fb = sb(ph, "ffb", [128, 24], F32)
            S.dma("sp", fw[:], ffn_conv_w[l], writes=[fw.res])
            S.dma("sp", fb[:], ffn_conv_b[l], writes=[fb.res])
            gB = sb(ph, "ln2g", [128, D], F32)
            bB = sb(ph, "ln2b", [128, D], F32)
            S.dma("sp", gB[:], ln2_g[l], writes=[gB.res])
            S.dma("sp", bB[:], ln2_b[l], writes=[bB.res])
            epsT = sb(ph, "eps2", [128, 1], F32)
            S.op("dve", lambda E: E.memset(epsT[:], LN_EPS), [], [epsT.res])
            TBf = 512
            wu = [sb(ph, "p6wu%d" % i, [128, 2, 8, 128], BF16) for i in range(4)]
            pgt = [ps(ph, "p6pg%d" % i, [128, 1024], F32) for i in range(2)]
            put = [ps(ph, "p6pu%d" % i, [128, 512], F32) for i in range(2)]
            po = ps(ph, "p6po", [128, D], F32)
            yv = [sb(ph, "p6y%d" % i, [128, TBf], F32) for i in range(2)]
            gl = [sb(ph, "p6g%d" % i, [128, TBf], F32) for i in range(2)]
            actT = [sb(ph, "p6act%d" % i, [128, 24, TBf], BF16) for i in range(1)]
            xres = [sb(ph, "p6xr%d" % i, [128, D], F32) for i in range(2)]
            lnb = ln_bufs(ph, "b")
            cn = {"w": 0, "p": 0, "k": 0}
            for tb in range(L_SEQ // TBf):
                t0 = tb * TBf
                at = actT[0]
                first, lastb = (t0 == 0), (t0 + TBf == L_SEQ)
                loaded = {}

                def load(i):
                    w = wu[cn["w"] % 4]
                    cn["w"] += 1
                    S.dma("sp", w[:, 0, :, :], WUP_b[l, i], writes=[w.res])
                    S.dma("sp", w[:, 1, :, :], WUP_b[l, 24 + i], writes=[w.res])
                    loaded[i] = w

                PF = 2
                for i in range(24 + PF):
                    if i < 24:
                        load(i)
                    m = i - PF
                    if m < 0:
                        continue
                    w = loaded.pop(m)
                    pg_, pu_ = pgt[cn["p"] % 2], put[cn["p"] % 2]
                    y_, g_ = yv[cn["p"] % 2], gl[cn["p"] % 2]
                    cn["p"] += 1
                    c_lo = 1 if first else 0
                    c_hi = 513 if lastb else 514
                    for (ca, cb_) in ((c_lo, 512), (512, c_hi)):
                        ta, tb_ = t0 - 1 + ca, t0 - 1 + cb_
                        for kc in range(8):
                            mm(pg_[:, ca:cb_], w[:, 0, kc, :], xT[:, kc, ta:tb_], kc == 0, kc == 7, [w.res] + xt_res(ta, tb_ - ta), pg_.res)
                    for kc in range(8):
                        mm(pu_[:, :], w[:, 1, kc, :], xT[:, kc, t0:t0 + TBf], kc == 0, kc == 7, [w.res] + xt_res(t0, TBf), pu_.res)
                    S.op("act", lambda E: E.activation(out=y_[:], in_=pg_[:, 1:513], func=AF.Identity, scale=fw[:, m, 1:2], bias=fb[:, m:m + 1]),
                         [pg_.res, fw.res, fb.res], [y_.res])
                    S.op("dve", lambda E: E.scalar_tensor_tensor(out=y_[:, c_lo:512], in0=pg_[:, c_lo:512], scalar=fw[:, m, 0:1], in1=y_[:, c_lo:512],
                                                                 op0=ALU.mult, op1=ALU.add), [pg_.res, fw.res, y_.res], [y_.res])
                    nh = c_hi - 2
                    S.op("dve", lambda E: E.scalar_tensor_tensor(out=y_[:, 0:nh], in0=pg_[:, 2:2 + nh], scalar=fw[:, m, 2:3], in1=y_[:, 0:nh],
                                                                 op0=ALU.mult, op1=ALU.add), [pg_.res, fw.res, y_.res], [y_.res])
                    S.op("act", lambda E: E.activation(out=g_[:], in_=y_[:], func=AF.Gelu_apprx_tanh), [y_.res], [g_.res])
                    S.op("dve", lambda E: E.tensor_tensor(out=at[:, m, :], in0=g_[:], in1=pu_[:, :], op=ALU.mult), [g_.res, pu_.res], [at.res])
                for tt in range(TBf // 128):
                    tok0 = t0 + tt * 128
                    k = cn["k"]
                    cn["k"] += 1
                    xr_ = xres[k % 2]
                    S.dma("sp", xr_[:], X1[s, tok0:tok0 + 128, :], writes=[xr_.res])
                    for nh in range(2):
                        for kc in range(24):
                            mm(po[:, nh * 512:(nh + 1) * 512], at[:, kc, tt * 128:(tt + 1) * 128], wdn[:, kc, nh * 512:(nh + 1) * 512],
                               kc == 0, kc == 23, [at.res, wdn.res], po.res)
                    o_ = ln_epilogue_act(lnb, po, xr_, gB, bB, epsT, k)
                    if last:
                        S.dma("pool", y_out[s, tok0:tok0 + 128, :], o_[:], reads=[o_.res])
                    else:
                        S.dma("pool", X2[s, tok0:tok0 + 128, :], o_[:], reads=[o_.res])
            S.barrier()

    g.p5_merge = p5_merge
    g.p6_ffn = p6_ffn

    def hyena_filter_all(l):
        pf_filter(l)
        fft_stage1(KTOK, 128)
        fft_stage2("filter", l)

    def hyena_seq(l, s):
        p2a_hyconv(l, s)
        fft_stage1(UTOK[s], 64)
        fft_stage2("signal", l)
        p2d_hyout(l, s)

    g.hyena_filter_all = hyena_filter_all
    g.hyena_seq = hyena_seq
    g.pf_filter = pf_filter
    g.p3_attention = p3_attention
    g.p3b_attn_merge = p3b_attn_merge
    g.p4_rglru = p4_rglru

    g.cast_weights = cast_weights
    g.p1a_load_x = p1a_load_x
    g.p1b_inproj = p1b_inproj
    return S, g, es, locals()


def _pcol(v):
    v = np.asarray(v)
    C = v.shape[-1]
    lead = v.shape[:-1]
    v = v.reshape(lead + (C // 128, 128))
    v = np.moveaxis(v, -1, 0)
    v = np.moveaxis(v, -1, 1)
    return np.ascontiguousarray(v)


def _qk_perm():
    cols = list(range(1536))
    for kind in range(2):
        base = 1536 + kind * 768
        for gi in range(3):
            for half in range(2):
                for h in range(4):
                    for e in range(32):
                        cols.append(base + (gi * 4 + h) * 64 + half * 32 + e)
    cols += list(range(3072, 4864))
    return np.array(cols)


_CONST_CACHE = {}


def make_consts():
    if _CONST_CACHE:
        return _CONST_CACHE
    bf = ml_dtypes.bfloat16
    f32 = np.float32
    c = {}
    c["c_identf"] = np.eye(128, dtype=f32)
    c["c_identb"] = np.eye(128).astype(bf)
    inv = (np.float32(10000.0) ** (-np.arange(0, 64, 2, dtype=f32) / np.float32(64))).astype(f32)
    ang = (np.arange(L_SEQ, dtype=f32)[:, None] * inv[None, :]).astype(f32)
    c["c_ropec"] = np.ascontiguousarray(np.tile(np.cos(ang).T.astype(f32), (4, 1)))
    c["c_ropes"] = np.ascontiguousarray(np.tile(np.sin(ang).T.astype(f32), (4, 1)))
    t = np.linspace(0.0, 1.0, L_SEQ, dtype=f32)[:, None]
    w = (np.float32(2.0 * math.pi) * np.arange(L_SEQ, dtype=f32)[:, None] / np.float32(L_SEQ)).astype(f32)
    bands = np.linspace(1e-4, 7, 8, dtype=f32)[None, :]
    z = np.concatenate([t, np.cos(bands * w), -np.sin(bands * w)], axis=-1).astype(f32)
    c["c_zT"] = np.ascontiguousarray(z.T)
    c["c_tv"] = np.ascontiguousarray(np.tile(t.T, (128, 1)).astype(f32))
    deltas = np.abs(np.linspace(math.log(1e-2) / 1.5, math.log(1e-2) / 0.3, D_HY, dtype=f32))
    c["c_ndelta"] = _pcol(-deltas).astype(f32)
    n1 = np.arange(128)[:, None]
    f1 = np.arange(128)[None, :]
    a = 2 * np.pi * n1 * f1 / 128.0
    c["c_f1"] = np.stack([np.cos(a), -np.sin(a)], axis=1).astype(bf)
    f1c = np.arange(128)[:, None]
    n2 = np.arange(64)[None, :]
    a = 2 * np.pi * f1c * n2 / NFFT
    c["c_tw"] = np.stack([np.cos(a), np.sin(a)], axis=1).astype(f32)
    n2c = np.arange(64)[:, None]
    f2 = np.arange(64)[None, :]
    a = 2 * np.pi * n2c * f2 / 64.0
    C2, S2 = np.cos(a), np.sin(a)
    l2re = np.concatenate([C2, S2], axis=0)
    l2im = np.concatenate([-S2, C2], axis=0)
    c["c_l2x"] = np.stack([np.concatenate([l2re, l2im], axis=1), np.concatenate([l2re, l2re], axis=1),
                           np.concatenate([-l2im, l2im], axis=1)], axis=1).astype(bf)
    m0 = np.concatenate([np.concatenate([C2, S2], axis=1), np.concatenate([-S2, C2], axis=1)], axis=0)
    m1 = np.concatenate([np.concatenate([S2, -C2], axis=1), np.concatenate([-C2, -S2], axis=1)], axis=0)
    c["c_i2x"] = np.stack([m0, m1], axis=1).astype(bf)
    f1c = np.arange(128)[:, None]
    t1 = np.arange(64)[None, :]
    a = 2 * np.pi * f1c * t1 / 128.0
    c["c_i1"] = np.stack([np.cos(a) / NFFT, -np.sin(a) / NFFT], axis=1).astype(bf)
    p = np.arange(128)[:, None]
    j = np.arange(128)[None, :]
    c["c_mask"] = np.stack([(p >= j), (p <= j)], axis=1).astype(bf)
    neg = np.where(np.stack([(p >= j), (p <= j)], axis=1), 0.0, -30000.0)
    c["c_negm"] = np.stack([neg, neg], axis=1).astype(bf)
    _CONST_CACHE.update(c)
    return c


def prep_shared(inputs, NL=NLAYER):
    f32 = np.float32
    g = {}
    perm = _qk_perm()
    g["w_in"] = np.ascontiguousarray(np.asarray(inputs["w_in"], f32)[:NL][:, :, perm])
    for k in ("w_gate", "w_br_a", "w_br_b", "w_br_c", "w_o", "w_up", "w_down"):
        g[k] = np.ascontiguousarray(np.asarray(inputs[k], f32)[:NL])
    A = lambda k: np.asarray(inputs[k], f32)[:NL]
    g["hy_conv_w"] = np.stack([_pcol(A("hy_conv_w")[l]) for l in range(NL)])
    g["hy_conv_b"] = np.stack([_pcol(A("hy_conv_b")[l]) for l in range(NL)])
    g["hy_bias"] = np.stack([_pcol(A("hy_bias")[l]) for l in range(NL)])
    g["hy_w1"] = A("hy_filt_w1")
    g["hy_w2"] = A("hy_filt_w2")
    g["hy_w3"] = A("hy_filt_w3")
    g["hy_b1"] = A("hy_filt_b1")[:, :, None]
    g["hy_b2"] = A("hy_filt_b2")[:, :, None]
    g["hy_fr"] = A("hy_filt_freq")[:, :, None]
    g["hy_b3"] = np.stack([_pcol(A("hy_filt_b3")[l]) for l in range(NL)])
    g["rg_conv_w"] = np.stack([_pcol(A("rg_conv_w")[l]) for l in range(NL)])
    g["rg_conv_b"] = np.stack([_pcol(A("rg_conv_b")[l]) for l in range(NL)])
    gw = A("rg_gate_w")
    bd = np.zeros((NL, 2, 2, 4, 128, 128), f32)
    for cc in range(4):
        bd[:, :, :, cc, 0:64, 0:64] = gw[:, :, :, 2 * cc]
        bd[:, :, :, cc, 64:128, 64:128] = gw[:, :, :, 2 * cc + 1]
    g["rg_gate_w"] = bd
    g["rg_gate_b"] = np.stack([_pcol(A("rg_gate_b")[l]) for l in range(NL)])
    g["rg_gate_b"] = np.ascontiguousarray(np.transpose(g["rg_gate_b"], (0, 1, 3, 4, 2)))
    g["rg_lam"] = np.ascontiguousarray(np.transpose(np.stack([_pcol(A("rg_lam")[l]) for l in range(NL)]), (0, 1, 3, 2)))
    g["b_gate"] = np.stack([_pcol(A("b_gate")[l]) for l in range(NL)])
    g["ffn_conv_w"] = np.stack([_pcol(A("ffn_conv_w")[l]) for l in range(NL)])
    g["ffn_conv_b"] = np.stack([_pcol(A("ffn_conv_b")[l]) for l in range(NL)])
    for k in ("ln1_g", "ln1_b", "ln2_g", "ln2_b"):
        g[k] = np.ascontiguousarray(np.broadcast_to(A(k)[:, None, :], (NL, 128, D)))
    g.update(make_consts())
    return {k: np.ascontiguousarray(v) for k, v in g.items()}


def build_full(nc, NS=2, NL=NLAYER, dbg=None):
    S, g, es, loc = build_program(nc, NS=NS, NL=NL, dbg=dbg)
    for l in range(NL):
        g.cast_weights(l)
        g.hyena_filter_all(l)
    for s in range(NS):
        g.p1a_load_x(s)
        for l in range(NL):
            last = (l == NL - 1)
            g.p1b_inproj(l, s)
            g.hyena_seq(l, s)
            g.p3_attention(l, s)
            g.p3b_attn_merge(l, s)
            g.p4_rglru(l, s)
            g.p5_merge(l, s)
            g.p1a_load_x(s, loc["X1"][s])
            g.p6_ffn(l, s, last)
            if not last:
                g.p1a_load_x(s, loc["X2"][s])
    S.barrier()
    es.close()
    return S


def kernel(**inputs):
    n_cores = 8
    xp = np.asarray(inputs["x_prompt"], np.float32)
    xs = np.asarray(inputs["x_sample"], np.float32)
    shared = prep_shared(inputs, NL=NLAYER)
    nc = bass.Bass("TRN2", target_bir_lowering=False)
    build_full(nc, NS=2, NL=NLAYER)
    in_maps = []
    for c in range(n_cores):
        m = dict(shared)
        m["x"] = np.ascontiguousarray(np.stack([xp[c], xs[c % 4]]))
        in_maps.append(m)
    res = run_bass_kernel_spmd(nc, in_maps, core_ids=list(range(n_cores)))
    y_prompt = np.stack([np.asarray(res.results[c]["y"][0], np.float32) for c in range(8)])
    y_sample = np.stack([np.asarray(res.results[c]["y"][1], np.float32) for c in range(4)])
    return (y_prompt, y_sample)
```

```python
import math
from contextlib import ExitStack

import numpy as np
import ml_dtypes
import concourse.bass as bass
import concourse.mybir as mybir
from concourse.bass_utils import run_bass_kernel_spmd

F32 = mybir.dt.float32
BF16 = mybir.dt.bfloat16
AF = mybir.ActivationFunctionType
ALU = mybir.AluOpType
AX = mybir.AxisListType

L_SEQ = 4096
D = 1024
D_HY = 512
D_IN = 4864
D_FF = 3072
NLAYER = 2
ALPHA = (2 * NLAYER) ** 0.25
LN_EPS = 1e-5
TB = 1024
NFFT = 8192


class Res:
    __slots__ = ("w", "r", "psum")

    def __init__(self, psum=False):
        self.w = None
        self.r = {}
        self.psum = psum


class Sched:
    LIM = 40000

    def __init__(self, nc, es):
        self.nc, self.es = nc, es
        self.eng = dict(pe=nc.tensor, act=nc.scalar, dve=nc.vector, pool=nc.gpsimd, sp=nc.sync)
        self.sems = {}
        self.epoch = {k: 0 for k in self.eng}
        self.cnt = {k: 0 for k in self.eng}
        self.waited = {k: {} for k in self.eng}
        self.ND = 40
        self.dcount = 0
        self.dlast = {}
        self.nops = 0

    def _sem(self, key):
        if key not in self.sems:
            self.sems[key] = self.es.enter_context(self.nc.semaphore("s%d" % len(self.sems)))
        return self.sems[key]

    def op(self, e, fn, reads=(), writes=(), dma=False, pe_chain=False):
        deps = {}

        def add(ev):
            if ev is None:
                return
            k, v = ev
            if deps.get(k, 0) < v:
                deps[k] = v

        for r in reads:
            add(r.w)
            if r.psum:
                for k, ev in r.r.items():
                    if k[0] != e:
                        add(ev)
        for w in writes:
            add(w.w)
            for ev in w.r.values():
                add(ev)
        if dma:
            j = self.dcount
            self.dcount += 1
            slot = j % self.ND
            val = 16 * (j // self.ND + 1)
            if j >= self.ND:
                add((("d", slot), val - 16))
            ev = (("d", slot), val)
            self.dlast[("d", slot)] = val
        else:
            if self.cnt[e] >= self.LIM:
                self.epoch[e] += 1
                self.cnt[e] = 0
            self.cnt[e] += 1
            ev = ((e, self.epoch[e]), self.cnt[e])
        E = self.eng[e]
        wd = self.waited[e]
        for k, v in deps.items():
            if pe_chain and k[0] == e:
                continue
            if wd.get(k, 0) >= v:
                continue
            E.wait_ge(self._sem(k), v)
            wd[k] = v
        ins = fn(E)
        ins.then_inc(self._sem(ev[0]), 16 if dma else 1)
        self.nops += 1
        for w in writes:
            w.w = ev
            w.r = {}
        for r in reads:
            if r.w is not ev:
                r.r[ev[0]] = ev
        return ev

    def dma(self, q, out, in_, reads=(), writes=(), **kw):
        return self.op(q, lambda E: E.dma_start(out=out, in_=in_, **kw), reads=reads, writes=writes, dma=True)

    def barrier(self):
        evs = {}
        for e in self.eng:
            if self.cnt[e] > 0:
                evs[(e, self.epoch[e])] = self.cnt[e]
        evs.update(self.dlast)
        for e, E in self.eng.items():
            wd = self.waited[e]
            for k, v in evs.items():
                if wd.get(k, 0) >= v:
                    continue
                E.wait_ge(self._sem(k), v)
                wd[k] = v


class T:
    def __init__(self, t, n=1, psum=False):
        self.t = t
        self.rs = [Res(psum) for _ in range(n)]

    @property
    def res(self):
        return self.rs[0]

    def __getitem__(self, idx):
        return self.t[idx]


class Ctx:
    pass


def build_program(nc, NS=2, NL=2, dbg=None, stop_after=None):
    dbg = dbg or {}
    es = ExitStack()
    S = Sched(nc, es)
    g = Ctx()

    def din(name, shape, dt=F32):
        return nc.dram_tensor(name, list(shape), dt, kind="ExternalInput").ap()

    def dscr(name, shape, dt):
        kind = "ExternalOutput" if name in dbg else "Internal"
        return nc.dram_tensor(name, list(shape), dt, kind=kind).ap()

    x_in = din("x", [NS, L_SEQ, D])
    y_out = nc.dram_tensor("y", [NS, L_SEQ, D], F32, kind="ExternalOutput").ap()
    w_in = din("w_in", [NL, D, D_IN])
    w_gate = din("w_gate", [NL, D, 3 * D])
    w_br_a = din("w_br_a", [NL, 512, D])
    w_br_b = din("w_br_b", [NL, 256, D])
    w_br_c = din("w_br_c", [NL, 512, D])
    w_o = din("w_o", [NL, D, D])
    w_up = din("w_up", [NL, D, 2 * D_FF])
    w_down = din("w_down", [NL, D_FF, D])
    hy_conv_w = din("hy_conv_w", [NL, 128, 12, 3])
    hy_conv_b = din("hy_conv_b", [NL, 128, 12])
    hy_bias = din("hy_bias", [NL, 128, 4])
    hy_w1 = din("hy_w1", [NL, 17, 64])
    hy_w2 = din("hy_w2", [NL, 64, 64])
    hy_w3 = din("hy_w3", [NL, 64, 1024])
    hy_b1 = din("hy_b1", [NL, 64, 1])
    hy_b2 = din("hy_b2", [NL, 64, 1])
    hy_fr = din("hy_fr", [NL, 64, 1])
    hy_b3 = din("hy_b3", [NL, 128, 8])
    rg_conv_w = din("rg_conv_w", [NL, 128, 4, 4])
    rg_conv_b = din("rg_conv_b", [NL, 128, 4])
    rg_gate_w = din("rg_gate_w", [NL, 2, 2, 4, 128, 128])
    rg_gate_b = din("rg_gate_b", [NL, 128, 2, 2, 4])
    rg_lam = din("rg_lam", [NL, 128, 2, 4])
    b_gate = din("b_gate", [NL, 128, 24])
    ffn_conv_w = din("ffn_conv_w", [NL, 128, 24, 3])
    ffn_conv_b = din("ffn_conv_b", [NL, 128, 24])
    ln1_g = din("ln1_g", [NL, 128, D])
    ln1_b = din("ln1_b", [NL, 128, D])
    ln2_g = din("ln2_g", [NL, 128, D])
    ln2_b = din("ln2_b", [NL, 128, D])
    c_identf = din("c_identf", [128, 128])
    c_identb = din("c_identb", [128, 128], BF16)
    c_ropec = din("c_ropec", [128, L_SEQ])
    c_ropes = din("c_ropes", [128, L_SEQ])
    c_zT = din("c_zT", [17, L_SEQ])
    c_tv = din("c_tv", [128, L_SEQ])
    c_ndelta = din("c_ndelta", [128, 4])
    c_f1 = din("c_f1", [128, 2, 128], BF16)
    c_tw = din("c_tw", [128, 2, 64])
    c_l2x = din("c_l2x", [128, 3, 128], BF16)
    c_i2x = din("c_i2x", [128, 2, 128], BF16)
    c_negm = din("c_negm", [128, 2, 2, 128], BF16)
    c_i1 = din("c_i1", [128, 2, 64], BF16)
    c_mask = din("c_mask", [128, 2, 128], BF16)

    WIN_b = dscr("WIN_b", [NL, 38, 128, 8, 128], BF16)
    WV_b = dscr("WV_b", [NL, 128, 8, 768], BF16)
    WG_b = dscr("WG_b", [NL, 24, 128, 8, 128], BF16)
    WBA_b = dscr("WBA_b", [NL, 8, 128, 4, 128], BF16)
    WBB_b = dscr("WBB_b", [NL, 8, 128, 2, 128], BF16)
    WBC_b = dscr("WBC_b", [NL, 8, 128, 4, 128], BF16)
    WO_b = dscr("WO_b", [NL, 128, 8, D], BF16)
    WUP_b = dscr("WUP_b", [NL, 48, 128, 8, 128], BF16)
    WDN_b = dscr("WDN_b", [NL, 128, 24, D], BF16)
    XA = dscr("XA", [NS, 1536, L_SEQ], F32)
    QK = dscr("QK", [NS, 2, 3, 2, 128, L_SEQ], BF16)
    VT = dscr("VT", [NS, L_SEQ, 768], BF16)
    XC = dscr("XC", [NS, 1024, L_SEQ], F32)
    X0 = dscr("X0", [NS, 512, L_SEQ], F32)
    UT = dscr("UT", [NS, 512, L_SEQ], BF16)
    UTOK = dscr("UTOK", [NS, L_SEQ, 512], BF16)
    KTOK = dscr("KTOK", [NFFT, 512], BF16)
    D1 = dscr("D1", [2, 64, 128, 512], BF16)
    KH = dscr("KH", [NL, 128, 128, 2, 512], BF16)
    D2 = dscr("D2", [2, 128, 64, 512], BF16)
    YA = dscr("YA", [NS, 512, L_SEQ], BF16)
    AO = dscr("AO", [NS, 3, L_SEQ, 260], F32)
    YB = dscr("YB", [NS, 256, L_SEQ], BF16)
    YC = dscr("YC", [NS, 512, L_SEQ], BF16)
    X1 = dscr("X1", [NS, L_SEQ, D], F32)
    GT = dscr("GT", [NS, 3 * D, L_SEQ], BF16)
    X2 = dscr("X2", [NS, L_SEQ, D], F32)
    g.HFB = dscr("HFB", [2, 512, L_SEQ], F32)
    g.ASUM = dscr("ASUM", [128, 16], F32)

    uid = [0]

    def sb(ph, name, shape, dt, n=1):
        uid[0] += 1
        return T(ph.enter_context(nc.sbuf_tensor("%s_%d" % (name, uid[0]), list(shape), dt)), n)

    def ps(ph, name, shape, dt=F32, n=1):
        uid[0] += 1
        esz = 4 if dt == F32 else 2
        per = int(np.prod(shape[1:]))
        nb = (per * esz + 2047) // 2048
        t = ph.enter_context(nc.psum_tensor("%s_%d" % (name, uid[0]), [128, nb * 2048 // esz], dt))
        ap = t[:shape[0], :per]
        if len(shape) == 3:
            ap = ap.rearrange("p (a b) -> p a b", a=shape[1])
        elif len(shape) == 4:
            ap = ap.rearrange("p (a b c) -> p a b c", a=shape[1], b=shape[2])
        return T(ap, n, psum=True)

    xT = sb(es, "xT", [128, 8, L_SEQ], BF16, n=32)
    identf = sb(es, "identf", [128, 128], F32)
    identb = sb(es, "identb", [128, 128], BF16)
    S.dma("sp", identf[:], c_identf[:, :], writes=[identf.res])
    S.dma("sp", identb[:], c_identb[:, :], writes=[identb.res])

    rr = {"i": 0}

    def evac_eng(choices=("act", "dve")):
        rr["i"] += 1
        return choices[rr["i"] % len(choices)]

    def copy_op(e, out, in_, reads, writes):
        if e == "act":
            return S.op("act", lambda E: E.activation(out=out, in_=in_, func=AF.Copy), reads=reads, writes=writes)
        return S.op(e, lambda E: E.tensor_copy(out=out, in_=in_), reads=reads, writes=writes)

    def mm(out, lhsT, rhs, start, stop, reads, pres):
        S.op("pe", lambda E: E.matmul(out, lhsT=lhsT, rhs=rhs, start=start, stop=stop),
             reads=reads, writes=[pres], pe_chain=True)

    def xt_res(t0, n):
        return xT.rs[t0 // 128:(t0 + n + 127) // 128]

    def cast_weights(l):
        with ExitStack() as ph:
            st = [sb(ph, "cst%d" % i, [128, 8, 512], F32) for i in range(2)]
            sbb = [sb(ph, "csb%d" % i, [128, 4, 8, 128], BF16) for i in range(2)]
            k = [0]

            def stat(W, dst, K, Dw):
                KC = K // 128
                Wv = W.rearrange("(kc k) d -> k kc d", k=128)
                for d0 in range(0, Dw, 512):
                    wd = min(512, Dw - d0)
                    nm = wd // 128
                    a, b = st[k[0] % 2], sbb[k[0] % 2]
                    k[0] += 1
                    S.dma("sp", a[:, :KC, :wd], Wv[:, :, d0:d0 + wd], writes=[a.res])
                    copy_op(evac_eng(("act", "dve", "pool")),
                            b[:, :nm, :KC, :], a[:, :KC, :wd].rearrange("p kc (m j) -> p m kc j", j=128),
                            [a.res], [b.res])
                    S.dma("pool", dst[d0 // 128:d0 // 128 + nm].rearrange("m k kc j -> k m kc j"),
                          b[:, :nm, :KC, :], reads=[b.res])

            def mov(W, dst, K, Dw):
                KC = K // 128
                Wv = W.rearrange("(kc k) d -> k kc d", k=128)
                per = max(1, 4096 // Dw)
                for c0 in range(0, KC, per):
                    n = min(per, KC - c0)
                    a, b = st[k[0] % 2], sbb[k[0] % 2]
                    k[0] += 1
                    av = a[:].rearrange("p a b -> p (a b)")[:, :n * Dw].rearrange("p (a b) -> p a b", b=Dw)
                    bv = b[:].rearrange("p a b c -> p (a b c)")[:, :n * Dw].rearrange("p (a b) -> p a b", b=Dw)
                    S.dma("sp", av, Wv[:, c0:c0 + n, :], writes=[a.res])
                    copy_op(evac_eng(("act", "dve", "pool")), bv, av, [a.res], [b.res])
                    S.dma("pool", dst[:, c0:c0 + n, :], bv, reads=[b.res])

            stat(w_in[l], WIN_b[l], D, D_IN)
            mov(w_in[l][:, 3072:3840], WV_b[l], D, 768)
            stat(w_gate[l], WG_b[l], D, 3 * D)
            stat(w_br_a[l], WBA_b[l], 512, D)
            stat(w_br_b[l], WBB_b[l], 256, D)
            stat(w_br_c[l], WBC_b[l], 512, D)
            mov(w_o[l], WO_b[l], D, D)
            stat(w_up[l], WUP_b[l], D, 2 * D_FF)
            mov(w_down[l], WDN_b[l], D_FF, D)
            S.barrier()

    def p1a_load_x(s, src=None):
        src = x_in[s] if src is None else src
        with ExitStack() as ph:
            xs = [sb(ph, "xs%d" % i, [128, D], F32) for i in range(3)]
            tp = [ps(ph, "tp%d" % i, [128, 8, 128], F32) for i in range(2)]
            for tt in range(32):
                a, p = xs[tt % 3], tp[tt % 2]
                S.dma("sp", a[:], src[tt * 128:(tt + 1) * 128, :], writes=[a.res])
                for kc in range(8):
                    S.op("pe", lambda E, kc=kc: E.transpose(p[:, kc, :], a[:, kc * 128:(kc + 1) * 128], identf[:]),
                         reads=[a.res, identf.res], writes=[p.res], pe_chain=True)
                copy_op(evac_eng(), xT[:, :, tt * 128:(tt + 1) * 128], p[:, :, :], [p.res], [xT.rs[tt]])
            S.barrier()

    def p1b_inproj(l, s):
        with ExitStack() as ph:
            ropec = sb(ph, "ropec", [128, L_SEQ], F32)
            ropes = sb(ph, "ropes", [128, L_SEQ], F32)
            S.dma("sp", ropec[:], c_ropec[:, :], writes=[ropec.res])
            S.dma("sp", ropes[:], c_ropes[:, :], writes=[ropes.res])
            wv = sb(ph, "wv", [128, 8, 768], BF16)
            S.dma("sp", wv[:], WV_b[l], writes=[wv.res])
            NW = 4
            wt = [sb(ph, "wt%d" % i, [128, 8, 128], BF16) for i in range(NW)]
            pp = [ps(ph, "pp%d" % i, [128, TB], F32) for i in range(4)]
            stg = [sb(ph, "stg%d" % i, [128, TB], F32) for i in range(2)]
            tmp = [sb(ph, "rtmp%d" % i, [128, TB], F32) for i in range(4)]
            qks = [sb(ph, "qks%d" % i, [128, 2, TB], BF16) for i in range(2)]
            vst = [sb(ph, "vst%d" % i, [128, 768], BF16) for i in range(2)]
            ms = [m for m in range(38) if not (24 <= m < 30)]
            cnt = {"w": 0, "p": 0, "s": 0, "q": 0, "v": 0}
            for tb in range(L_SEQ // TB):
                t0 = tb * TB
                xr = xt_res(t0, TB)
                loaded = {}

                def load(i):
                    w = wt[cnt["w"] % NW]
                    cnt["w"] += 1
                    S.dma("sp", w[:], WIN_b[l, ms[i]], writes=[w.res])
                    loaded[i] = w

                def compute(i):
                    m = ms[i]
                    w = loaded.pop(i)
                    p = pp[cnt["p"] % 4]
                    cnt["p"] += 1
                    for nt in range(TB // 512):
                        for kc in range(8):
                            mm(p[:, nt * 512:(nt + 1) * 512], w[:, kc, :], xT[:, kc, t0 + nt * 512:t0 + (nt + 1) * 512],
                               kc == 0, kc == 7, [w.res] + xr, p.res)
                    return m, p

                pend = {}
                D_PF = 2
                for i in range(len(ms) + D_PF):
                    if i < len(ms):
                        load(i)
                    j = i - D_PF
                    if j < 0:
                        continue
                    m, p = compute(j)
                    if m < 12 or m >= 30:
                        a = stg[cnt["s"] % 2]
                        cnt["s"] += 1
                        copy_op(evac_eng(), a[:], p[:], [p.res], [a.res])
                        if m < 12:
                            dst = XA[s, m * 128:(m + 1) * 128, t0:t0 + TB]
                        else:
                            dst = XC[s, (m - 30) * 128:(m - 29) * 128, t0:t0 + TB]
                        S.dma("pool", dst, a[:], reads=[a.res])
                    else:
                        jj = m - 12
                        if jj % 2 == 0:
                            pend["A"] = p
                        else:
                            pa, pb = pend.pop("A"), p
                            kind, gidx = (jj // 2) // 3, (jj // 2) % 3
                            o = qks[cnt["q"] % 2]
                            cnt["q"] += 1
                            c_, s_ = ropec[:, t0:t0 + TB], ropes[:, t0:t0 + TB]
                            t1, t2, t3, t4 = tmp
                            S.op("dve", lambda E: E.tensor_tensor(out=t1[:], in0=pa[:], in1=c_, op=ALU.mult), [pa.res, ropec.res], [t1.res])
                            S.op("dve", lambda E: E.tensor_tensor(out=t2[:], in0=pb[:], in1=s_, op=ALU.mult), [pb.res, ropes.res], [t2.res])
                            S.op("dve", lambda E: E.tensor_tensor(out=t3[:], in0=pb[:], in1=c_, op=ALU.mult), [pb.res, ropec.res], [t3.res])
                            S.op("dve", lambda E: E.tensor_tensor(out=t4[:], in0=pa[:], in1=s_, op=ALU.mult), [pa.res, ropes.res], [t4.res])
                            S.op("pool", lambda E: E.tensor_tensor(out=o[:, 0, :], in0=t1[:], in1=t2[:], op=ALU.subtract), [t1.res, t2.res], [o.res])
                            S.op("pool", lambda E: E.tensor_tensor(out=o[:, 1, :], in0=t3[:], in1=t4[:], op=ALU.add), [t3.res, t4.res], [o.res])
                            S.dma("pool", QK[s, kind, gidx].rearrange("h p t -> p h t")[:, :, t0:t0 + TB], o[:, :, :], reads=[o.res])
                for tt in range(TB // 128):
                    tk = t0 + tt * 128
                    p = pp[cnt["p"] % 4]
                    cnt["p"] += 1
                    for (c0, c1) in ((0, 512), (512, 768)):
                        for kc in range(8):
                            mm(p[:, c0:c1], xT[:, kc, tk:tk + 128], wv[:, kc, c0:c1], kc == 0, kc == 7,
                               [wv.res] + xt_res(tk, 128), p.res)
                    a = vst[cnt["v"] % 2]
                    cnt["v"] += 1
                    copy_op(evac_eng(), a[:], p[:, 0:768], [p.res], [a.res])
                    S.dma("pool", VT[s, tk:tk + 128, :], a[:], reads=[a.res])
            S.barrier()


    def to_tokmajor(ph, tiles, dst, tag):
        tp = [ps(ph, "tk_tp%s%d" % (tag, i), [128, 4, 128], BF16) for i in range(2)]
        st = [sb(ph, "tk_st%s%d" % (tag, i), [128, 512], BF16) for i in range(3)]
        for tt in range(32):
            p, a = tp[tt % 2], st[tt % 3]
            for cc in range(4):
                S.op("pe", lambda E, cc=cc: E.transpose(p[:, cc, :], tiles[cc][:, tt * 128:(tt + 1) * 128], identb[:]),
                     reads=[tiles[cc].res, identb.res], writes=[p.res], pe_chain=True)
            copy_op(evac_eng(), a[:].rearrange("p (a b) -> p a b", a=4), p[:, :, :], [p.res], [a.res])
            S.dma("sp", dst[tt * 128:(tt + 1) * 128, :], a[:], reads=[a.res])

    def pf_filter(l):
        HFB = g.HFB
        with ExitStack() as ph:
            zT = sb(ph, "zT", [17, L_SEQ], F32)
            tv = sb(ph, "tv", [128, L_SEQ], F32)
            w1 = sb(ph, "fw1", [17, 64], F32)
            w2 = sb(ph, "fw2", [64, 64], F32)
            w3 = sb(ph, "fw3", [64, 1024], F32)
            b1 = sb(ph, "fb1", [64, 1], F32)
            b2 = sb(ph, "fb2", [64, 1], F32)
            fr = sb(ph, "ffr", [64, 1], F32)
            frb1 = sb(ph, "frb1", [64, 1], F32)
            frb2 = sb(ph, "frb2", [64, 1], F32)
            b3 = sb(ph, "fb3", [128, 8], F32)
            nd = sb(ph, "fnd", [128, 4], F32)
            halfpi = sb(ph, "halfpi", [128, 1], F32)
            asum = sb(ph, "asum", [128, 16], F32)
            for (t_, src) in ((zT, c_zT), (tv, c_tv), (w1, hy_w1[l]), (w2, hy_w2[l]), (w3, hy_w3[l]), (b1, hy_b1[l]),
                              (b2, hy_b2[l]), (fr, hy_fr[l]), (b3, hy_b3[l]), (nd, c_ndelta)):
                S.dma("sp", t_[:], src, writes=[t_.res])
            S.op("dve", lambda E: E.memset(halfpi[:], math.pi / 2), [], [halfpi.res])
            S.op("dve", lambda E: E.tensor_tensor(out=frb1[:], in0=fr[:], in1=b1[:], op=ALU.mult), [fr.res, b1.res], [frb1.res])
            S.op("dve", lambda E: E.tensor_tensor(out=frb2[:], in0=fr[:], in1=b2[:], op=ALU.mult), [fr.res, b2.res], [frb2.res])
            HS = 2048
            h1 = sb(ph, "fh1", [64, HS], F32)
            h2 = sb(ph, "fh2", [64, HS], F32)
            dec = sb(ph, "fdec", [128, HS], F32)
            hk = [sb(ph, "fhk%d" % i, [128, HS], F32) for i in range(2)]
            ta = sb(ph, "fta", [64, 512], F32)
            tab = sb(ph, "ftab", [64, 512], F32)
            ts1 = sb(ph, "fts1", [64, 512], F32)
            ts2 = sb(ph, "fts2", [64, 512], F32)
            pq = [ps(ph, "fpq%d" % i, [128, 512], F32) for i in range(4)]
            pc = [0]

            def sin_layer(dst, wmat, K, rhs_t, rhs_res, hs0, frb):
                for nt in range(HS // 512):
                    p = pq[pc[0] % 4]
                    pc[0] += 1
                    c0 = nt * 512
                    off = hs0 + c0 if rhs_t is zT else c0
                    rhs = rhs_t[:K, off:off + 512]
                    mm(p[:64, :], wmat[:K, :], rhs, True, True, [wmat.res, rhs_res], p.res)
                    S.op("act", lambda E: E.activation(out=ta[:], in_=p[:64, :], func=AF.Identity, scale=fr[:], bias=frb[:]),
                         [p.res, fr.res, frb.res], [ta.res])
                    S.op("dve", lambda E: E.scalar_tensor_tensor(out=tab[:], in0=ta[:], scalar=-1.0, in1=ta[:], op0=ALU.mult, op1=ALU.max), [ta.res], [tab.res])
                    S.op("act", lambda E: E.activation(out=ts1[:], in_=ta[:], func=AF.Sin, scale=0.5), [ta.res], [ts1.res])
                    S.op("act", lambda E: E.activation(out=ts2[:], in_=tab[:], func=AF.Sin, scale=-0.5, bias=halfpi[:64, :]),
                         [tab.res, halfpi.res], [ts2.res])
                    S.op("dve", lambda E: E.scalar_tensor_tensor(out=dst[:, c0:c0 + 512], in0=ts1[:], scalar=2.0, in1=ts2[:],
                                                                 op0=ALU.mult, op1=ALU.mult), [ts1.res, ts2.res], [dst.res])

            S.op("dve", lambda E: E.memset(asum[:], 0.0), [], [asum.res])
            ki = 0
            for hs in range(2):
                hs0 = hs * HS
                sin_layer(h1, w1, 17, zT, zT.res, hs0, frb1)
                sin_layer(h2, w2, 64, h1, h1.res, hs0, frb2)
                for cc in range(4):
                    S.op("act", lambda E: E.activation(out=dec[:], in_=tv[:, hs0:hs0 + HS], func=AF.Exp, scale=nd[:, cc:cc + 1]),
                         [tv.res, nd.res], [dec.res])
                    for dirn in range(2):
                        mc = dirn * 4 + cc
                        o = hk[ki % 2]
                        ki += 1
                        for nt in range(HS // 512):
                            p = pq[pc[0] % 4]
                            pc[0] += 1
                            c0 = nt * 512
                            mm(p[:, :], w3[:, mc * 128:(mc + 1) * 128], h2[:, c0:c0 + 512], True, True, [w3.res, h2.res], p.res)
                            S.op("dve", lambda E: E.scalar_tensor_tensor(out=o[:, c0:c0 + 512], in0=p[:, :], scalar=b3[:, mc:mc + 1],
                                                                         in1=dec[:, c0:c0 + 512], op0=ALU.add, op1=ALU.mult),
                                 [p.res, b3.res, dec.res], [o.res])
                        if dirn == 1 and hs == 0:
                            S.op("dve", lambda E: E.memset(o[:, 0:1], 0.0), [], [o.res])
                        col = mc * 2 + hs
                        S.op("dve", lambda E: E.tensor_reduce(out=asum[:, col:col + 1], in_=o[:], axis=AX.X, op=ALU.add,
                                                              apply_absolute_value=True), [o.res], [asum.res])
                        S.dma("pool", HFB[dirn, cc * 128:(cc + 1) * 128, hs0:hs0 + HS], o[:], reads=[o.res])
            S.dma("pool", g.ASUM[:, :], asum[:], reads=[asum.res])
            S.barrier()
        with ExitStack() as ph:
            asum2 = sb(ph, "asum2", [128, 16], F32)
            nrm = sb(ph, "fnrm", [128, 4], F32)
            rinv = sb(ph, "frinv", [128, 4], F32)
            S.dma("sp", asum2[:], g.ASUM[:, :], writes=[asum2.res])
            for cc in range(4):
                S.op("dve", lambda E, cc=cc: E.tensor_tensor(out=nrm[:, cc:cc + 1], in0=asum2[:, 2 * cc:2 * cc + 1],
                                                             in1=asum2[:, 2 * cc + 1:2 * cc + 2], op=ALU.add), [asum2.res], [nrm.res])
                for extra in (2 * (4 + cc), 2 * (4 + cc) + 1):
                    S.op("dve", lambda E, cc=cc, extra=extra: E.tensor_tensor(out=nrm[:, cc:cc + 1], in0=nrm[:, cc:cc + 1],
                                                                              in1=asum2[:, extra:extra + 1], op=ALU.add),
                         [asum2.res, nrm.res], [nrm.res])
            S.op("dve", lambda E: E.reciprocal(out=rinv[:], in_=nrm[:]), [nrm.res], [rinv.res])
            hb = [sb(ph, "fhb%d" % i, [128, L_SEQ], F32) for i in range(2)]
            kb = [sb(ph, "fkb%d" % i, [128, L_SEQ], BF16) for i in range(4)]
            for dirn in range(2):
                for cc in range(4):
                    a = hb[cc % 2]
                    S.dma("sp", a[:], HFB[dirn, cc * 128:(cc + 1) * 128, :], writes=[a.res])
                    o = kb[cc]
                    if dirn == 0:
                        S.op("dve", lambda E: E.tensor_scalar(out=o[:], in0=a[:], scalar1=rinv[:, cc:cc + 1], scalar2=None, op0=ALU.mult),
                             [a.res, rinv.res], [o.res])
                    else:
                        S.op("dve", lambda E: E.memset(o[:, 0:1], 0.0), [], [o.res])
                        S.op("dve", lambda E: E.tensor_scalar(out=o[:, 1:L_SEQ], in0=a[:, L_SEQ - 1:0:-1], scalar1=rinv[:, cc:cc + 1],
                                                              scalar2=None, op0=ALU.mult), [a.res, rinv.res], [o.res])
                to_tokmajor(ph, kb, KTOK[dirn * L_SEQ:(dirn + 1) * L_SEQ, :], "f%d" % dirn)
            S.barrier()

    def fft_stage1(src, NR):
        with ExitStack() as ph:
            f1m = sb(ph, "f1m", [128, 2, 128], BF16)
            tw = sb(ph, "tw", [128, 2, 64], F32)
            S.dma("sp", f1m[:], c_f1[:, :, :], writes=[f1m.res])
            S.dma("sp", tw[:], c_tw[:, :, :], writes=[tw.res])
            NC2 = 8
            uh = [sb(ph, "uh%d" % i, [128, NC2, 512], BF16) for i in range(2)]
            pq = [ps(ph, "s1p%d" % i, [128, 512], F32) for i in range(6)]
            t1 = [sb(ph, "s1t1%d" % i, [128, 512], F32) for i in range(2)]
            t2 = [sb(ph, "s1t2%d" % i, [128, 512], F32) for i in range(2)]
            oo = [sb(ph, "s1o%d" % i, [128, 2, 512], BF16) for i in range(3)]
            srcv = src.rearrange("(a b) c -> a b c", b=64)
            pc = 0
            for ch in range(64 // NC2):
                u = uh[ch % 2]
                S.dma("sp", u[:NR], srcv[:, ch * NC2:(ch + 1) * NC2, :], writes=[u.res])
                for j in range(NC2):
                    n2 = ch * NC2 + j
                    pr, pi = pq[pc % 6], pq[(pc + 1) % 6]
                    pc += 2
                    mm(pr[:, :], f1m[:NR, 0, :], u[:NR, j, :], True, True, [f1m.res, u.res], pr.res)
                    mm(pi[:, :], f1m[:NR, 1, :], u[:NR, j, :], True, True, [f1m.res, u.res], pi.res)
                    a1, a2, o = t1[n2 % 2], t2[n2 % 2], oo[n2 % 3]
                    Tr, Ti = tw[:, 0, n2:n2 + 1], tw[:, 1, n2:n2 + 1]
                    S.op("act", lambda E: E.activation(out=a1[:], in_=pi[:, :], func=AF.Identity, scale=Ti), [pi.res, tw.res], [a1.res])
                    S.op("act", lambda E: E.activation(out=a2[:], in_=pr[:, :], func=AF.Identity, scale=Ti), [pr.res, tw.res], [a2.res])
                    S.op("dve", lambda E: E.scalar_tensor_tensor(out=o[:, 0, :], in0=pr[:, :], scalar=Tr, in1=a1[:], op0=ALU.mult, op1=ALU.add),
                         [pr.res, tw.res, a1.res], [o.res])
                    S.op("dve", lambda E: E.scalar_tensor_tensor(out=o[:, 1, :], in0=pi[:, :], scalar=Tr, in1=a2[:], op0=ALU.mult, op1=ALU.subtract),
                         [pi.res, tw.res, a2.res], [o.res])
                    S.dma("pool", D1[:, n2, :, :].rearrange("r f c -> f r c"), o[:, :, :], reads=[o.res])
            S.barrier()

    def fft_stage2(mode, l):
        with ExitStack() as ph:
            l2 = sb(ph, "l2m", [128, 3, 128], BF16)
            S.dma("sp", l2[:], c_l2x[:, :, :], writes=[l2.res])
            FC = 4
            ain = [sb(ph, "s2a%d" % i, [128, FC, 512], BF16) for i in range(3)]
            d1v = D1.rearrange("r n f c -> (r n) f c")
            if mode == "filter":
                pU = [ps(ph, "s2pu%d" % i, [128, 2, 512], F32) for i in range(3)]
                st = [sb(ph, "s2st%d" % i, [128, 2, 512], BF16) for i in range(3)]
            else:
                pU = [ps(ph, "s2pu%d" % i, [128, 512], F32) for i in range(4)]
                pB = [ps(ph, "s2pb%d" % i, [128, 512], F32) for i in range(4)]
                i2 = sb(ph, "i2m", [128, 2, 128], BF16)
                S.dma("sp", i2[:], c_i2x[:, :, :], writes=[i2.res])
                kh = [sb(ph, "s2kh%d" % i, [128, FC, 2, 512], BF16) for i in range(3)]
                p1 = [sb(ph, "s2p1%d" % i, [128, 512], BF16) for i in range(4)]
                p2 = [sb(ph, "s2p2%d" % i, [128, 512], BF16) for i in range(4)]
                oo = [sb(ph, "s2o%d" % i, [128, FC, 512], BF16) for i in range(3)]
            pendB = []

            def flushB(f1, j, q1, q2, o, ch):
                pb = pB[f1 % 4]
                mm(pb[:, :], i2[:, 0, :], q1[:], True, False, [i2.res, q1.res], pb.res)
                mm(pb[:, :], i2[:, 1, :], q2[:], False, True, [i2.res, q2.res], pb.res)
                copy_op("act", o[:, j, :], pb[:, :], [pb.res], [o.res])
                if j == FC - 1:
                    for r_ in range(2):
                        S.dma("pool", D2[r_, ch * FC:(ch + 1) * FC, :, :].rearrange("f t c -> t f c"), o[64 * r_:64 * r_ + 64, :, :], reads=[o.res])

            for ch in range(128 // FC):
                a = ain[ch % 3]
                S.dma("sp", a[:], d1v[:, ch * FC:(ch + 1) * FC, :], writes=[a.res])
                if mode != "filter":
                    k = kh[ch % 3]
                    S.dma("sp", k[:].rearrange("p f a c -> p f (a c)"),
                          KH[l, ch * FC:(ch + 1) * FC].rearrange("f p a c -> p f (a c)"), writes=[k.res])
                    o = oo[ch % 3]
                for j in range(FC):
                    f1 = ch * FC + j
                    if mode == "filter":
                        pu = pU[f1 % 3]
                        mm(pu[:, 0, :], l2[:, 1, :], a[:, j, :], True, True, [l2.res, a.res], pu.res)
                        mm(pu[:, 1, :], l2[:, 2, :], a[:, j, :], True, True, [l2.res, a.res], pu.res)
                        t_ = st[f1 % 3]
                        copy_op(evac_eng(), t_[:], pu[:, :, :], [pu.res], [t_.res])
                        S.dma("pool", KH[l, f1].rearrange("p a c -> p (a c)"), t_[:].rearrange("p a c -> p (a c)"), reads=[t_.res])
                        continue
                    pu = pU[f1 % 4]
                    q1, q2 = p1[f1 % 4], p2[f1 % 4]
                    mm(pu[:, :], l2[:, 0, :], a[:, j, :], True, True, [l2.res, a.res], pu.res)
                    S.op("dve", lambda E: E.tensor_tensor(out=q1[:], in0=pu[:, :], in1=k[:, j, 0, :], op=ALU.mult), [pu.res, k.res], [q1.res])
                    S.op("dve", lambda E: E.tensor_tensor(out=q2[:], in0=pu[:, :], in1=k[:, j, 1, :], op=ALU.mult), [pu.res, k.res], [q2.res])
                    if pendB:
                        flushB(*pendB.pop())
                    pendB.append((f1, j, q1, q2, o, ch))
            if mode != "filter" and pendB:
                flushB(*pendB.pop())
            S.barrier()

    def conv3_rows(raw, out, w_t, b_ap, wcol, n):
        S.op("pool", lambda E: E.tensor_scalar(out=out[:, :n], in0=raw[:, 1:n + 1], scalar1=w_t[:, wcol, 1:2], scalar2=b_ap,
                                               op0=ALU.mult, op1=ALU.add), [raw.res, w_t.res], [out.res])
        S.op("dve", lambda E: E.scalar_tensor_tensor(out=out[:, :n], in0=raw[:, 0:n], scalar=w_t[:, wcol, 0:1], in1=out[:, :n],
                                                     op0=ALU.mult, op1=ALU.add), [raw.res, w_t.res, out.res], [out.res])
        S.op("dve", lambda E: E.scalar_tensor_tensor(out=out[:, :n], in0=raw[:, 2:n + 2], scalar=w_t[:, wcol, 2:3], in1=out[:, :n],
                                                     op0=ALU.mult, op1=ALU.add), [raw.res, w_t.res, out.res], [out.res])

    def load_halo(raw, src_rows, t0, n):
        lo, hi = max(t0 - 1, 0), min(t0 + n + 1, L_SEQ)
        wr = [raw.res]
        if t0 == 0:
            S.op("pool", lambda E: E.memset(raw[:, 0:1], 0.0), [], wr)
        if t0 + n >= L_SEQ:
            S.op("pool", lambda E: E.memset(raw[:, n + 1:n + 2], 0.0), [], wr)
        S.dma("sp", raw[:, lo - (t0 - 1):hi - (t0 - 1)], src_rows[:, lo:hi], writes=wr)

    def p2a_hyconv(l, s):
        with ExitStack() as ph:
            cw = sb(ph, "hcw", [128, 12, 3], F32)
            cb = sb(ph, "hcb", [128, 12], F32)
            S.dma("sp", cw[:], hy_conv_w[l], writes=[cw.res])
            S.dma("sp", cb[:], hy_conv_b[l], writes=[cb.res])
            HS = 2048
            raw = [sb(ph, "hraw%d" % i, [128, HS + 2], F32) for i in range(3)]
            cv = [sb(ph, "hcv%d" % i, [128, HS], F32) for i in range(3)]
            ut = [sb(ph, "hut%d" % i, [128, L_SEQ], BF16) for i in range(4)]
            ri = 0
            bgt = sb(ph, "hbg", [128, 24], F32)
            S.dma("sp", bgt[:], b_gate[l], writes=[bgt.res])
            gwt = [sb(ph, "hgw%d" % i, [128, 8, 128], BF16) for i in range(4)]
            gpp = [ps(ph, "hgp%d" % i, [128, TB], F32) for i in range(3)]
            gst = [sb(ph, "hgs%d" % i, [128, TB], BF16) for i in range(2)]
            gitems = [(tb, m) for tb in range(L_SEQ // TB) for m in range(24)]
            gstate = {"next": 0, "loaded": {}, "lw": 0}

            def gate_load(i):
                if i >= len(gitems) or i in gstate["loaded"]:
                    return
                w = gwt[gstate["lw"] % 4]
                gstate["lw"] += 1
                S.dma("sp", w[:], WG_b[l, gitems[i][1]], writes=[w.res])
                gstate["loaded"][i] = w

            def gate_items(n):
                for _ in range(n):
                    i = gstate["next"]
                    if i >= len(gitems):
                        return
                    gstate["next"] += 1
                    gate_load(i)
                    gate_load(i + 1)
                    gate_load(i + 2)
                    tb, m = gitems[i]
                    w = gstate["loaded"].pop(i)
                    t0g = tb * TB
                    p = gpp[i % 3]
                    for nt in range(TB // 512):
                        for kc in range(8):
                            mm(p[:, nt * 512:(nt + 1) * 512], w[:, kc, :], xT[:, kc, t0g + nt * 512:t0g + (nt + 1) * 512],
                               kc == 0, kc == 7, [w.res] + xt_res(t0g, TB), p.res)
                    o = gst[i % 2]
                    S.op("act", lambda E: E.activation(out=o[:], in_=p[:, :], func=AF.Sigmoid, bias=bgt[:, m:m + 1]), [p.res, bgt.res], [o.res])
                    S.dma("act", GT[s, m * 128:(m + 1) * 128, t0g:t0g + TB], o[:], reads=[o.res])

            for cc in range(4):
                for hs in range(2):
                    t0 = hs * HS
                    outs = []
                    for part in range(3):
                        ch = part * 4 + cc
                        r_, c_ = raw[ri % 3], cv[ri % 3]
                        ri += 1
                        load_halo(r_, XA[s, ch * 128:(ch + 1) * 128, :], t0, HS)
                        conv3_rows(r_, c_, cw, cb[:, ch:ch + 1], ch, HS)
                        outs.append(c_)
                        gate_items(4)
                    S.dma("pool", X0[s, cc * 128:(cc + 1) * 128, t0:t0 + HS], outs[0][:], reads=[outs[0].res])
                    S.op("pool", lambda E: E.tensor_tensor(out=ut[cc][:, t0:t0 + HS], in0=outs[1][:], in1=outs[2][:], op=ALU.mult),
                         [outs[1].res, outs[2].res], [ut[cc].res])
                S.dma("pool", UT[s, cc * 128:(cc + 1) * 128, :], ut[cc][:], reads=[ut[cc].res])
            gate_items(len(gitems))
            to_tokmajor(ph, ut, UTOK[s], "u")
            S.barrier()

    def p2d_hyout(l, s):
        with ExitStack() as ph:
            i1 = sb(ph, "i1m", [128, 2, 64], BF16)
            hb_ = sb(ph, "hbias", [128, 4], F32)
            S.dma("sp", i1[:], c_i1[:, :, :], writes=[i1.res])
            S.dma("sp", hb_[:], hy_bias[l], writes=[hb_.res])
            zT_ = sb(ph, "zTt", [128, 4, L_SEQ], BF16)
            TC = 4
            bin_ = [sb(ph, "bin%d" % i, [128, 2, TC, 512], BF16) for i in range(2)]
            d2v = D2.rearrange("r f t c -> f r t c")
            py = [ps(ph, "p2dy%d" % i, [64, 512], F32) for i in range(3)]
            pz = [ps(ph, "p2dz%d" % i, [128, 4, 64], BF16) for i in range(3)]
            ysb = [sb(ph, "ysb%d" % i, [64, 512], BF16) for i in range(3)]
            tw = sb(ph, "twd", [128, 2, 64], F32)
            S.dma("sp", tw[:], c_tw[:, :, :], writes=[tw.res])
            tw1 = [sb(ph, "tw1%d" % i, [128, 512], F32) for i in range(2)]
            tw2 = [sb(ph, "tw2%d" % i, [128, 512], F32) for i in range(2)]
            btw = [sb(ph, "btw%d" % i, [128, 2, 512], BF16) for i in range(4)]
            pendz = []
            pendm = []

            def flush_m(bt, t2):
                p, y, z = py[t2 % 3], ysb[t2 % 3], pz[t2 % 3]
                mm(p[:, :], i1[:, 0, :], bt[:, 0, :], True, False, [i1.res, bt.res], p.res)
                mm(p[:, :], i1[:, 1, :], bt[:, 1, :], False, True, [i1.res, bt.res], p.res)
                copy_op("act", y[:], p[:, :], [p.res], [y.res])
                if pendz:
                    flush_z(*pendz.pop())
                pendz.append((y, z, t2))

            def flush_z(y, z, t2):
                for cc in range(4):
                    S.op("pe", lambda E, cc=cc: E.transpose(z[:, cc, :], y[:, cc * 128:(cc + 1) * 128], identb[:64, :64]),
                         reads=[y.res, identb.res], writes=[z.res], pe_chain=True)
                copy_op("dve", zT_[:, :, t2:L_SEQ:64], z[:, :, :], [z.res], [zT_.res])

            for ch in range(64 // TC):
                b = bin_[ch % 2]
                S.dma("sp", b[:], d2v[:, :, ch * TC:(ch + 1) * TC, :], writes=[b.res])
                for j in range(TC):
                    t2 = ch * TC + j
                    a1, a2, bt = tw1[t2 % 2], tw2[t2 % 2], btw[t2 % 4]
                    Tr, Ti = tw[:, 0, t2:t2 + 1], tw[:, 1, t2:t2 + 1]
                    S.op("act", lambda E: E.activation(out=a1[:], in_=b[:, 1, j, :], func=AF.Identity, scale=Ti), [b.res, tw.res], [a1.res])
                    S.op("act", lambda E: E.activation(out=a2[:], in_=b[:, 0, j, :], func=AF.Identity, scale=Ti), [b.res, tw.res], [a2.res])
                    S.op("dve", lambda E: E.scalar_tensor_tensor(out=bt[:, 0, :], in0=b[:, 0, j, :], scalar=Tr, in1=a1[:], op0=ALU.mult, op1=ALU.subtract),
                         [b.res, tw.res, a1.res], [bt.res])
                    S.op("dve", lambda E: E.scalar_tensor_tensor(out=bt[:, 1, :], in0=b[:, 1, j, :], scalar=Tr, in1=a2[:], op0=ALU.mult, op1=ALU.add),
                         [b.res, tw.res, a2.res], [bt.res])
                    if pendm:
                        flush_m(*pendm.pop())
                    pendm.append((bt, t2))
            if pendm:
                flush_m(*pendm.pop())
            if pendz:
                flush_z(*pendz.pop())
            HS = 1024
            utl = [sb(ph, "utl%d" % i, [128, HS], BF16) for i in range(2)]
            x0l = [sb(ph, "x0l%d" % i, [128, HS], F32) for i in range(2)]
            tm = [sb(ph, "ytm%d" % i, [128, HS], F32) for i in range(2)]
            yo = [sb(ph, "yo%d" % i, [128, HS], BF16) for i in range(2)]
            k = 0
            for cc in range(4):
                for hs in range(L_SEQ // HS):
                    t0 = hs * HS
                    u_, x_, t_, o_ = utl[k % 2], x0l[k % 2], tm[k % 2], yo[k % 2]
                    k += 1
                    S.dma("sp", u_[:], UT[s, cc * 128:(cc + 1) * 128, t0:t0 + HS], writes=[u_.res])
                    S.dma("sp", x_[:], X0[s, cc * 128:(cc + 1) * 128, t0:t0 + HS], writes=[x_.res])
                    S.op("dve", lambda E: E.scalar_tensor_tensor(out=t_[:], in0=u_[:], scalar=hb_[:, cc:cc + 1], in1=zT_[:, cc, t0:t0 + HS],
                                                                 op0=ALU.mult, op1=ALU.add), [u_.res, hb_.res, zT_.res], [t_.res])
                    S.op("pool", lambda E: E.tensor_tensor(out=o_[:], in0=t_[:], in1=x_[:], op=ALU.mult), [t_.res, x_.res], [o_.res])
                    S.dma("pool", YA[s, cc * 128:(cc + 1) * 128, t0:t0 + HS], o_[:], reads=[o_.res])
            S.barrier()


    def p3_attention(l, s):
        with ExitStack() as ph:
            mask = sb(ph, "amask", [128, 2, 128], BF16)
            S.dma("sp", mask[:], c_mask[:, :, :], writes=[mask.res])
            raw = [sb(ph, "araw%d" % i, [64, L_SEQ], BF16) for i in range(2)]
            Qs = [sb(ph, "aQ%d" % i, [64, L_SEQ], BF16) for i in range(4)]
            Ks = [sb(ph, "aK%d" % i, [64, L_SEQ + 128 * 16], BF16) for i in range(4)]
            NV = 8
            vraw = [sb(ph, "avr%d" % i, [128, 256], BF16) for i in range(NV)]
            vint = [sb(ph, "avx%d" % i, [128, 4, 65], BF16) for i in range(NV)]
            vfirsts = [sb(ph, "avf%d" % i, [128, 4, 65], BF16) for i in range(4)]
            vlasts = [sb(ph, "avl%d" % i, [128, 4, 65], BF16) for i in range(4)]
            for v_ in vint + vfirsts + vlasts:
                S.op("pool", lambda E: E.memset(v_[:], 1.0), [], [v_.res])
            for v_ in vfirsts:
                S.op("pool", lambda E: E.memset(v_[0:64], 0.0), [], [v_.res])
            for v_ in vlasts:
                S.op("pool", lambda E: E.memset(v_[64:128], 0.0), [], [v_.res])
            negm = sb(ph, "anegm", [128, 2, 2, 128], BF16)
            S.dma("sp", negm[:], c_negm[:, :, :, :], writes=[negm.res])
            psc = [ps(ph, "asc%d" % i, [128, 4, 2, 128], F32) for i in range(3)]
            ppo = [ps(ph, "apo%d" % i, [128, 4, 65], F32) for i in range(2)]
            pe_ = [sb(ph, "ape%d" % i, [128, 4, 2, 128], BF16) for i in range(3)]
            aos = [sb(ph, "aos%d" % i, [128, 260], F32) for i in range(3)]
            cn = {"raw": 0, "v": 0, "vi": 0, "vf": 0, "vl": 0, "sc": 0, "po": 0, "ao": 0}
            for gi, d in enumerate((1, 4, 16)):
                n = L_SEQ // d
                W = n + 128
                for h in range(4):
                    for kind, dstt in ((0, Qs[h]), (1, Ks[h])):
                        r_ = raw[cn["raw"] % 2]
                        cn["raw"] += 1
                        S.dma("sp", r_[0:32, :], QK[s, kind, gi, 0, 32 * h:32 * h + 32, :], writes=[r_.res])
                        S.dma("sp", r_[32:64, :], QK[s, kind, gi, 1, 32 * h:32 * h + 32, :], writes=[r_.res])
                        src = r_[:, :].rearrange("p (i r) -> p r i", r=d)
                        if kind == 0:
                            dv = dstt[:, :].rearrange("p (r i) -> p r i", r=d)
                            copy_op(evac_eng(("dve", "act")), dv, src, [r_.res], [dstt.res])
                        else:
                            kv = dstt[:, :d * W].rearrange("p (r w) -> p r w", r=d)
                            S.op("pool", lambda E: E.memset(kv[:, :, 0:64], 0.0), [], [dstt.res])
                            S.op("pool", lambda E: E.memset(kv[:, :, 64 + n:W], 0.0), [], [dstt.res])
                            copy_op(evac_eng(("dve", "act")), kv[:, :, 64:64 + n], src, [r_.res], [dstt.res])
                nqb = n // 128
                blocks = [(r, qb) for r in range(d) for qb in range(nqb)]
                vt = {}
                st1 = {}

                def get_v(r, m):
                    if (r, m) in vt:
                        return vt[(r, m)]
                    vr = vraw[cn["v"] % NV]
                    if m == 0:
                        ve, p0, p1 = vfirsts[cn["vf"] % 4], 64, 128
                        cn["vf"] += 1
                    elif m == nqb:
                        ve, p0, p1 = vlasts[cn["vl"] % 4], 0, 64
                        cn["vl"] += 1
                    else:
                        ve, p0, p1 = vint[cn["vi"] % NV], 0, 128
                        cn["vi"] += 1
                    cn["v"] += 1
                    j0 = 128 * m - 64 + p0
                    npos = p1 - p0
                    pos0 = r + d * j0
                    srcv = VT[s, pos0:pos0 + d * (npos - 1) + 1:d, gi * 256:(gi + 1) * 256]
                    S.dma("sp", vr[p0:p1, :], srcv, writes=[vr.res])
                    S.op("pool", lambda E: E.tensor_copy(out=ve[p0:p1, :, 0:64], in_=vr[p0:p1, :].rearrange("p (h e) -> p h e", h=4)),
                         [vr.res], [ve.res])
                    vt[(r, m)] = ve
                    return ve

                def prefetch_v(i):
                    if i < len(blocks):
                        r_, qb_ = blocks[i]
                        get_v(r_, qb_)
                        get_v(r_, qb_ + 1)

                prefetch_v(0)
                prefetch_v(1)

                def stage1(i):
                    r, qb = blocks[i]
                    prefetch_v(i + 2)
                    va, vb = get_v(r, qb), get_v(r, qb + 1)
                    for key in [k_ for k_ in vt if (k_[0] < r) or (k_[0] == r and k_[1] < qb)]:
                        vt.pop(key)
                    sc = psc[cn["sc"] % 3]
                    pe1 = pe_[cn["sc"] % 3]
                    cn["sc"] += 1
                    for hp in range(2):
                        mm(sc[:, 2 * hp:2 * hp + 2, :, :], identb[:], negm[:, :, :, :], True, False, [identb.res, negm.res], sc.res)
                        for h in (2 * hp, 2 * hp + 1):
                            kv = Ks[h][:, :d * W].rearrange("p (r w) -> p r w", r=d)
                            qv = Qs[h][:, :].rearrange("p (r i) -> p r i", r=d)
                            q_ap = qv[:, r, 128 * qb:128 * qb + 128]
                            mm(sc[:, h, 0, :], kv[:, r, 128 * qb:128 * qb + 128], q_ap, False, False, [Ks[h].res, Qs[h].res], sc.res)
                            mm(sc[:, h, 1, :], kv[:, r, 128 * qb + 128:128 * qb + 256], q_ap, False, h == 2 * hp + 1, [Ks[h].res, Qs[h].res], sc.res)
                    S.op("act", lambda E: E.activation(out=pe1[:], in_=sc[:, :, :, :], func=AF.Exp, scale=0.125), [sc.res], [pe1.res])
                    st1[i] = (pe1, va, vb)

                def stage2(i):
                    r, qb = blocks[i]
                    pe1, va, vb = st1.pop(i)
                    po = ppo[cn["po"] % 2]
                    cn["po"] += 1
                    for h in range(4):
                        mm(po[:, h, :], pe1[:, h, 0, :], va[:, h, :], True, False, [pe1.res, va.res], po.res)
                        mm(po[:, h, :], pe1[:, h, 1, :], vb[:, h, :], False, True, [pe1.res, vb.res], po.res)
                    a = aos[cn["ao"] % 3]
                    cn["ao"] += 1
                    copy_op("dve", a[:].rearrange("p (h e) -> p h e", h=4), po[:, :, :], [po.res], [a.res])
                    pos0 = r + d * 128 * qb
                    S.dma("sp", AO[s, gi, pos0:pos0 + d * 127 + 1:d, :], a[:], reads=[a.res])

                for i in range(len(blocks) + 1):
                    if i < len(blocks):
                        stage1(i)
                    if i >= 1:
                        stage2(i - 1)
            S.barrier()

    def p3b_attn_merge(l, s):
        with ExitStack() as ph:
            a3 = [sb(ph, "m3a%d" % i, [128, 3, 260], F32) for i in range(2)]
            acc = [sb(ph, "m3acc%d" % i, [128, 260], F32) for i in range(2)]
            rd = [sb(ph, "m3rd%d" % i, [128, 4], F32) for i in range(2)]
            yb = [sb(ph, "m3yb%d" % i, [128, 256], BF16) for i in range(2)]
            pt = [ps(ph, "m3pt%d" % i, [128, 2, 128], BF16) for i in range(2)]
            ybT = [sb(ph, "m3ybT%d" % i, [128, 2, 1024], BF16) for i in range(2)]
            for tt in range(32):
                a, c, r_, y, p = a3[tt % 2], acc[tt % 2], rd[tt % 2], yb[tt % 2], pt[tt % 2]
                o = ybT[(tt // 8) % 2]
                S.dma("sp", a[:], AO[s, :, tt * 128:(tt + 1) * 128, :].rearrange("g t c -> t g c"), writes=[a.res])
                S.op("dve", lambda E: E.tensor_tensor(out=c[:], in0=a[:, 0, :], in1=a[:, 1, :], op=ALU.add), [a.res], [c.res])
                S.op("dve", lambda E: E.tensor_tensor(out=c[:], in0=c[:], in1=a[:, 2, :], op=ALU.add), [a.res, c.res], [c.res])
                cv = c[:].rearrange("p (h e) -> p h e", h=4)
                S.op("dve", lambda E: E.reciprocal(out=r_[:], in_=cv[:, :, 64]), [c.res], [r_.res])
                for h in range(4):
                    S.op("act", lambda E, h=h: E.activation(out=y[:, h * 64:(h + 1) * 64], in_=cv[:, h, 0:64], func=AF.Identity,
                                                            scale=r_[:, h:h + 1]), [c.res, r_.res], [y.res])
                for j in range(2):
                    S.op("pe", lambda E, j=j: E.transpose(p[:, j, :], y[:, j * 128:(j + 1) * 128], identb[:]),
                         reads=[y.res, identb.res], writes=[p.res], pe_chain=True)
                copy_op("act", o[:, :, (tt % 8) * 128:(tt % 8 + 1) * 128], p[:, :, :], [p.res], [o.res])
                if tt % 8 == 7:
                    t0 = (tt // 8) * 1024
                    S.dma("pool", YB[s].rearrange("(a p) t -> p a t", p=128)[:, :, t0:t0 + 1024], o[:, :, :], reads=[o.res])
            S.barrier()

    def p4_rglru(l, s):
        with ExitStack() as ph:
            cw = sb(ph, "rcw", [128, 4, 4], F32)
            cb = sb(ph, "rcb", [128, 4], F32)
            gb = sb(ph, "rgb", [128, 2, 2, 4], F32)
            lam = sb(ph, "rlam", [128, 8], F32)
            c8 = sb(ph, "rc8", [128, 8], F32)
            c16 = sb(ph, "rc16", [128, 8], F32)
            one = sb(ph, "rone", [128, 1], F32)
            S.dma("sp", cw[:], rg_conv_w[l], writes=[cw.res])
            S.dma("sp", cb[:], rg_conv_b[l], writes=[cb.res])
            S.dma("sp", gb[:], rg_gate_b[l], writes=[gb.res])
            S.dma("sp", lam[:], rg_lam[l].rearrange("p a b -> p (a b)"), writes=[lam.res])
            S.op("dve", lambda E: E.memset(one[:], 1.0), [], [one.res])
            S.op("act", lambda E: E.activation(out=c8[:], in_=lam[:], func=AF.Exp, scale=-1.0), [lam.res], [c8.res])
            S.op("act", lambda E: E.activation(out=c8[:], in_=c8[:], func=AF.Ln, bias=one[:]), [c8.res, one.res], [c8.res])
            S.op("dve", lambda E: E.tensor_scalar(out=c16[:], in0=c8[:], scalar1=-16.0, scalar2=None, op0=ALU.mult), [c8.res], [c16.res])
            S.op("dve", lambda E: E.tensor_scalar(out=c8[:], in0=c8[:], scalar1=-8.0, scalar2=None, op0=ALU.mult), [c8.res, c16.res], [c8.res])
            gw = [sb(ph, "rgw%d" % i, [128, 128], F32) for i in range(4)]
            raws = [sb(ph, "rraw%d" % i, [128, L_SEQ + 3], F32) for i in range(2)]
            xrs = [sb(ph, "rxr%d" % i, [128, L_SEQ], F32) for i in range(2)]
            xrb = sb(ph, "rxrb", [128, L_SEQ], BF16)
            a_t = sb(ph, "rat", [128, L_SEQ], F32)
            xn = sb(ph, "rxn", [128, L_SEQ], F32)
            hf = sb(ph, "rhf", [128, L_SEQ], F32)
            CH = 1024
            r_t = sb(ph, "rrt", [128, L_SEQ], F32)
            gwb = [sb(ph, "rgwb%d" % i, [128, 128], BF16) for i in range(4)]
            pg = [ps(ph, "rpg%d" % i, [128, CH], F32) for i in range(4)]
            pc = 0
            yo_view = xn[:].bitcast(BF16)[:, 0:L_SEQ]

            def conv(cc):
                raw, xr = raws[cc % 2], xrs[cc % 2]
                S.op("pool", lambda E: E.memset(raw[:, 0:2], 0.0), [], [raw.res])
                S.op("pool", lambda E: E.memset(raw[:, L_SEQ + 2:L_SEQ + 3], 0.0), [], [raw.res])
                S.dma("sp", raw[:, 2:L_SEQ + 2], XC[s, cc * 128:(cc + 1) * 128, :], writes=[raw.res])
                S.op("pool", lambda E: E.tensor_scalar(out=xr[:], in0=raw[:, 2:L_SEQ + 2], scalar1=cw[:, cc, 2:3], scalar2=cb[:, cc:cc + 1],
                                                       op0=ALU.mult, op1=ALU.add), [raw.res, cw.res, cb.res], [xr.res])
                for k in (0, 1, 3):
                    S.op("dve", lambda E, k=k: E.scalar_tensor_tensor(out=xr[:], in0=raw[:, k:k + L_SEQ], scalar=cw[:, cc, k:k + 1], in1=xr[:],
                                                                      op0=ALU.mult, op1=ALU.add), [raw.res, cw.res, xr.res], [xr.res])

            conv(0)
            copy_op("act", xrb[:], xrs[0][:], [xrs[0].res], [xrb.res])
            for cc in range(4):
                raw, xr = raws[cc % 2], xrs[cc % 2]
                for dirn in range(2):
                    for gate in range(2):
                        S.dma("sp", gw[dirn * 2 + gate][:], rg_gate_w[l, dirn, gate, cc], writes=[gw[dirn * 2 + gate].res])
                        copy_op("dve", gwb[dirn * 2 + gate][:], gw[dirn * 2 + gate][:], [gw[dirn * 2 + gate].res], [gwb[dirn * 2 + gate].res])
                if cc + 1 < 4:
                    conv(cc + 1)
                for dirn in range(2):
                    idx = dirn * 4 + cc
                    for c0 in range(0, L_SEQ, CH):
                        pr_, pi_ = pg[pc % 4], pg[(pc + 1) % 4]
                        pc += 2
                        for nt in range(CH // 512):
                            sl = slice(c0 + nt * 512, c0 + (nt + 1) * 512)
                            mm(pr_[:, nt * 512:(nt + 1) * 512], gwb[dirn * 2][:], xrb[:, sl], True, True, [gwb[dirn * 2].res, xrb.res], pr_.res)
                            mm(pi_[:, nt * 512:(nt + 1) * 512], gwb[dirn * 2 + 1][:], xrb[:, sl], True, True, [gwb[dirn * 2 + 1].res, xrb.res], pi_.res)
                        S.op("act", lambda E: E.activation(out=r_t[:, c0:c0 + CH], in_=pr_[:], func=AF.Sigmoid, bias=gb[:, dirn, 0, cc:cc + 1]), [pr_.res, gb.res], [r_t.res])
                        S.op("act", lambda E: E.activation(out=xn[:, c0:c0 + CH], in_=pi_[:], func=AF.Sigmoid, bias=gb[:, dirn, 1, cc:cc + 1]), [pi_.res, gb.res], [xn.res])
                    S.op("pool", lambda E: E.tensor_tensor(out=xn[:], in0=xn[:], in1=xr[:], op=ALU.mult), [xn.res, xr.res], [xn.res])
                    S.op("act", lambda E: E.activation(out=a_t[:], in_=r_t[:], func=AF.Exp, scale=c8[:, idx:idx + 1]), [r_t.res, c8.res], [a_t.res])
                    S.op("act", lambda E: E.activation(out=r_t[:], in_=r_t[:], func=AF.Exp, scale=c16[:, idx:idx + 1]), [r_t.res, c16.res], [r_t.res])
                    S.op("act", lambda E: E.activation(out=r_t[:], in_=r_t[:], func=AF.Sqrt, scale=-1.0, bias=one[:]), [r_t.res, one.res], [r_t.res])
                    bcol = 0 if dirn == 0 else L_SEQ - 1
                    S.op("dve", lambda E: E.memset(r_t[:, bcol:bcol + 1], 1.0), [r_t.res], [r_t.res])
                    S.op("dve", lambda E: E.tensor_tensor(out=xn[:], in0=xn[:], in1=r_t[:], op=ALU.mult), [xn.res, r_t.res], [xn.res])
                    if dirn == 0:
                        S.op("dve", lambda E: E.tensor_tensor_scan(out=hf[:, :], data0=a_t[:, :], data1=xn[:, :], initial=0.0, op0=ALU.mult, op1=ALU.add),
                             [a_t.res, xn.res], [hf.res])
                    else:
                        S.dma("sp", r_t[:], XC[s, 512 + cc * 128:512 + (cc + 1) * 128, :], writes=[r_t.res])
                        hb = raw
                        S.op("dve", lambda E: E.tensor_tensor_scan(out=hb[:, L_SEQ - 1::-1], data0=a_t[:, ::-1], data1=xn[:, ::-1], initial=0.0,
                                                                  op0=ALU.mult, op1=ALU.add), [a_t.res, xn.res], [hb.res])
                        S.op("pool", lambda E: E.tensor_tensor(out=hf[:], in0=hf[:], in1=hb[:, 0:L_SEQ], op=ALU.add), [hf.res, hb.res], [hf.res])
                S.op("act", lambda E: E.activation(out=r_t[:], in_=r_t[:], func=AF.Gelu_apprx_tanh), [r_t.res], [r_t.res])
                S.op("pool", lambda E: E.tensor_tensor(out=yo_view, in0=hf[:], in1=r_t[:], op=ALU.mult), [hf.res, r_t.res, xn.res], [xn.res])
                S.dma("pool", YC[s, cc * 128:(cc + 1) * 128, :], yo_view, reads=[xn.res])
                if cc + 1 < 4:
                    copy_op("act", xrb[:], xrs[(cc + 1) % 2][:], [xrs[(cc + 1) % 2].res], [xrb.res])
            S.barrier()

    def ln_epilogue(lnbuf, po, xres, gB, bB, epsT, k):
        ysb, st, ti, out = lnbuf["y"][k % 2], lnbuf["st"][k % 2], lnbuf["ti"][k % 2], lnbuf["o"][k % 2]
        I32 = mybir.dt.int32
        S.op("dve", lambda E: E.memset(st[:, 0:2], 0.0), [st.res], [st.res])
        S.op("dve", lambda E: E.scalar_tensor_tensor(out=ysb[:], in0=xres[:], scalar=float(ALPHA), in1=po[:, :], op0=ALU.mult, op1=ALU.add,
                                                     accum_out=st[:, 0:1]), [xres.res, po.res, st.res], [ysb.res, st.res])
        S.op("dve", lambda E: E.scalar_tensor_tensor(out=out[:], in0=ysb[:], scalar=1.0, in1=ysb[:], op0=ALU.mult, op1=ALU.mult,
                                                     accum_out=st[:, 1:2]), [ysb.res, st.res], [out.res, st.res])
        S.op("dve", lambda E: E.tensor_scalar(out=st[:, 2:3], in0=st[:, 0:1], scalar1=1.0 / D, scalar2=None, op0=ALU.mult), [st.res], [st.res])
        S.op("dve", lambda E: E.tensor_tensor(out=st[:, 3:4], in0=st[:, 2:3], in1=st[:, 2:3], op=ALU.mult), [st.res], [st.res])
        S.op("dve", lambda E: E.scalar_tensor_tensor(out=st[:, 4:5], in0=st[:, 1:2], scalar=1.0 / D, in1=st[:, 3:4], op0=ALU.mult, op1=ALU.subtract),
             [st.res], [st.res])
        S.op("dve", lambda E: E.tensor_scalar(out=st[:, 4:5], in0=st[:, 4:5], scalar1=float(LN_EPS), scalar2=None, op0=ALU.add), [st.res], [st.res])
        S.op("dve", lambda E: E.tensor_single_scalar(out=ti[:, 0:1], in_=st[:, 4:5].bitcast(I32), scalar=1, op=ALU.logical_shift_right),
             [st.res], [ti.res])
        S.op("dve", lambda E: E.tensor_scalar(out=ti[:, 1:2], in0=ti[:, 0:1], scalar1=-1.0, scalar2=1597463007.0, op0=ALU.mult, op1=ALU.add),
             [ti.res], [ti.res])
        S.op("dve", lambda E: E.tensor_copy(out=st[:, 6:7], in_=ti[:, 1:2].bitcast(F32)), [ti.res], [st.res])
        for _ in range(3):
            S.op("dve", lambda E: E.scalar_tensor_tensor(out=st[:, 5:6], in0=st[:, 6:7], scalar=st[:, 4:5], in1=st[:, 6:7], op0=ALU.mult, op1=ALU.mult),
                 [st.res], [st.res])
            S.op("dve", lambda E: E.tensor_scalar(out=st[:, 5:6], in0=st[:, 5:6], scalar1=-0.5, scalar2=1.5, op0=ALU.mult, op1=ALU.add), [st.res], [st.res])
            S.op("dve", lambda E: E.tensor_tensor(out=st[:, 6:7], in0=st[:, 6:7], in1=st[:, 5:6], op=ALU.mult), [st.res], [st.res])
        S.op("dve", lambda E: E.tensor_scalar(out=ysb[:], in0=ysb[:], scalar1=st[:, 2:3], scalar2=st[:, 6:7], op0=ALU.subtract, op1=ALU.mult),
             [ysb.res, st.res], [ysb.res])
        S.op("pool", lambda E: E.tensor_tensor(out=out[:], in0=ysb[:], in1=gB[:], op=ALU.mult), [ysb.res, gB.res], [out.res])
        S.op("pool", lambda E: E.tensor_tensor(out=out[:], in0=out[:], in1=bB[:], op=ALU.add), [out.res, bB.res], [out.res])
        return out

    def ln_epilogue_act(lnbuf, po, xres, gB, bB, epsT, k):
        ysb, st, out = lnbuf["y"][k % 2], lnbuf["st"][k % 2], lnbuf["o"][k % 2]
        S.op("dve", lambda E: E.scalar_tensor_tensor(out=ysb[:], in0=xres[:], scalar=float(ALPHA), in1=po[:, :], op0=ALU.mult, op1=ALU.add),
             [xres.res, po.res], [ysb.res])
        S.op("act", lambda E: E.activation(out=out[:], in_=ysb[:], func=AF.Identity, accum_out=st[:, 0:1]), [ysb.res], [out.res, st.res])
        S.op("act", lambda E: E.activation(out=out[:], in_=ysb[:], func=AF.Square, accum_out=st[:, 1:2]), [ysb.res, out.res], [out.res, st.res])
        S.op("dve", lambda E: E.tensor_scalar(out=st[:, 2:3], in0=st[:, 0:1], scalar1=1.0 / D, scalar2=None, op0=ALU.mult), [st.res], [st.res])
        S.op("dve", lambda E: E.tensor_tensor(out=st[:, 3:4], in0=st[:, 2:3], in1=st[:, 2:3], op=ALU.mult), [st.res], [st.res])
        S.op("dve", lambda E: E.scalar_tensor_tensor(out=st[:, 4:5], in0=st[:, 1:2], scalar=1.0 / D, in1=st[:, 3:4], op0=ALU.mult, op1=ALU.subtract),
             [st.res], [st.res])
        S.op("act", lambda E: E.activation(out=st[:, 5:6], in_=st[:, 4:5], func=AF.Sqrt, bias=epsT[:]), [st.res, epsT.res], [st.res])
        S.op("dve", lambda E: E.reciprocal(out=st[:, 6:7], in_=st[:, 5:6]), [st.res], [st.res])
        S.op("dve", lambda E: E.tensor_scalar(out=ysb[:], in0=ysb[:], scalar1=st[:, 2:3], scalar2=st[:, 6:7], op0=ALU.subtract, op1=ALU.mult),
             [ysb.res, st.res], [ysb.res])
        S.op("pool", lambda E: E.tensor_tensor(out=out[:], in0=ysb[:], in1=gB[:], op=ALU.mult), [ysb.res, gB.res], [out.res])
        S.op("pool", lambda E: E.tensor_tensor(out=out[:], in0=out[:], in1=bB[:], op=ALU.add), [out.res, bB.res], [out.res])
        return out

    def ln_bufs(ph, tag):
        return {"y": [sb(ph, "lny%s%d" % (tag, i), [128, D], F32) for i in range(2)],
                "st": [sb(ph, "lnst%s%d" % (tag, i), [128, 8], F32) for i in range(2)],
                "ti": [sb(ph, "lnti%s%d" % (tag, i), [128, 2], mybir.dt.int32) for i in range(2)],
                "o": [sb(ph, "lno%s%d" % (tag, i), [128, D], F32) for i in range(2)]}

    def to_xT(bufs, xo, tok0, k):
        xb, pt = bufs["xb"][k % 2], bufs["pt"][k % 2]
        copy_op("act", xb[:], xo[:], [xo.res], [xb.res])
        for kc in range(8):
            S.op("pe", lambda E, kc=kc: E.transpose(pt[:, kc, :], xb[:, kc * 128:(kc + 1) * 128], identb[:]),
                 reads=[xb.res, identb.res], writes=[pt.res], pe_chain=True)
        copy_op("dve", xT[:, :, tok0:tok0 + 128], pt[:, :, :], [pt.res], [xT.rs[tok0 // 128]])

    def p5_merge(l, s):
        with ExitStack() as ph:
            wo = sb(ph, "wo", [128, 8, D], BF16)
            S.dma("sp", wo[:], WO_b[l], writes=[wo.res])
            bg = sb(ph, "bg", [128, 24], F32)
            S.dma("sp", bg[:], b_gate[l], writes=[bg.res])
            gB = sb(ph, "ln1g", [128, D], F32)
            bB = sb(ph, "ln1b", [128, D], F32)
            S.dma("sp", gB[:], ln1_g[l], writes=[gB.res])
            S.dma("sp", bB[:], ln1_b[l], writes=[bB.res])
            epsT = sb(ph, "eps1", [128, 1], F32)
            S.op("dve", lambda E: E.memset(epsT[:], LN_EPS), [], [epsT.res])
            TBm = 512
            NG = 6
            gtl = [sb(ph, "p5gt%d" % i, [128, TBm], BF16) for i in range(NG)]
            yin = [sb(ph, "p5y%d" % i, [128, 10, TBm], BF16) for i in range(2)]
            NWB = NG
            wb = [sb(ph, "p5wb%d" % i, [128, 4, 128], BF16) for i in range(NWB)]
            pp = [ps(ph, "p5pp%d" % i, [128, TBm], F32) for i in range(4)]
            tmpm = [sb(ph, "p5tm%d" % i, [128, TBm], F32) for i in range(1)] * 2
            maccs = [sb(ph, "p5macc%d" % i, [128, TBm], F32) for i in range(2)]
            tmps = [sb(ph, "p5tmps%d" % i, [128, TBm], F32) for i in range(4)]
            mixT = [sb(ph, "p5mix%d" % i, [128, 8, TBm], BF16) for i in range(2)]
            po = [ps(ph, "p5po%d" % i, [128, D], F32) for i in range(2)]
            xres = [sb(ph, "p5xr%d" % i, [128, D], F32) for i in range(2)]
            lnb = ln_bufs(ph, "a")
            KC = (4, 2, 4)
            WB = (WBA_b, WBB_b, WBC_b)
            yoff = (0, 4, 6)
            cn = {"w": 0, "p": 0, "g": 0, "k": 0}
            pend = []
            x_src = x_in[s] if l == 0 else X2[s]
            NB = L_SEQ // TBm
            items = [(m, br) for m in range(8) for br in range(3)]

            def wo_tile(tb, tt):
                mx = mixT[tb % 2]
                tok0 = tb * TBm + tt * 128
                k = cn["k"]
                cn["k"] += 1
                p_ = po[k % 2]
                xr_ = xres[k % 2]
                S.dma("sp", xr_[:], x_src[tok0:tok0 + 128, :], writes=[xr_.res])
                for nh in range(2):
                    for kc in range(8):
                        mm(p_[:, nh * 512:(nh + 1) * 512], mx[:, kc, tt * 128:(tt + 1) * 128], wo[:, kc, nh * 512:(nh + 1) * 512],
                           kc == 0, kc == 7, [mx.res, wo.res], p_.res)
                o_ = ln_epilogue(lnb, p_, xr_, gB, bB, epsT, k)
                S.dma("pool", X1[s, tok0:tok0 + 128, :], o_[:], reads=[o_.res])

            def load_y(tb_):
                yy_ = yin[tb_ % 2]
                ta = tb_ * TBm
                S.dma("sp", yy_[:, 0:4, :], YA[s].rearrange("(a p) t -> p a t", p=128)[:, :, ta:ta + TBm], writes=[yy_.res])
                S.dma("sp", yy_[:, 4:6, :], YB[s].rearrange("(a p) t -> p a t", p=128)[:, :, ta:ta + TBm], writes=[yy_.res])
                S.dma("sp", yy_[:, 6:10, :], YC[s].rearrange("(a p) t -> p a t", p=128)[:, :, ta:ta + TBm], writes=[yy_.res])

            for tb in range(NB + 1):
                if tb == NB:
                    for tt in range(TBm // 128):
                        wo_tile(tb - 1, tt)
                    break
                t0 = tb * TBm
                if tb == 0:
                    load_y(0)
                if tb + 1 < NB:
                    load_y(tb + 1)
                yi = yin[tb % 2]
                mx = mixT[tb % 2]
                loaded = {}

                def load(i):
                    m, br = items[i]
                    b = wb[cn["w"] % NWB]
                    gt_ = gtl[cn["w"] % NG]
                    cn["w"] += 1
                    col = br * 8 + m
                    S.dma("sp", b[:, :KC[br], :], WB[br][l, m], writes=[b.res])
                    S.dma("sp", gt_[:], GT[s, col * 128:(col + 1) * 128, t0:t0 + TBm], writes=[gt_.res])
                    loaded[i] = (b, gt_)

                PF = 3
                for i in range(len(items) + PF):
                    if i < len(items):
                        load(i)
                    j = i - PF
                    if j < 0:
                        continue
                    m, br = items[j]
                    b, gt_ = loaded.pop(j)
                    pt_ = pp[cn["p"] % 4]
                    cn["p"] += 1
                    col = br * 8 + m
                    for kc in range(KC[br]):
                        mm(pt_[:, :], b[:, kc, :], yi[:, yoff[br] + kc, :], kc == 0, kc == KC[br] - 1, [b.res, yi.res], pt_.res)
                    mac_ = maccs[m % 2]
                    if br == 0:
                        S.op("dve", lambda E: E.tensor_tensor(out=mac_[:], in0=gt_[:], in1=pt_[:, :], op=ALU.mult), [gt_.res, pt_.res], [mac_.res])
                    else:
                        t_ = tmps[(m % 2) * 2 + (br - 1)]
                        S.op("dve", lambda E: E.tensor_tensor(out=t_[:], in0=gt_[:], in1=pt_[:, :], op=ALU.mult), [gt_.res, pt_.res], [t_.res])
                        if br == 1:
                            S.op("dve", lambda E: E.tensor_tensor(out=mac_[:], in0=mac_[:], in1=t_[:], op=ALU.add), [mac_.res, t_.res], [mac_.res])
                        else:
                            S.op("dve", lambda E: E.tensor_tensor(out=mx[:, m, :], in0=mac_[:], in1=t_[:], op=ALU.add), [mac_.res, t_.res], [mx.res])
                    if tb >= 1 and j % 6 == 5:
                        wo_tile(tb - 1, j // 6)
            S.barrier()

    def p6_ffn(l, s, last):
        with ExitStack() as ph:
            wdn = sb(ph, "wdn", [128, 24, D], BF16)
            S.dma("sp", wdn[:, 0:12, :], WDN_b[l, :, 0:12, :], writes=[wdn.res])
            S.dma("sp", wdn[:, 12:24, :], WDN_b[l, :, 12:24, :], writes=[wdn.res])
            fw = sb(ph, "ffw", [128, 24, 3], F32)
            fb = sb(ph, "ffb", [128, 24], F32)
            S.dma("sp", fw[:], ffn_conv_w[l], writes=[fw.res])
            S.dma("sp", fb[:], ffn_conv_b[l], writes=[fb.res])
            gB = sb(ph, "ln2g", [128, D], F32)
            bB = sb(ph, "ln2b", [128, D], F32)
            S.dma("sp", gB[:], ln2_g[l], writes=[gB.res])
            S.dma("sp", bB[:], ln2_b[l], writes=[bB.res])
            epsT = sb(ph, "eps2", [128, 1], F32)
            S.op("dve", lambda E: E.memset(epsT[:], LN_EPS), [], [epsT.res])
            TBf = 512
            wu = [sb(ph, "p6wu%d" % i, [128, 2, 8, 128], BF16) for i in range(4)]
            pgt = [ps(ph, "p6pg%d" % i, [128, 1024], F32) for i in range(2)]
            put = [ps(ph, "p6pu%d" % i, [128, 512], F32) for i in range(2)]
            po = ps(ph, "p6po", [128, D], F32)
            yv = [sb(ph, "p6y%d" % i, [128, TBf], F32) for i in range(2)]
            gl = [sb(ph, "p6g%d" % i, [128, TBf], F32) for i in range(2)]
            actT = [sb(ph, "p6act%d" % i, [128, 24, TBf], BF16) for i in range(1)]
            xres = [sb(ph, "p6xr%d" % i, [128, D], F32) for i in range(2)]
            lnb = ln_bufs(ph, "b")
            cn = {"w": 0, "p": 0, "k": 0}
            for tb in range(L_SEQ // TBf):
                t0 = tb * TBf
                at = actT[0]
                first, lastb = (t0 == 0), (t0 + TBf == L_SEQ)
                loaded = {}

                def load(i):
                    w = wu[cn["w"] % 4]
                    cn["w"] += 1
                    S.dma("sp", w[:, 0, :, :], WUP_b[l, i], writes=[w.res])
                    S.dma("sp", w[:, 1, :, :], WUP_b[l, 24 + i], writes=[w.res])
                    loaded[i] = w

                PF = 2
                for i in range(24 + PF):
                    if i < 24:
                        load(i)
                    m = i - PF
                    if m < 0:
                        continue
                    w = loaded.pop(m)
                    pg_, pu_ = pgt[cn["p"] % 2], put[cn["p"] % 2]
                    y_, g_ = yv[cn["p"] % 2], gl[cn["p"] % 2]
                    cn["p"] += 1
                    c_lo = 1 if first else 0
                    c_hi = 513 if lastb else 514
                    for (ca, cb_) in ((c_lo, 512), (512, c_hi)):
                        ta, tb_ = t0 - 1 + ca, t0 - 1 + cb_
                        for kc in range(8):
                            mm(pg_[:, ca:cb_], w[:, 0, kc, :], xT[:, kc, ta:tb_], kc == 0, kc == 7, [w.res] + xt_res(ta, tb_ - ta), pg_.res)
                    for kc in range(8):
                        mm(pu_[:, :], w[:, 1, kc, :], xT[:, kc, t0:t0 + TBf], kc == 0, kc == 7, [w.res] + xt_res(t0, TBf), pu_.res)
                    S.op("act", lambda E: E.activation(out=y_[:], in_=pg_[:, 1:513], func=AF.Identity, scale=fw[:, m, 1:2], bias=fb[:, m:m + 1]),
                         [pg_.res, fw.res, fb.res], [y_.res])
                    S.op("dve", lambda E: E.scalar_tensor_tensor(out=y_[:, c_lo:512], in0=pg_[:, c_lo:512], scalar=fw[:, m, 0:1], in1=y_[:, c_lo:512],
                                                                 op0=ALU.mult, op1=ALU.add), [pg_.res, fw.res, y_.res], [y_.res])
                    nh = c_hi - 2
                    S.op("dve", lambda E: E.scalar_tensor_tensor(out=y_[:, 0:nh], in0=pg_[:, 2:2 + nh], scalar=fw[:, m, 2:3], in1=y_[:, 0:nh],
                                                                 op0=ALU.mult, op1=ALU.add), [pg_.res, fw.res, y_.res], [y_.res])
                    S.op("act", lambda E: E.activation(out=g_[:], in_=y_[:], func=AF.Gelu_apprx_tanh), [y_.res], [g_.res])
                    S.op("dve", lambda E: E.tensor_tensor(out=at[:, m, :], in0=g_[:], in1=pu_[:, :], op=ALU.mult), [g_.res, pu_.res], [at.res])
                for tt in range(TBf // 128):
                    tok0 = t0 + tt * 128
                    k = cn["k"]
                    cn["k"] += 1
                    xr_ = xres[k % 2]
                    S.dma("sp", xr_[:], X1[s, tok0:tok0 + 128, :], writes=[xr_.res])
                    for nh in range(2):
                        for kc in range(24):
                            mm(po[:, nh * 512:(nh + 1) * 512], at[:, kc, tt * 128:(tt + 1) * 128], wdn[:, kc, nh * 512:(nh + 1) * 512],
                               kc == 0, kc == 23, [at.res, wdn.res], po.res)
                    o_ = ln_epilogue_act(lnb, po, xr_, gB, bB, epsT, k)
                    if last:
                        S.dma("pool", y_out[s, tok0:tok0 + 128, :], o_[:], reads=[o_.res])
                    else:
                        S.dma("pool", X2[s, tok0:tok0 + 128, :], o_[:], reads=[o_.res])
            S.barrier()

    g.p5_merge = p5_merge
    g.p6_ffn = p6_ffn

    def hyena_filter_all(l):
        pf_filter(l)
        fft_stage1(KTOK, 128)
        fft_stage2("filter", l)

    def hyena_seq(l, s):
        p2a_hyconv(l, s)
        fft_stage1(UTOK[s], 64)
        fft_stage2("signal", l)
        p2d_hyout(l, s)

    g.hyena_filter_all = hyena_filter_all
    g.hyena_seq = hyena_seq
    g.pf_filter = pf_filter
    g.p3_attention = p3_attention
    g.p3b_attn_merge = p3b_attn_merge
    g.p4_rglru = p4_rglru

    g.cast_weights = cast_weights
    g.p1a_load_x = p1a_load_x
    g.p1b_inproj = p1b_inproj
    return S, g, es, locals()


def _pcol(v):
    v = np.asarray(v)
    C = v.shape[-1]
    lead = v.shape[:-1]
    v = v.reshape(lead + (C // 128, 128))
    v = np.moveaxis(v, -1, 0)
    v = np.moveaxis(v, -1, 1)
    return np.ascontiguousarray(v)


def _qk_perm():
    cols = list(range(1536))
    for kind in range(2):
        base = 1536 + kind * 768
        for gi in range(3):
            for half in range(2):
                for h in range(4):
                    for e in range(32):
                        cols.append(base + (gi * 4 + h) * 64 + half * 32 + e)
    cols += list(range(3072, 4864))
    return np.array(cols)


_CONST_CACHE = {}


def make_consts():
    if _CONST_CACHE:
        return _CONST_CACHE
    bf = ml_dtypes.bfloat16
    f32 = np.float32
    c = {}
    c["c_identf"] = np.eye(128, dtype=f32)
    c["c_identb"] = np.eye(128).astype(bf)
    inv = (np.float32(10000.0) ** (-np.arange(0, 64, 2, dtype=f32) / np.float32(64))).astype(f32)
    ang = (np.arange(L_SEQ, dtype=f32)[:, None] * inv[None, :]).astype(f32)
    c["c_ropec"] = np.ascontiguousarray(np.tile(np.cos(ang).T.astype(f32), (4, 1)))
    c["c_ropes"] = np.ascontiguousarray(np.tile(np.sin(ang).T.astype(f32), (4, 1)))
    t = np.linspace(0.0, 1.0, L_SEQ, dtype=f32)[:, None]
    w = (np.float32(2.0 * math.pi) * np.arange(L_SEQ, dtype=f32)[:, None] / np.float32(L_SEQ)).astype(f32)
    bands = np.linspace(1e-4, 7, 8, dtype=f32)[None, :]
    z = np.concatenate([t, np.cos(bands * w), -np.sin(bands * w)], axis=-1).astype(f32)
    c["c_zT"] = np.ascontiguousarray(z.T)
    c["c_tv"] = np.ascontiguousarray(np.tile(t.T, (128, 1)).astype(f32))
    deltas = np.abs(np.linspace(math.log(1e-2) / 1.5, math.log(1e-2) / 0.3, D_HY, dtype=f32))
    c["c_ndelta"] = _pcol(-deltas).astype(f32)
    n1 = np.arange(128)[:, None]
    f1 = np.arange(128)[None, :]
    a = 2 * np.pi * n1 * f1 / 128.0
    c["c_f1"] = np.stack([np.cos(a), -np.sin(a)], axis=1).astype(bf)
    f1c = np.arange(128)[:, None]
    n2 = np.arange(64)[None, :]
    a = 2 * np.pi * f1c * n2 / NFFT
    c["c_tw"] = np.stack([np.cos(a), np.sin(a)], axis=1).astype(f32)
    n2c = np.arange(64)[:, None]
    f2 = np.arange(64)[None, :]
    a = 2 * np.pi * n2c * f2 / 64.0
    C2, S2 = np.cos(a), np.sin(a)
    l2re = np.concatenate([C2, S2], axis=0)
    l2im = np.concatenate([-S2, C2], axis=0)
    c["c_l2x"] = np.stack([np.concatenate([l2re, l2im], axis=1), np.concatenate([l2re, l2re], axis=1),
                           np.concatenate([-l2im, l2im], axis=1)], axis=1).astype(bf)
    m0 = np.concatenate([np.concatenate([C2, S2], axis=1), np.concatenate([-S2, C2], axis=1)], axis=0)
    m1 = np.concatenate([np.concatenate([S2, -C2], axis=1), np.concatenate([-C2, -S2], axis=1)], axis=0)
    c["c_i2x"] = np.stack([m0, m1], axis=1).astype(bf)
    f1c = np.arange(128)[:, None]
    t1 = np.arange(64)[None, :]
    a = 2 * np.pi * f1c * t1 / 128.0
    c["c_i1"] = np.stack([np.cos(a) / NFFT, -np.sin(a) / NFFT], axis=1).astype(bf)
    p = np.arange(128)[:, None]
    j = np.arange(128)[None, :]
    c["c_mask"] = np.stack([(p >= j), (p <= j)], axis=1).astype(bf)
    neg = np.where(np.stack([(p >= j), (p <= j)], axis=1), 0.0, -30000.0)
    c["c_negm"] = np.stack([neg, neg], axis=1).astype(bf)
    _CONST_CACHE.update(c)
    return c


def prep_shared(inputs, NL=NLAYER):
    f32 = np.float32
    g = {}
    perm = _qk_perm()
    g["w_in"] = np.ascontiguousarray(np.asarray(inputs["w_in"], f32)[:NL][:, :, perm])
    for k in ("w_gate", "w_br_a", "w_br_b", "w_br_c", "w_o", "w_up", "w_down"):
        g[k] = np.ascontiguousarray(np.asarray(inputs[k], f32)[:NL])
    A = lambda k: np.asarray(inputs[k], f32)[:NL]
    g["hy_conv_w"] = np.stack([_pcol(A("hy_conv_w")[l]) for l in range(NL)])
    g["hy_conv_b"] = np.stack([_pcol(A("hy_conv_b")[l]) for l in range(NL)])
    g["hy_bias"] = np.stack([_pcol(A("hy_bias")[l]) for l in range(NL)])
    g["hy_w1"] = A("hy_filt_w1")
    g["hy_w2"] = A("hy_filt_w2")
    g["hy_w3"] = A("hy_filt_w3")
    g["hy_b1"] = A("hy_filt_b1")[:, :, None]
    g["hy_b2"] = A("hy_filt_b2")[:, :, None]
    g["hy_fr"] = A("hy_filt_freq")[:, :, None]
    g["hy_b3"] = np.stack([_pcol(A("hy_filt_b3")[l]) for l in range(NL)])
    g["rg_conv_w"] = np.stack([_pcol(A("rg_conv_w")[l]) for l in range(NL)])
    g["rg_conv_b"] = np.stack([_pcol(A("rg_conv_b")[l]) for l in range(NL)])
    gw = A("rg_gate_w")
    bd = np.zeros((NL, 2, 2, 4, 128, 128), f32)
    for cc in range(4):
        bd[:, :, :, cc, 0:64, 0:64] = gw[:, :, :, 2 * cc]
        bd[:, :, :, cc, 64:128, 64:128] = gw[:, :, :, 2 * cc + 1]
    g["rg_gate_w"] = bd
    g["rg_gate_b"] = np.stack([_pcol(A("rg_gate_b")[l]) for l in range(NL)])
    g["rg_gate_b"] = np.ascontiguousarray(np.transpose(g["rg_gate_b"], (0, 1, 3, 4, 2)))
    g["rg_lam"] = np.ascontiguousarray(np.transpose(np.stack([_pcol(A("rg_lam")[l]) for l in range(NL)]), (0, 1, 3, 2)))
    g["b_gate"] = np.stack([_pcol(A("b_gate")[l]) for l in range(NL)])
    g["ffn_conv_w"] = np.stack([_pcol(A("ffn_conv_w")[l]) for l in range(NL)])
    g["ffn_conv_b"] = np.stack([_pcol(A("ffn_conv_b")[l]) for l in range(NL)])
    for k in ("ln1_g", "ln1_b", "ln2_g", "ln2_b"):
        g[k] = np.ascontiguousarray(np.broadcast_to(A(k)[:, None, :], (NL, 128, D)))
    g.update(make_consts())
    return {k: np.ascontiguousarray(v) for k, v in g.items()}


def build_full(nc, NS=2, NL=NLAYER, dbg=None):
    S, g, es, loc = build_program(nc, NS=NS, NL=NL, dbg=dbg)
    for l in range(NL):
        g.cast_weights(l)
        g.hyena_filter_all(l)
    for s in range(NS):
        g.p1a_load_x(s)
        for l in range(NL):
            last = (l == NL - 1)
            g.p1b_inproj(l, s)
            g.hyena_seq(l, s)
            g.p3_attention(l, s)
            g.p3b_attn_merge(l, s)
            g.p4_rglru(l, s)
            g.p5_merge(l, s)
            g.p1a_load_x(s, loc["X1"][s])
            g.p6_ffn(l, s, last)
            if not last:
                g.p1a_load_x(s, loc["X2"][s])
    S.barrier()
    es.close()
    return S


def kernel(**inputs):
    n_cores = 8
    xp = np.asarray(inputs["x_prompt"], np.float32)
    xs = np.asarray(inputs["x_sample"], np.float32)
    shared = prep_shared(inputs, NL=NLAYER)
    nc = bass.Bass("TRN2", target_bir_lowering=False)
    build_full(nc, NS=2, NL=NLAYER)
    in_maps = []
    for c in range(n_cores):
        m = dict(shared)
        m["x"] = np.ascontiguousarray(np.stack([xp[c], xs[c % 4]]))
        in_maps.append(m)
    res = run_bass_kernel_spmd(nc, in_maps, core_ids=list(range(n_cores)))
    y_prompt = np.stack([np.asarray(res.results[c]["y"][0], np.float32) for c in range(8)])
    y_sample = np.stack([np.asarray(res.results[c]["y"][1], np.float32) for c in range(4)])
    return (y_prompt, y_sample)
```
